# Optimizing a Trainium2 kernel written in Bass

```python
import jax, jax.numpy as jnp
from jax import lax
import numpy as np

D_MODEL = 1024
BATCH = 4
SEQ = 8192
DEPTH = 1

HEAD_DIM = 64
D_MIX = D_MODEL
D_RWKV = D_MIX // 2
D_FOX = D_MIX - D_RWKV
H_RWKV = D_RWKV // HEAD_DIM
H_FOX = D_FOX // HEAD_DIM
DECAY_LORA = 64
AAA_LORA = 64
Q_BLOCK = 128
NORM_EPS = 1e-6
GN_EPS = 64e-5

RW_COLS = 3 * D_RWKV + DECAY_LORA + AAA_LORA
FOX_COLS = 3 * D_FOX + H_FOX
GATE_COLS = D_MIX
D_IN_PROJ = RW_COLS + FOX_COLS + GATE_COLS

kernel_name = "hymba_rwkv7_fox_hybrid"


def rmsnorm(x, g, eps=NORM_EPS):
    xf = x.astype(jnp.float32)
    y = xf * lax.rsqrt(jnp.mean(xf * xf, axis=-1, keepdims=True) + eps)
    return (y * g.astype(jnp.float32)).astype(x.dtype)


def rwkv7_branch(p, mu, w_up, w0, a_up, a0, k_k, k_a, r_k, gn_g, gn_b):
    B, S, _ = p.shape
    dt = p.dtype
    prev = jnp.pad(p, ((0, 0), (1, 0), (0, 0)))[:, :-1]
    p = p + (prev - p) * mu
    r = p[..., 0:D_RWKV]
    k = p[..., D_RWKV:2 * D_RWKV]
    v = p[..., 2 * D_RWKV:3 * D_RWKV]
    wl = p[..., 3 * D_RWKV:3 * D_RWKV + DECAY_LORA]
    al = p[..., 3 * D_RWKV + DECAY_LORA:]
    w = w0 + jnp.tanh(wl) @ w_up
    w = (-jax.nn.softplus(-w) - 0.5).astype(jnp.float32)
    decay = jnp.exp(-jnp.exp(w))
    a = jax.nn.sigmoid(a0 + al @ a_up)

    hs = lambda t: t.reshape(B, S, H_RWKV, HEAD_DIM)
    kk = hs(k * k_k).astype(jnp.float32)
    kk = kk / jnp.maximum(jnp.sqrt(jnp.sum(kk * kk, axis=-1, keepdims=True)), 1e-12)
    k = k * (1.0 + (a - 1.0) * k_a)
    r_h, k_h, v_h, a_h, w_h = hs(r), hs(k), hs(v), hs(a), decay.reshape(B, S, H_RWKV, HEAD_DIM)

    to_time = lambda t: jnp.swapaxes(t.astype(jnp.float32), 0, 1)
    xs = (to_time(r_h), to_time(w_h), to_time(k_h), to_time(v_h), to_time(kk), to_time(a_h))

    def step(state, inp):
        rt, wt, kt, vt, kkt, at = inp
        sa = jnp.einsum('bhvk,bhk->bhv', state, -kkt)
        state = (state * wt[:, :, None, :]
                 + sa[..., :, None] * (kkt * at)[..., None, :]
                 + vt[..., :, None] * kt[..., None, :])
        yt = jnp.einsum('bhvk,bhk->bhv', state, rt)
        return state, yt

    s0 = jnp.zeros((B, H_RWKV, HEAD_DIM, HEAD_DIM), jnp.float32)
    _, y = lax.scan(step, s0, xs)
    y = jnp.swapaxes(y, 0, 1)
    mean = jnp.mean(y, axis=-1, keepdims=True)
    var = jnp.mean(jnp.square(y - mean), axis=-1, keepdims=True)
    y = ((y - mean) * lax.rsqrt(var + GN_EPS)).reshape(B, S, D_RWKV)
    y = y * gn_g.astype(jnp.float32) + gn_b.astype(jnp.float32)
    bonus = jnp.sum((r_h * k_h * r_k).astype(jnp.float32), axis=-1, keepdims=True) * v_h.astype(jnp.float32)
    y = y + bonus.reshape(B, S, D_RWKV)
    return y.astype(dt)


def fox_branch(p, b_f, q_g, k_g):
    B, S, _ = p.shape
    dt = p.dtype
    heads = lambda t: t.reshape(B, S, H_FOX, HEAD_DIM).transpose(0, 2, 1, 3)
    q = rmsnorm(heads(p[..., 0:D_FOX]), q_g)
    k = rmsnorm(heads(p[..., D_FOX:2 * D_FOX]), k_g)
    v = heads(p[..., 2 * D_FOX:3 * D_FOX])
    fl = p[..., 3 * D_FOX:]
    log_f = jax.nn.log_sigmoid((fl + b_f).astype(jnp.float32))
    c = jnp.cumsum(log_f, axis=1).transpose(0, 2, 1)
    scale = HEAD_DIM ** -0.5
    nb = S // Q_BLOCK
    q_blk = q.reshape(B, H_FOX, nb, Q_BLOCK, HEAD_DIM).transpose(2, 0, 1, 3, 4)
    c_blk = c.reshape(B, H_FOX, nb, Q_BLOCK).transpose(2, 0, 1, 3)
    pos = jnp.arange(S, dtype=jnp.int32)
    pos_blk = pos.reshape(nb, Q_BLOCK)
    kf = k.astype(jnp.float32)
    vf = v.astype(jnp.float32)

    def attend(args):
        qb, cb, pb = args
        s = jnp.einsum('bhqd,bhkd->bhqk', qb.astype(jnp.float32), kf) * scale
        s = s + cb[..., :, None] - c[:, :, None, :]
        mask = pos[None, :] <= pb[:, None]
        s = jnp.where(mask[None, None], s, -jnp.inf)
        prob = jax.nn.softmax(s, axis=-1)
        return jnp.einsum('bhqk,bhkd->bhqd', prob, vf).astype(dt)

    o = lax.map(attend, (q_blk, c_blk, pos_blk))
    return o.transpose(1, 0, 3, 2, 4).reshape(B, S, D_FOX)


def setup_inputs(seed: int = 0) -> dict:
    key = jax.random.key(seed)
    ks = jax.random.split(key, 20)
    f32 = jnp.float32
    n = lambda k, shape, s: (jax.random.normal(k, shape, f32) * s)
    L = DEPTH
    return {
        "x": jax.random.normal(ks[0], (BATCH, SEQ, D_MODEL), f32),
        "norm_g": 1.0 + n(ks[1], (L, D_MODEL), 0.02),
        "w_in": n(ks[2], (L, D_MODEL, D_IN_PROJ), D_MODEL ** -0.5),
        "rw_mu": jax.random.uniform(ks[3], (L, RW_COLS), f32, 0.0, 1.0),
        "rw_w_up": n(ks[4], (L, DECAY_LORA, D_RWKV), 0.5 * DECAY_LORA ** -0.5),
        "rw_w0": jax.random.uniform(ks[5], (L, D_RWKV), f32, -5.0, -0.5),
        "rw_a_up": n(ks[6], (L, AAA_LORA, D_RWKV), 0.5 * AAA_LORA ** -0.5),
        "rw_a0": n(ks[7], (L, D_RWKV), 0.1),
        "rw_k_k": 0.85 + n(ks[8], (L, D_RWKV), 0.05),
        "rw_k_a": 1.0 + n(ks[9], (L, D_RWKV), 0.05),
        "rw_r_k": n(ks[10], (L, H_RWKV, HEAD_DIM), 0.1),
        "rw_gn_g": 1.0 + n(ks[11], (L, D_RWKV), 0.02),
        "rw_gn_b": n(ks[12], (L, D_RWKV), 0.01),
        "fox_q_g": 1.0 + n(ks[13], (L, HEAD_DIM), 0.02),
        "fox_k_g": 1.0 + n(ks[14], (L, HEAD_DIM), 0.02),
        "fox_b_f": 3.0 + n(ks[15], (L, H_FOX), 0.5),
        "w_out": n(ks[16], (L, D_MIX, D_MODEL), D_MIX ** -0.5),
        "final_g": 1.0 + n(ks[17], (D_MODEL,), 0.02),
    }


def reference(x, norm_g, w_in, rw_mu, rw_w_up, rw_w0, rw_a_up, rw_a0, rw_k_k, rw_k_a, rw_r_k,
              rw_gn_g, rw_gn_b, fox_q_g, fox_k_g, fox_b_f, w_out, final_g):
    for l in range(DEPTH):
        h = rmsnorm(x, norm_g[l])
        p = h @ w_in[l]
        p_rw = p[..., :RW_COLS]
        p_fox = p[..., RW_COLS:RW_COLS + FOX_COLS]
        gate = p[..., RW_COLS + FOX_COLS:]
        y_rw = rwkv7_branch(p_rw, rw_mu[l], rw_w_up[l], rw_w0[l], rw_a_up[l], rw_a0[l],
                            rw_k_k[l], rw_k_a[l], rw_r_k[l], rw_gn_g[l], rw_gn_b[l])
        y_fox = fox_branch(p_fox, fox_b_f[l], fox_q_g[l], fox_k_g[l])
        y = jnp.concatenate([y_rw, y_fox], axis=-1) * jax.nn.silu(gate)
        x = x + y @ w_out[l]
    return rmsnorm(x, final_g)
```

```python
import numpy as np
import concourse.bass as bass
import concourse.mybir as mybir
from concourse.bass_utils import run_bass_kernel_spmd

F32 = mybir.dt.float32
BF16 = mybir.dt.bfloat16
AF = mybir.ActivationFunctionType
ALU = mybir.AluOpType
AX = mybir.AxisListType

D_MODEL = 1024
KC = 8
HD = 64
NH = 4
RWC = 896
FXC = 772
NORM_EPS = 1e-6
GN_EPS = 64e-5
N_DCH = 6


class Buf:
    __slots__ = ("name", "lw", "rd", "psum")

    def __init__(self, name, psum=False):
        self.name = name
        self.lw = None
        self.rd = {}
        self.psum = psum


class Sched:
    def __init__(self, nc):
        self.nc = nc
        self.eng = {"pe": nc.tensor, "act": nc.scalar, "dve": nc.vector, "pool": nc.gpsimd, "sp": nc.sync}
        self.sem = {}
        self.cnt = {}
        for k in self.eng:
            self.sem[k] = nc.semaphore("sem_" + k).__enter__()
            self.cnt[k] = 0
        for c in range(N_DCH):
            k = "d%d" % c
            self.sem[k] = nc.semaphore("sem_" + k).__enter__()
            self.cnt[k] = 0
        self.sem["cc"] = nc.semaphore("sem_cc").__enter__()
        self.cnt["cc"] = 0
        self.waited = {k: {} for k in self.eng}
        self.next_dch = 0
        self.limit = None
        self.total = 0
        self.log = []

    def _lim(self):
        self.total += 1
        if self.log is not None:
            import sys as _s
            fr = _s._getframe(2)
            self.log.append((self.total, fr.f_lineno))
        return self.limit is not None and self.total > self.limit

    def _val(self, k, i):
        return i * 16 if (k[0] == "d" and k != "dve") else i

    def _deps(self, reads, writes):
        deps = {}

        def need(k, i):
            if deps.get(k, 0) < i:
                deps[k] = i

        for b in reads:
            if b.lw is not None:
                need(*b.lw)
            if b.psum:
                for k, i in b.rd.items():
                    need(k, i)
        for b in writes:
            if b.lw is not None:
                need(*b.lw)
            for k, i in b.rd.items():
                need(k, i)
        return deps

    def _wait(self, e, deps):
        w = self.waited[e]
        for k, i in deps.items():
            if w.get(k, 0) < i:
                self.eng[e].wait_ge(self.sem[k], self._val(k, i))
                w[k] = i

    def _commit(self, me, reads, writes):
        k, i = me
        for b in reads:
            b.rd[k] = i
        for b in writes:
            b.lw = me
            b.rd = {}

    def op(self, e, fn, reads=(), writes=()):
        if self._lim():
            return
        deps = self._deps(reads, writes)
        if e == "pe":
            deps.pop("pe", None)
        self._wait(e, deps)
        ins = fn(self.eng[e])
        self.cnt[e] += 1
        ins.then_inc(self.sem[e], 1)
        self._commit((e, self.cnt[e]), reads, writes)

    def dma(self, out, in_, reads=(), writes=(), q="sp"):
        if self._lim():
            return
        c = "d%d" % self.next_dch
        self.next_dch = (self.next_dch + 1) % N_DCH
        deps = self._deps(reads, writes)
        if self.cnt[c] > 0:
            deps[c] = max(deps.get(c, 0), self.cnt[c])
        self._wait(q, deps)
        ins = self.eng[q].dma_start(out=out, in_=in_)
        self.cnt[c] += 1
        ins.then_inc(self.sem[c], 16)
        self._commit((c, self.cnt[c]), reads, writes)

    def collective(self, fn, reads=(), writes=()):
        if self._lim():
            return
        deps = self._deps(reads, writes)
        self._wait("pool", deps)
        ins = fn(self.eng["pool"])
        self.cnt["cc"] += 1
        ins.then_inc(self.sem["cc"], 1)
        self._commit(("cc", self.cnt["cc"]), reads, writes)

    def barrier(self):
        for e in self.eng:
            deps = {k: c for k, c in self.cnt.items() if c > 0 and k != e}
            self._wait(e, deps)

    def finish(self):
        deps = {k: c for k, c in self.cnt.items() if c > 0 and k != "sp"}
        self._wait("sp", deps)


def build_nc(S, debug=False, stop_after=9, limit=None):
    NT = S // 128
    NB = S // 512
    SH = S // 2
    nc = bass.Bass("TRN2", target_bir_lowering=False)

    def din(name, shape, dt=F32):
        return nc.dram_tensor(name, list(shape), dt, kind="ExternalInput").ap()

    x = din("x", [S, D_MODEL])
    xres = din("xres", [SH, D_MODEL])
    w_rw = din("w_rw", [D_MODEL, RWC])
    w_fox = din("w_fox", [D_MODEL, FXC])
    w_grw = din("w_grw", [D_MODEL, 256])
    w_gfx = din("w_gfx", [D_MODEL, 256])
    w_outp = din("w_outp", [D_MODEL, D_MODEL])
    g_col = din("g_col", [128, KC])
    pb1_d = din("pb1", [128, 2688])
    pb2_d = din("pb2", [128, 132])
    pb3_d = din("pb3", [128, D_MODEL])
    w_up_d = din("w_up", [64, 256])
    a_up_d = din("a_up", [64, 256])
    out = nc.dram_tensor("out", [SH, D_MODEL], F32, kind="ExternalOutput").ap()
    dbg_kind = "ExternalOutput" if debug else "Internal"
    ybuf = [nc.dram_tensor("ybuf%d" % c, [64, S], BF16, kind=dbg_kind).ap() for c in range(8)]
    yall = [nc.dram_tensor("yall%d" % c, [128, S], BF16, kind=dbg_kind).ap() for c in range(8)]
    sel_d = din("sel", [128, 2])
    ybuf_b = [Buf("ybuf%d" % c) for c in range(8)]
    yall_b = [Buf("yall%d" % c) for c in range(8)]
    out_b = Buf("out")

    sch = Sched(nc)
    sch.limit = limit
    op = sch.op

    class Scope:
        def __init__(self):
            self.stack = []

        def sb(self, name, shape, dt=F32):
            cm = nc.sbuf_tensor("s_" + name, list(shape), dt)
            t = cm.__enter__()
            self.stack.append(cm)
            return t, Buf(name)

        def ps(self, name, shape, dt=F32):
            cm = nc.psum_tensor("p_" + name, list(shape), dt)
            t = cm.__enter__()
            self.stack.append(cm)
            return t, Buf(name, psum=True)

        def close(self):
            while self.stack:
                self.stack.pop().__exit__(None, None, None)

    G = Scope()
    ident_bf, ident_bf_b = G.sb("ident_bf", [128, 128], BF16)
    ident_f, ident_f_b = G.sb("ident_f", [128, 128], F32)
    gcol, gcol_b = G.sb("gcol", [128, KC], F32)
    stage, stage_b = G.sb("stage", [128, 1024], F32)
    stage2, stage2_b = G.sb("stage2", [128, 1024], F32)

    def mk_ident(t, b):
        op("pool", lambda e: e.memset(t[:], 0.0), [], [b])
        op("pool", lambda e: e.affine_select(out=t[:], in_=t[:], pattern=[[-1, 128]], compare_op=ALU.not_equal,
                                             fill=1.0, base=0, channel_multiplier=1), [b], [b])

    mk_ident(ident_bf, ident_bf_b)
    mk_ident(ident_f, ident_f_b)
    sch.dma(gcol[:], g_col[:, :], [], [gcol_b])

    def load_weight(dst, dst_b, src, ncols, post=None):
        for c in range(KC):
            st, stb = (stage, stage_b) if c % 2 == 0 else (stage2, stage2_b)
            sch.dma(st[:, 0:ncols], src[c * 128:(c + 1) * 128, :], [], [stb])
            if post is None:
                op("dve", lambda e: e.tensor_scalar(out=dst[:, c, :], in0=st[:, 0:ncols], scalar1=gcol[:, c:c + 1],
                                                    scalar2=None, op0=ALU.mult), [stb, gcol_b], [dst_b])
            else:
                for (d2, d2b, pt, ptb) in post:
                    op("dve", lambda e: e.scalar_tensor_tensor(out=d2[:, c, :], in0=st[:, 0:ncols],
                                                               scalar=gcol[:, c:c + 1], in1=pt, op0=ALU.mult,
                                                               op1=ALU.mult), [stb, gcol_b, ptb], [d2b])

    def rmsnorm_to_hT(xt, xt_b, h, h_b, junk, junk_b, ss, ss_b, tpb, tpb_b, dst_ap_fn, dst_b):
        op("act", lambda e: e.activation(out=junk[:], in_=xt[:], func=AF.Square, accum_out=ss[:, 0:1]),
           [xt_b], [junk_b, ss_b])
        op("act", lambda e: e.activation(out=ss[:, 1:2], in_=ss[:, 0:1], func=AF.Ln, scale=1.0 / D_MODEL,
                                         bias=NORM_EPS), [ss_b], [ss_b])
        op("act", lambda e: e.activation(out=ss[:, 2:3], in_=ss[:, 1:2], func=AF.Exp, scale=-0.5), [ss_b], [ss_b])
        op("act", lambda e: e.activation(out=h[:], in_=xt[:], func=AF.Copy, scale=ss[:, 2:3]), [xt_b, ss_b], [h_b])
        tpv = tpb[:].rearrange("p (c t) -> p c t", c=KC)
        for c in range(KC):
            op("pe", lambda e: e.transpose(out=tpv[:, c, :], in_=h[:, c * 128:(c + 1) * 128], identity=ident_bf[:]),
               [h_b, ident_bf_b], [tpb_b])
        op("dve", lambda e: e.tensor_copy(out=dst_ap_fn(), in_=tpv), [tpb_b], [dst_b])

    P1 = Scope()
    Wa, Wa_b = P1.sb("Wa", [128, KC, RWC], BF16)
    Wb, Wb_b = P1.sb("Wb", [128, KC, RWC], BF16)
    Wg, Wg_b = P1.sb("Wg1", [128, KC, 256], BF16)
    pb1, pb1_b = P1.sb("pb1", [128, 2688], F32)
    omm, omm_b = P1.sb("omm", [128, RWC], F32)
    wup, wup_b = P1.sb("wup", [64, 256], F32)
    aup, aup_b = P1.sb("aup", [64, 256], F32)
    sch.dma(pb1[:], pb1_d[:, :], [], [pb1_b])
    sch.dma(wup[:], w_up_d[:, :], [], [wup_b])
    sch.dma(aup[:], a_up_d[:, :], [], [aup_b])
    MU = pb1[:, 0:896]
    W0 = pb1[:, 896:1152]
    A0 = pb1[:, 1152:1408]
    KK_ = pb1[:, 1408:1664]
    KA_ = pb1[:, 1664:1920]
    RK_ = pb1[:, 1920:2176]
    GNG = pb1[:, 2176:2432]
    GNB = pb1[:, 2432:2688]
    op("dve", lambda e: e.tensor_scalar(out=omm[:], in0=MU, scalar1=-1.0, scalar2=1.0, op0=ALU.mult, op1=ALU.add),
       [pb1_b], [omm_b])
    load_weight(None, None, w_rw, RWC, post=[(Wa, Wa_b, omm[:], omm_b), (Wb, Wb_b, MU, pb1_b)])
    load_weight(Wg, Wg_b, w_grw, 256)

    tri_i, tri_i_b = P1.sb("tri_i", [128, 128], F32)
    tri_s, tri_s_b = P1.sb("tri_s", [128, 128], F32)
    tri_r, tri_r_b = P1.sb("tri_r", [128, 128], F32)
    mask4, mask4_b = P1.sb("mask4", [128, 512], F32)
    maskr4, maskr4_b = P1.sb("maskr4", [128, 512], F32)
    ones_c, ones_c_b = P1.sb("ones_c", [128, 1], F32)

    def mk_tri(t, b, cm, pat, cmp_op):
        op("pool", lambda e: e.memset(t[:], 1.0), [], [b])
        op("pool", lambda e: e.affine_select(out=t[:], in_=t[:], pattern=[[pat, 128]], compare_op=cmp_op, fill=0.0,
                                             base=0, channel_multiplier=cm), [b], [b])
        op("pool", lambda e: e.memset(t[0:64, 64:128], 0.0), [b], [b])
        op("pool", lambda e: e.memset(t[64:128, 0:64], 0.0), [b], [b])

    mk_tri(tri_i, tri_i_b, -1, 1, ALU.is_ge)
    mk_tri(tri_s, tri_s_b, -1, 1, ALU.is_gt)
    mk_tri(tri_r, tri_r_b, 1, -1, ALU.is_gt)
    op("pool", lambda e: e.memset(ones_c[:], 1.0), [], [ones_c_b])
    for q_, (src, srcb) in enumerate([(tri_s, tri_s_b), (tri_i, tri_i_b), (tri_s, tri_s_b), (tri_i, tri_i_b)]):
        op("pool", lambda e: e.tensor_copy(out=mask4[:, q_ * 128:(q_ + 1) * 128], in_=src[:]), [srcb], [mask4_b])
        op("pool", lambda e: e.tensor_copy(out=maskr4[:, q_ * 128:(q_ + 1) * 128], in_=tri_r[:]), [tri_r_b], [maskr4_b])

    bank = [P1.ps("bank%d" % i, [128, 512], F32) for i in range(7)]
    bankT, bankT_b = P1.ps("bankT", [128, 1024], BF16)

    def sb(name, shape, dt=F32):
        return P1.sb(name, shape, dt)

    xt2 = [sb("xt%d" % i, [128, 1024]) for i in range(2)]
    h, h_b = sb("h", [128, 1024], BF16)
    junk, junk_b = sb("junk", [128, 1024], BF16)
    ss, ss_b = sb("ss", [128, 4])
    hT2 = [sb("hT%d" % i, [128, KC, 129], BF16) for i in range(2)]
    r32, r32_b = sb("r32", [128, 256])
    k32, k32_b = sb("k32", [128, 256])
    v32, v32_b = sb("v32", [128, 256])
    lt32, lt32_b = sb("lt32", [128, 128])
    lT, lT_b = sb("lT", [64, 2, 128])
    lwn, lwn_b = sb("lwn", [128, 256])
    t1, t1_b = sb("t1", [128, 256])
    t2, t2_b = sb("t2", [128, 256])
    a_, a_b = sb("a_", [128, 256])
    kk, kk_b = sb("kk", [128, 256])
    km, km_b = sb("km", [128, 256])
    ba, ba_b = sb("ba", [128, 256])
    s4, s4_b = sb("s4", [128, 16])
    ecum, ecum_b = sb("ecum", [128, 256])
    encum, encum_b = sb("encum", [128, 256])
    eprev, eprev_b = sb("eprev", [128, 256])
    erc, erc_b = sb("erc", [128, 256])
    wc, wc_b = sb("wc", [64, 8])
    tl4 = sb("tl4", [128, 4, 256], BF16)
    bh, bh_b = sb("bh", [128, 256], BF16)
    kh, kh_b = sb("kh", [128, 256], BF16)
    vb, vb_b = sb("vb", [128, 256], BF16)
    bhm, bhm_b = sb("bhm", [128, 2, 256], BF16)
    khm, khm_b = sb("khm", [128, 2, 256], BF16)
    cmask, cmask_b = sb("cmask", [128, 2])
    op("pool", lambda e: e.memset(cmask[:], 0.0), [], [cmask_b])
    op("pool", lambda e: e.memset(cmask[0:64, 0:1], 1.0), [cmask_b], [cmask_b])
    op("pool", lambda e: e.memset(cmask[64:128, 1:2], 1.0), [cmask_b], [cmask_b])
    TT, TT_b = sb("TT", [64, 16, 128], BF16)
    A_sb = [sb("A%d" % hh, [128, 512], BF16) for hh in range(NH)]
    Ni2 = [sb("Ni%d" % i, [128, 512], BF16) for i in range(2)]
    NiT2 = [sb("NiT%d" % i, [128, 512], BF16) for i in range(2)]
    X32, X32_b = sb("X32", [128, 512])
    Xb, Xb_b = sb("Xb", [128, 512], BF16)
    GTs, GTs_b = sb("GTs", [64, 1024])
    M0T, M0T_b = sb("M0T", [64, 512])
    Nst, Nst_b = sb("Nst", [64, 512])
    ST2 = [sb("ST%d" % i, [64, 256]) for i in range(2)]
    stt, stt_b = sb("stt", [64, 256])
    y32, y32_b = sb("y32", [128, 256])
    sg, sg_b = sb("sg", [128, 256])
    bon, bon_b = sb("bon", [128, 256])
    yg, yg_b = sb("yg", [128, 256], BF16)
    ygT, ygT_b = sb("ygT", [128, 2, 512], BF16)

    op("dve", lambda e: e.memset(ST2[0][0][:], 0.0), [], [ST2[0][1]])
    op("dve", lambda e: e.memset(GTs[:], 0.0), [], [GTs_b])
    op("dve", lambda e: e.memset(hT2[0][0][:, :, 0:1], 0.0), [], [hT2[0][1]])

    def v3(ap, hh=NH):
        return ap.rearrange("p (h d) -> p h d", h=hh)

    def bc(ap_small, hh=NH, d=HD):
        return ap_small.unsqueeze(2).to_broadcast([128, hh, d])

    st_cur = 0
    for i in range(NT):
        xt, xt_b = xt2[i % 2]
        hT, hT_b = hT2[i % 2]
        hTp, hTp_b = hT2[(i + 1) % 2]
        sch.dma(xt[:], x[i * 128:(i + 1) * 128, :], [], [xt_b])
        rmsnorm_to_hT(xt, xt_b, h, h_b, junk, junk_b, ss, ss_b, bankT, bankT_b, lambda: hT[:, :, 1:129], hT_b)
        if i > 0:
            op("pool", lambda e: e.tensor_copy(out=hT[:, :, 0:1], in_=hTp[:, :, 128:129]), [hTp_b], [hT_b])
        pr0, pr0_b = bank[0]
        pr1, pr1_b = bank[1]
        for (pt, ptb, n0, n1) in ((pr0, pr0_b, 0, 512), (pr1, pr1_b, 512, 896)):
            for c in range(KC):
                op("pe", lambda e: e.matmul(pt[:, 0:n1 - n0], hT[:, c, 1:129], Wa[:, c, n0:n1], start=(c == 0), stop=False),
                   [hT_b, Wa_b], [ptb])
            for c in range(KC):
                op("pe", lambda e: e.matmul(pt[:, 0:n1 - n0], hT[:, c, 0:128], Wb[:, c, n0:n1], start=False, stop=(c == KC - 1)),
                   [hT_b, Wb_b], [ptb])
        pg, pg_b = bank[2]
        for c in range(KC):
            op("pe", lambda e: e.matmul(pg[:, 0:256], hT[:, c, 1:129], Wg[:, c, :], start=(c == 0), stop=(c == KC - 1)),
               [hT_b, Wg_b], [pg_b])
        op("act", lambda e: e.activation(out=r32[:], in_=pr0[:, 0:256], func=AF.Copy), [pr0_b], [r32_b])
        op("dve", lambda e: e.tensor_copy(out=k32[:], in_=pr0[:, 256:512]), [pr0_b], [k32_b])
        op("act", lambda e: e.activation(out=v32[:], in_=pr1[:, 0:256], func=AF.Copy), [pr1_b], [v32_b])
        op("act", lambda e: e.activation(out=lt32[:, 0:64], in_=pr1[:, 256:320], func=AF.Exp, scale=-2.0), [pr1_b], [lt32_b])
        op("dve", lambda e: e.tensor_scalar(out=lt32[:, 0:64], in0=lt32[:, 0:64], scalar1=1.0, scalar2=None, op0=ALU.add),
           [lt32_b], [lt32_b])
        op("dve", lambda e: e.reciprocal(out=lt32[:, 0:64], in_=lt32[:, 0:64]), [lt32_b], [lt32_b])
        op("dve", lambda e: e.tensor_scalar(out=lt32[:, 0:64], in0=lt32[:, 0:64], scalar1=2.0, scalar2=-1.0, op0=ALU.mult,
                                            op1=ALU.add), [lt32_b], [lt32_b])
        op("dve", lambda e: e.tensor_copy(out=lt32[:, 64:128], in_=pr1[:, 320:384]), [pr1_b, lt32_b], [lt32_b])
        ptl, ptl_b = bank[3]
        for q_ in range(2):
            op("pe", lambda e: e.transpose(out=ptl[0:64, q_ * 128:(q_ + 1) * 128], in_=lt32[:, q_ * 64:(q_ + 1) * 64],
                                           identity=ident_f[:]), [lt32_b, ident_f_b], [ptl_b])
        op("dve", lambda e: e.tensor_copy(out=lT[:].rearrange("p a t -> p (a t)"), in_=ptl[0:64, 0:256]), [ptl_b], [lT_b])
        pl, pl_b = bank[4]
        op("pe", lambda e: e.matmul(pl[:, 0:256], lT[:, 0, :], wup[:], start=True, stop=True), [lT_b, wup_b], [pl_b])
        op("pe", lambda e: e.matmul(pl[:, 256:512], lT[:, 1, :], aup[:], start=True, stop=True), [lT_b, aup_b], [pl_b])
        op("dve", lambda e: e.tensor_tensor(out=t1[:], in0=pl[:, 0:256], in1=W0, op=ALU.add), [pl_b, pb1_b], [t1_b])
        op("act", lambda e: e.activation(out=t1[:], in_=t1[:], func=AF.Exp, scale=-1.0), [t1_b], [t1_b])
        op("act", lambda e: e.activation(out=t1[:], in_=t1[:], func=AF.Ln, bias=1.0), [t1_b], [t1_b])
        op("act", lambda e: e.activation(out=lwn[:], in_=t1[:], func=AF.Exp, scale=-1.0, bias=-0.5), [t1_b], [lwn_b])
        op("dve", lambda e: e.tensor_tensor(out=t2[:], in0=pl[:, 256:512], in1=A0, op=ALU.add), [pl_b, pb1_b], [t2_b])
        op("act", lambda e: e.activation(out=t2[:], in_=t2[:], func=AF.Exp, scale=-1.0), [t2_b], [t2_b])
        op("dve", lambda e: e.tensor_scalar(out=t2[:], in0=t2[:], scalar1=1.0, scalar2=None, op0=ALU.add), [t2_b], [t2_b])
        op("dve", lambda e: e.reciprocal(out=a_[:], in_=t2[:]), [t2_b], [a_b])
        op("dve", lambda e: e.tensor_tensor(out=kk[:], in0=k32[:], in1=KK_, op=ALU.mult), [k32_b, pb1_b], [kk_b])
        op("dve", lambda e: e.tensor_tensor(out=t1[:], in0=kk[:], in1=kk[:], op=ALU.mult), [kk_b], [t1_b])
        op("dve", lambda e: e.tensor_reduce(out=s4[:, 0:4], in_=v3(t1[:]), axis=AX.X, op=ALU.add), [t1_b], [s4_b])
        op("dve", lambda e: e.tensor_scalar(out=s4[:, 0:4], in0=s4[:, 0:4], scalar1=1e-24, scalar2=None, op0=ALU.max),
           [s4_b], [s4_b])
        op("act", lambda e: e.activation(out=s4[:, 0:4], in_=s4[:, 0:4], func=AF.Ln), [s4_b], [s4_b])
        op("act", lambda e: e.activation(out=s4[:, 0:4], in_=s4[:, 0:4], func=AF.Exp, scale=-0.5), [s4_b], [s4_b])
        op("dve", lambda e: e.tensor_tensor(out=v3(kk[:]), in0=v3(kk[:]), in1=bc(s4[:, 0:4]), op=ALU.mult),
           [kk_b, s4_b], [kk_b])
        op("dve", lambda e: e.scalar_tensor_tensor(out=t1[:], in0=a_[:], scalar=-1.0, in1=KA_, op0=ALU.add, op1=ALU.mult),
           [a_b, pb1_b], [t1_b])
        op("dve", lambda e: e.scalar_tensor_tensor(out=km[:], in0=t1[:], scalar=1.0, in1=k32[:], op0=ALU.add, op1=ALU.mult),
           [t1_b, k32_b], [km_b])
        op("dve", lambda e: e.tensor_tensor(out=ba[:], in0=kk[:], in1=a_[:], op=ALU.mult), [kk_b, a_b], [ba_b])
        op("dve", lambda e: e.tensor_tensor(out=t1[:], in0=r32[:], in1=km[:], op=ALU.mult), [r32_b, km_b], [t1_b])
        op("dve", lambda e: e.tensor_tensor(out=t1[:], in0=t1[:], in1=RK_, op=ALU.mult), [t1_b, pb1_b], [t1_b])
        op("dve", lambda e: e.tensor_reduce(out=s4[:, 4:8], in_=v3(t1[:]), axis=AX.X, op=ALU.add), [t1_b], [s4_b])
        op("dve", lambda e: e.tensor_tensor(out=v3(bon[:]), in0=v3(v32[:]), in1=bc(s4[:, 4:8]), op=ALU.mult),
           [v32_b, s4_b], [bon_b])
        op("act", lambda e: e.activation(out=t2[:], in_=pg[:, 0:256], func=AF.Exp, scale=-1.0), [pg_b], [t2_b])
        op("dve", lambda e: e.tensor_scalar(out=t2[:], in0=t2[:], scalar1=1.0, scalar2=None, op0=ALU.add), [t2_b], [t2_b])
        op("dve", lambda e: e.reciprocal(out=t2[:], in_=t2[:]), [t2_b], [t2_b])
        op("dve", lambda e: e.tensor_tensor(out=sg[:], in0=t2[:], in1=pg[:, 0:256], op=ALU.mult), [t2_b, pg_b], [sg_b])
        pc0, pc0_b = bank[5]
        pc1, pc1_b = bank[6]
        op("pe", lambda e: e.matmul(pc0[:, 0:256], tri_i[:], lwn[:], start=True, stop=True), [tri_i_b, lwn_b], [pc0_b])
        op("pe", lambda e: e.matmul(pc0[:, 256:512], tri_s[:], lwn[:], start=True, stop=True), [tri_s_b, lwn_b], [pc0_b])
        op("pe", lambda e: e.matmul(pc1[:, 0:256], tri_r[:], lwn[:], start=True, stop=True), [tri_r_b, lwn_b], [pc1_b])
        for cch in range(2):
            for hh in range(NH):
                col = 256 + cch * 4 + hh
                op("pe", lambda e: e.matmul(pc1[0:64, col:col + 1], lwn[cch * 64:(cch + 1) * 64, hh * 64:(hh + 1) * 64],
                                            ones_c[cch * 64:(cch + 1) * 64, :], start=True, stop=True),
                   [lwn_b, ones_c_b], [pc1_b])
        op("act", lambda e: e.activation(out=ecum[:], in_=pc0[:, 0:256], func=AF.Exp, scale=-1.0), [pc0_b], [ecum_b])
        op("act", lambda e: e.activation(out=encum[:], in_=pc0[:, 0:256], func=AF.Exp), [pc0_b], [encum_b])
        op("act", lambda e: e.activation(out=eprev[:], in_=pc0[:, 256:512], func=AF.Exp, scale=-1.0), [pc0_b], [eprev_b])
        op("act", lambda e: e.activation(out=erc[:], in_=pc1[:, 0:256], func=AF.Exp, scale=-1.0), [pc1_b], [erc_b])
        op("act", lambda e: e.activation(out=wc[:], in_=pc1[0:64, 256:264], func=AF.Exp, scale=-1.0), [pc1_b], [wc_b])
        tl, tl_b = tl4
        X3 = X32[:].rearrange("p (h c) -> p h c", h=NH)
        op("dve", lambda e: e.scalar_tensor_tensor(out=X3[:, :, 0:64], in0=v3(kk[:]), scalar=-1.0, in1=v3(eprev[:]),
                                                   op0=ALU.mult, op1=ALU.mult), [kk_b, eprev_b], [X32_b])
        op("pool", lambda e: e.tensor_copy(out=v3(tl[:, 0, :]), in_=X3[:, :, 0:64]), [X32_b], [tl_b])
        op("dve", lambda e: e.tensor_tensor(out=tl[:, 1, :], in0=r32[:], in1=ecum[:], op=ALU.mult), [r32_b, ecum_b], [tl_b])
        op("dve", lambda e: e.tensor_tensor(out=tl[:, 2, :], in0=ba[:], in1=encum[:], op=ALU.mult), [ba_b, encum_b], [tl_b])
        op("dve", lambda e: e.tensor_tensor(out=tl[:, 3, :], in0=km[:], in1=encum[:], op=ALU.mult), [km_b, encum_b], [tl_b])
        op("pool", lambda e: e.tensor_tensor(out=bh[:], in0=ba[:], in1=erc[:], op=ALU.mult), [ba_b, erc_b], [bh_b])
        op("pool", lambda e: e.tensor_tensor(out=kh[:], in0=km[:], in1=erc[:], op=ALU.mult), [km_b, erc_b], [kh_b])
        op("pool", lambda e: e.tensor_copy(out=vb[:], in_=v32[:]), [v32_b], [vb_b])
        TTp = bankT[0:64, :].rearrange("p (a t) -> p a t", a=8)
        for rnd in range(2):
            for tq in range(2):
                ty = rnd * 2 + tq
                for hh in range(NH):
                    op("pe", lambda e: e.transpose(out=TTp[:, tq * 4 + hh, :], in_=tl[:, ty, hh * 64:(hh + 1) * 64],
                                                   identity=ident_bf[:]), [tl_b, ident_bf_b], [bankT_b])
            op("act", lambda e: e.activation(out=TT[:, rnd * 8:(rnd + 1) * 8, :], in_=TTp, func=AF.Copy), [bankT_b], [TT_b])

        def kT(ty, hh):
            return TT[:, ty * 4 + hh, :]

        pn, pn_b = bank[2]
        for hh in range(NH):
            pa, pa_b = bank[hh % 2]
            rhs2 = TT[:, hh:hh + 5:4, :]
            op("pe", lambda e: e.matmul(pa[:, 0:256], kT(2, hh), rhs2, start=True, stop=True), [TT_b], [pa_b])
            op("pe", lambda e: e.matmul(pa[:, 256:512], kT(3, hh), rhs2, start=True, stop=True), [TT_b], [pa_b])
            op("pe", lambda e: e.matmul(pn[:, hh * 128:(hh + 1) * 128], kT(0, hh), kT(2, hh), start=True, stop=True),
               [TT_b], [pn_b])
            A, A_b = A_sb[hh]
            op("dve", lambda e: e.tensor_tensor(out=A[:], in0=pa[:], in1=mask4[:], op=ALU.mult), [pa_b, mask4_b], [A_b])
        Ni, Ni_b = Ni2[0]
        op("dve", lambda e: e.tensor_tensor(out=Ni[:], in0=pn[:], in1=maskr4[:], op=ALU.mult), [pn_b, maskr4_b], [Ni_b])
        pq, pq_b = bank[3]
        for hh in range(NH):
            A, A_b = A_sb[hh]
            op("pe", lambda e: e.matmul(pq[:, hh * 64:(hh + 1) * 64], A[:, 256:384], vb[:, hh * 64:(hh + 1) * 64],
                                        start=True, stop=True), [A_b, vb_b], [pq_b])
        op("act", lambda e: e.activation(out=X3[:, :, 64:128], in_=v3(pq[:, 0:256]), func=AF.Copy), [pq_b], [X32_b])
        op("pool", lambda e: e.tensor_copy(out=Xb[:], in_=X32[:]), [X32_b], [Xb_b])
        for s_ in range(6):
            px, px_b = bank[4]
            cur = s_ % 2
            Ni, Ni_b = Ni2[cur]
            NiT, NiT_b = NiT2[cur]
            for hh in range(NH):
                if s_ == 0:
                    lt_, ltb = A_sb[hh][0][:, 0:128], A_sb[hh][1]
                else:
                    lt_, ltb = NiT[:, hh * 128:(hh + 1) * 128], NiT_b
                op("pe", lambda e: e.matmul(px[:, hh * 128:(hh + 1) * 128], lt_, Xb[:, hh * 128:(hh + 1) * 128],
                                            start=True, stop=True), [ltb, Xb_b], [px_b])
            op("dve", lambda e: e.tensor_tensor(out=X32[:], in0=X32[:], in1=px[:], op=ALU.add), [X32_b, px_b], [X32_b])
            op("pool", lambda e: e.tensor_copy(out=Xb[:], in_=X32[:]), [X32_b], [Xb_b])
            if s_ < 5:
                ps1, ps1_b = bank[5]
                ps2, ps2_b = bank[6]
                Nn_, Nn_b = Ni2[1 - cur]
                NnT, NnT_b = NiT2[1 - cur]
                for hh in range(NH):
                    if s_ == 0:
                        lt_, ltb = A_sb[hh][0][:, 0:128], A_sb[hh][1]
                    else:
                        lt_, ltb = NiT[:, hh * 128:(hh + 1) * 128], NiT_b
                    n_ = Ni[:, hh * 128:(hh + 1) * 128]
                    op("pe", lambda e: e.matmul(ps1[:, hh * 128:(hh + 1) * 128], lt_, n_, start=True, stop=True),
                       [ltb, Ni_b], [ps1_b])
                    op("pe", lambda e: e.matmul(ps2[:, hh * 128:(hh + 1) * 128], n_, lt_, start=True, stop=True),
                       [ltb, Ni_b], [ps2_b])
                op("act", lambda e: e.activation(out=Nn_[:], in_=ps1[:], func=AF.Copy), [ps1_b], [Nn_b])
                op("act", lambda e: e.activation(out=NnT[:], in_=ps2[:], func=AF.Copy), [ps2_b], [NnT_b])
        Xb3 = Xb[:].rearrange("p (h c) -> p h c", h=NH)
        pgt, pgt_b = bank[0]
        tlr = tl[:, 1, :]
        for hh in range(NH):
            A, A_b = A_sb[hh]
            op("pe", lambda e: e.matmul(pgt[0:64, hh * 128:(hh + 1) * 128], Xb3[:, hh, 0:64], A[:, 128:256],
                                        start=True, stop=False), [Xb_b, A_b], [pgt_b])
            op("pe", lambda e: e.matmul(pgt[0:64, hh * 128:(hh + 1) * 128], tlr[:, hh * 64:(hh + 1) * 64], ident_bf[:],
                                        start=False, stop=True), [tl_b, ident_bf_b], [pgt_b])
        pgt5 = pgt[0:64, :].rearrange("p (h c t) -> p h c t", h=NH, c=2)
        GT5 = GTs[:].rearrange("p (a h c t) -> p a h c t", a=2, h=NH, c=2)
        for cch in range(2):
            op("act", lambda e: e.activation(out=GT5[:, cch, :, cch, :], in_=pgt5[:, :, cch, :], func=AF.Copy), [pgt_b], [GTs_b])
        pm, pm_b = bank[1]
        pn2, pn2_b = bank[5]
        for cch in range(2):
            op("pool", lambda e: e.tensor_scalar(out=bhm[:, cch, :], in0=bh[:], scalar1=cmask[:, cch:cch + 1], scalar2=None,
                                                 op0=ALU.mult), [bh_b, cmask_b], [bhm_b])
            op("pool", lambda e: e.tensor_scalar(out=khm[:, cch, :], in0=kh[:], scalar1=cmask[:, cch:cch + 1], scalar2=None,
                                                 op0=ALU.mult), [kh_b, cmask_b], [khm_b])
        for cch in range(2):
            for hh in range(NH):
                cs = slice((cch * 4 + hh) * 64, (cch * 4 + hh + 1) * 64)
                hs = slice(hh * 64, (hh + 1) * 64)
                op("pe", lambda e: e.matmul(pm[0:64, cs], Xb3[:, hh, 0:64], bhm[:, cch, hs], start=True, stop=True),
                   [Xb_b, bhm_b], [pm_b])
                op("pe", lambda e: e.matmul(pn2[0:64, cs], bhm[:, cch, hs], Xb3[:, hh, 64:128], start=True, stop=False),
                   [Xb_b, bhm_b], [pn2_b])
                op("pe", lambda e: e.matmul(pn2[0:64, cs], khm[:, cch, hs], vb[:, hs], start=False, stop=True),
                   [khm_b, vb_b], [pn2_b])
        op("act", lambda e: e.activation(out=M0T[:], in_=pm[0:64, :], func=AF.Copy), [pm_b], [M0T_b])
        op("dve", lambda e: e.tensor_copy(out=Nst[:], in_=pn2[0:64, :]), [pn2_b], [Nst_b])
        py, py_b = bank[3]
        for hh in range(NH):
            A, A_b = A_sb[hh]
            hs = slice(hh * 64, (hh + 1) * 64)
            op("pe", lambda e: e.matmul(py[:, hs], A[:, 128:256], Xb3[:, hh, 64:128], start=(hh == 0), stop=False,
                                        skip_group_check=True), [A_b, Xb_b], [py_b])
            op("pe", lambda e: e.matmul(py[:, hs], A[:, 384:512], vb[:, hs], start=False, stop=False, skip_group_check=True),
               [A_b, vb_b], [py_b])
        pst, pst_b = bank[6]
        for cch in range(2):
            ST, ST_b = ST2[st_cur]
            STn, STn_b = ST2[1 - st_cur]
            for hh in range(NH):
                hs = slice(hh * 64, (hh + 1) * 64)
                op("pe", lambda e: e.matmul(py[:, hs], GTs[:, cch * 512 + hh * 128: cch * 512 + (hh + 1) * 128],
                                            ST[:, hs], start=False, stop=True, skip_group_check=True), [GTs_b, ST_b], [py_b])
            for hh in range(NH):
                hs = slice(hh * 64, (hh + 1) * 64)
                cs = slice((cch * 4 + hh) * 64, (cch * 4 + hh + 1) * 64)
                op("pe", lambda e: e.matmul(pst[0:64, hs], M0T[:, cs], ST[:, hs], start=True, stop=True), [M0T_b, ST_b], [pst_b])
            op("dve", lambda e: e.tensor_tensor(out=v3(stt[:]), in0=v3(ST[:]),
                                                in1=wc[:, cch * 4:(cch + 1) * 4].unsqueeze(2).to_broadcast([64, NH, HD]),
                                                op=ALU.mult), [ST_b, wc_b], [stt_b])
            op("dve", lambda e: e.tensor_tensor(out=stt[:], in0=stt[:], in1=Nst[:, cch * 256:(cch + 1) * 256], op=ALU.add),
               [stt_b, Nst_b], [stt_b])
            op("dve", lambda e: e.tensor_tensor(out=STn[:], in0=stt[:], in1=pst[0:64, 0:256], op=ALU.add),
               [stt_b, pst_b], [STn_b])
            st_cur = 1 - st_cur
        op("act", lambda e: e.activation(out=y32[:], in_=py[:, 0:256], func=AF.Copy), [py_b], [y32_b])
        op("dve", lambda e: e.tensor_reduce(out=s4[:, 8:12], in_=v3(y32[:]), axis=AX.X, op=ALU.add), [y32_b], [s4_b])
        op("dve", lambda e: e.tensor_scalar(out=s4[:, 8:12], in0=s4[:, 8:12], scalar1=1.0 / HD, scalar2=None, op0=ALU.mult),
           [s4_b], [s4_b])
        op("dve", lambda e: e.tensor_tensor(out=v3(y32[:]), in0=v3(y32[:]), in1=bc(s4[:, 8:12]), op=ALU.subtract),
           [y32_b, s4_b], [y32_b])
        op("dve", lambda e: e.tensor_tensor(out=t1[:], in0=y32[:], in1=y32[:], op=ALU.mult), [y32_b], [t1_b])
        op("dve", lambda e: e.tensor_reduce(out=s4[:, 12:16], in_=v3(t1[:]), axis=AX.X, op=ALU.add), [t1_b], [s4_b])
        op("act", lambda e: e.activation(out=s4[:, 12:16], in_=s4[:, 12:16], func=AF.Ln, scale=1.0 / HD, bias=GN_EPS),
           [s4_b], [s4_b])
        op("act", lambda e: e.activation(out=s4[:, 12:16], in_=s4[:, 12:16], func=AF.Exp, scale=-0.5), [s4_b], [s4_b])
        op("dve", lambda e: e.tensor_tensor(out=v3(y32[:]), in0=v3(y32[:]), in1=bc(s4[:, 12:16]), op=ALU.mult),
           [y32_b, s4_b], [y32_b])
        op("dve", lambda e: e.tensor_tensor(out=y32[:], in0=y32[:], in1=GNG, op=ALU.mult), [y32_b, pb1_b], [y32_b])
        op("dve", lambda e: e.tensor_tensor(out=y32[:], in0=y32[:], in1=GNB, op=ALU.add), [y32_b, pb1_b], [y32_b])
        op("dve", lambda e: e.tensor_tensor(out=y32[:], in0=y32[:], in1=bon[:], op=ALU.add), [y32_b, bon_b], [y32_b])
        op("dve", lambda e: e.tensor_tensor(out=yg[:], in0=y32[:], in1=sg[:], op=ALU.mult), [y32_b, sg_b], [yg_b])
        ygp = bankT[:, 0:256].rearrange("p (a t) -> p a t", a=2)
        for q_ in range(2):
            op("pe", lambda e: e.transpose(out=ygp[:, q_, :], in_=yg[:, q_ * 128:(q_ + 1) * 128], identity=ident_bf[:]),
               [yg_b, ident_bf_b], [bankT_b])
        ti = i % 4
        op("act", lambda e: e.activation(out=ygT[:, :, ti * 128:(ti + 1) * 128], in_=ygp, func=AF.Copy), [bankT_b], [ygT_b])
        if ti == 3:
            t0 = (i // 4) * 512
            for hd in range(NH):
                sch.dma(ybuf[hd][:, t0:t0 + 512], ygT[(hd % 2) * 64:(hd % 2 + 1) * 64, hd // 2, :], [ygT_b], [ybuf_b[hd]])

    sch.barrier()
    P1.close()

    def gather(c):
        sch.collective(lambda e: e.collective_compute("AllGather", ALU.bypass, replica_groups=[[0, 1], [2, 3], [4, 5], [6, 7]],
                                                      ins=[ybuf[c][:, :]], outs=[yall[c][:, :]]), [ybuf_b[c]], [yall_b[c]])

    if stop_after > 2:
        for c in range(4):
            gather(c)
    if stop_after <= 1:
        sch.finish()
        return nc

    P2 = Scope()

    def sb2(name, shape, dt=F32):
        return P2.sb(name, shape, dt)

    Wf, Wf_b = sb2("Wf", [128, KC, FXC], BF16)
    Wg2, Wg2_b = sb2("Wg2", [128, KC, 256], BF16)
    pb2, pb2_b = sb2("pb2", [128, 132])
    sch.dma(pb2[:], pb2_d[:, :], [], [pb2_b])
    load_weight(Wf, Wf_b, w_fox, FXC)
    load_weight(Wg2, Wg2_b, w_gfx, 256)
    QG = pb2[:, 0:64]
    KG = pb2[:, 64:128]
    BFB = pb2[:, 128:132]
    qgs, qgs_b = sb2("qgs", [128, 64])
    op("dve", lambda e: e.tensor_scalar(out=qgs[:], in0=QG, scalar1=HD ** -0.5, scalar2=None, op0=ALU.mult), [pb2_b], [qgs_b])
    tri_f, tri_f_b = sb2("tri_f", [128, 128])
    sel127, sel127_b = sb2("sel127", [128, 128])
    ones_f, ones_f_b = sb2("ones_f", [128, 64])
    op("pool", lambda e: e.memset(tri_f[:], 1.0), [], [tri_f_b])
    op("pool", lambda e: e.affine_select(out=tri_f[:], in_=tri_f[:], pattern=[[1, 128]], compare_op=ALU.is_ge, fill=0.0,
                                         base=0, channel_multiplier=-1), [tri_f_b], [tri_f_b])
    op("pool", lambda e: e.memset(sel127[:], 1.0), [], [sel127_b])
    op("pool", lambda e: e.affine_select(out=sel127[:], in_=sel127[:], pattern=[[0, 128]], compare_op=ALU.is_ge, fill=0.0,
                                         base=-127, channel_multiplier=1), [sel127_b], [sel127_b])
    op("pool", lambda e: e.memset(ones_f[:], 1.0), [], [ones_f_b])

    KT, KT_b = sb2("KT", [70, NH, S], BF16)
    vaug, vaug_b = sb2("vaug", [128, NT, NH, 65], BF16)
    op("pool", lambda e: e.memset(vaug[:], 1.0), [], [vaug_b])
    bank = [P2.ps("bank2_%d" % i, [128, 512], F32) for i in range(7)]
    bankT, bankT_b = P2.ps("bankT2", [128, 1024], BF16)
    xt2 = [sb2("xtb%d" % i, [128, 1024]) for i in range(2)]
    h, h_b = sb2("hb", [128, 1024], BF16)
    junk, junk_b = sb2("junkb", [128, 1024], BF16)
    ss, ss_b = sb2("ssb", [128, 4])
    hTb, hTb_b = sb2("hTb", [128, KC, 512], BF16)
    qk32, qk32_b = sb2("qk32", [128, 512])
    sq, sq_b = sb2("sq", [128, 512])
    s8, s8_b = sb2("s8", [128, 8])
    cn2 = [sb2("cn%d" % i, [128, 4]) for i in range(2)]
    z4, z4_b = sb2("z4", [128, 4])
    spl, spl_b = sb2("spl", [128, 3, 4], BF16)
    rr, rr_b = sb2("rr", [128, 4])
    qaug2 = [sb2("qaug%d" % i, [128, NH, 70], BF16) for i in range(2)]
    kaug2 = [sb2("kaug%d" % i, [128, NH, 70], BF16) for i in range(2)]
    QT, QT_b = sb2("QT", [70, NH, 512], BF16)
    sgT, sgT_b = sb2("sgT", [64, NH, 512], BF16)
    eg, eg_b = sb2("eg", [64, 512])
    pT3 = [sb2("pT%d" % i, [128, 512], BF16) for i in range(3)]
    osb, osb_b = sb2("osb", [65, 512])
    rec, rec_b = sb2("rec", [65, 512])
    yt32, yt32_b = sb2("yt32", [64, 512])
    yfT2 = [sb2("yfT%d" % i, [64, 512], BF16) for i in range(2)]
    for i_ in range(2):
        op("pool", lambda e: e.memset(qaug2[i_][0][:], 1.0), [], [qaug2[i_][1]])
        op("pool", lambda e: e.memset(kaug2[i_][0][:], 1.0), [], [kaug2[i_][1]])
    op("dve", lambda e: e.memset(cn2[1][0][:], 0.0), [], [cn2[1][1]])

    pt_rot = 0
    sc_rot = 0
    yf_rot = 0
    for qb in range(NB):
        for ti in range(4):
            i = qb * 4 + ti
            xt, xt_b = xt2[i % 2]
            sch.dma(xt[:], x[i * 128:(i + 1) * 128, :], [], [xt_b])
            rmsnorm_to_hT(xt, xt_b, h, h_b, junk, junk_b, ss, ss_b, bankT, bankT_b,
                          lambda: hTb[:, :, ti * 128:(ti + 1) * 128], hTb_b)
            pf0, pf0_b = bank[0]
            pf1, pf1_b = bank[1]
            for (pt, ptb, n0, n1) in ((pf0, pf0_b, 0, 512), (pf1, pf1_b, 512, FXC)):
                for c in range(KC):
                    op("pe", lambda e: e.matmul(pt[:, 0:n1 - n0], hTb[:, c, ti * 128:(ti + 1) * 128], Wf[:, c, n0:n1],
                                                start=(c == 0), stop=(c == KC - 1)), [hTb_b, Wf_b], [ptb])
            qa, qa_b = qaug2[i % 2]
            ka, ka_b = kaug2[i % 2]
            op("act", lambda e: e.activation(out=qk32[:], in_=pf0[:], func=AF.Copy), [pf0_b], [qk32_b])
            op("dve", lambda e: e.tensor_tensor(out=sq[:], in0=qk32[:], in1=qk32[:], op=ALU.mult), [qk32_b], [sq_b])
            op("dve", lambda e: e.tensor_reduce(out=s8[:], in_=v3(sq[:], 8), axis=AX.X, op=ALU.add), [sq_b], [s8_b])
            op("act", lambda e: e.activation(out=s8[:], in_=s8[:], func=AF.Ln, scale=1.0 / HD, bias=NORM_EPS), [s8_b], [s8_b])
            op("act", lambda e: e.activation(out=s8[:], in_=s8[:], func=AF.Exp, scale=-0.5), [s8_b], [s8_b])
            op("dve", lambda e: e.tensor_tensor(out=v3(qk32[:], 8), in0=v3(qk32[:], 8), in1=bc(s8[:], 8), op=ALU.mult),
               [qk32_b, s8_b], [qk32_b])
            op("dve", lambda e: e.tensor_tensor(out=qa[:, :, 0:64], in0=v3(qk32[:, 0:256]),
                                                in1=qgs[:].unsqueeze(1).to_broadcast([128, NH, HD]), op=ALU.mult),
               [qk32_b, qgs_b], [qa_b])
            op("dve", lambda e: e.tensor_tensor(out=ka[:, :, 0:64], in0=v3(qk32[:, 256:512]),
                                                in1=KG.unsqueeze(1).to_broadcast([128, NH, HD]), op=ALU.mult),
               [qk32_b, pb2_b], [ka_b])
            op("act", lambda e: e.activation(out=vaug[:, i, :, 0:64], in_=v3(pf1[:, 0:256]), func=AF.Copy), [pf1_b], [vaug_b])
            op("dve", lambda e: e.tensor_tensor(out=z4[:], in0=pf1[:, 256:260], in1=BFB, op=ALU.add), [pf1_b, pb2_b], [z4_b])
            op("act", lambda e: e.activation(out=z4[:], in_=z4[:], func=AF.Exp, scale=-1.0), [z4_b], [z4_b])
            op("act", lambda e: e.activation(out=z4[:], in_=z4[:], func=AF.Ln, bias=1.0), [z4_b], [z4_b])
            cn, cn_b = cn2[i % 2]
            cnp, cnp_b = cn2[(i + 1) % 2]
            pcs, pcs_b = bank[2]
            op("pe", lambda e: e.matmul(pcs[:, 0:4], tri_f[:], z4[:], start=True, stop=False), [tri_f_b, z4_b], [pcs_b])
            op("pe", lambda e: e.matmul(pcs[:, 0:4], sel127[:], cnp[:], start=False, stop=True), [sel127_b, cnp_b], [pcs_b])
            op("dve", lambda e: e.tensor_copy(out=cn[:], in_=pcs[:, 0:4]), [pcs_b], [cn_b])
            op("dve", lambda e: e.tensor_copy(out=spl[:, 0, :], in_=cn[:]), [cn_b], [spl_b])
            op("dve", lambda e: e.tensor_tensor(out=rr[:], in0=cn[:], in1=spl[:, 0, :], op=ALU.subtract), [cn_b, spl_b], [rr_b])
            op("dve", lambda e: e.tensor_copy(out=spl[:, 1, :], in_=rr[:]), [rr_b], [spl_b])
            op("dve", lambda e: e.tensor_tensor(out=rr[:], in0=rr[:], in1=spl[:, 1, :], op=ALU.subtract), [rr_b, spl_b], [rr_b])
            op("dve", lambda e: e.tensor_copy(out=spl[:, 2, :], in_=rr[:]), [rr_b], [spl_b])
            op("dve", lambda e: e.tensor_copy(out=ka[:, :, 67:70], in_=spl[:].rearrange("p s h -> p h s")), [spl_b], [ka_b])
            op("dve", lambda e: e.tensor_scalar(out=qa[:, :, 64:67], in0=spl[:].rearrange("p s h -> p h s"), scalar1=-1.0,
                                                scalar2=None, op0=ALU.mult), [spl_b], [qa_b])
            tqp = bankT[:].rearrange("p (a t) -> p a t", a=8)
            for hh in range(NH):
                op("pe", lambda e: e.transpose(out=tqp[0:70, hh, :], in_=qa[:, hh, :], identity=ident_bf[:]),
                   [qa_b, ident_bf_b], [bankT_b])
                op("pe", lambda e: e.transpose(out=tqp[0:70, 4 + hh, :], in_=ka[:, hh, :], identity=ident_bf[:]),
                   [ka_b, ident_bf_b], [bankT_b])
            op("dve", lambda e: e.tensor_copy(out=QT[:, :, ti * 128:(ti + 1) * 128], in_=tqp[0:70, 0:4, :]), [bankT_b], [QT_b])
            op("act", lambda e: e.activation(out=KT[:, :, i * 128:(i + 1) * 128], in_=tqp[0:70, 4:8, :], func=AF.Copy),
               [bankT_b], [KT_b])
        for hh in range(NH):
            pgt, pgt_b = bank[2]
            for c in range(KC):
                op("pe", lambda e: e.matmul(pgt[0:64, :], Wg2[:, c, hh * 64:(hh + 1) * 64], hTb[:, c, :], start=(c == 0),
                                            stop=(c == KC - 1)), [Wg2_b, hTb_b], [pgt_b])
            op("act", lambda e: e.activation(out=eg[:], in_=pgt[0:64, :], func=AF.Exp, scale=-1.0), [pgt_b], [eg_b])
            op("dve", lambda e: e.tensor_scalar(out=eg[:], in0=eg[:], scalar1=1.0, scalar2=None, op0=ALU.add), [eg_b], [eg_b])
            op("dve", lambda e: e.reciprocal(out=eg[:], in_=eg[:]), [eg_b], [eg_b])
            op("dve", lambda e: e.tensor_tensor(out=sgT[:, hh, :], in0=eg[:], in1=pgt[0:64, :], op=ALU.mult), [eg_b, pgt_b], [sgT_b])
        nkt = 4 * (qb + 1)
        for hh in range(NH):
            po, po_b = bank[3]
            for jt in range(nkt):
                jl = jt - 4 * qb
                col0 = 128 * jl if jl > 0 else 0
                sc, sc_b = bank[4 + sc_rot]
                sc_rot = (sc_rot + 1) % 3
                pT, pT_b = pT3[pt_rot]
                pt_rot = (pt_rot + 1) % 3
                op("pe", lambda e: e.matmul(sc[:, col0:512], KT[:, hh, jt * 128:(jt + 1) * 128], QT[:, hh, col0:512],
                                            start=True, stop=True), [KT_b, QT_b], [sc_b])
                op("act", lambda e: e.activation(out=pT[:, col0:512], in_=sc[:, col0:512], func=AF.Exp), [sc_b], [pT_b])
                if jl >= 0:
                    op("pool", lambda e: e.affine_select(out=pT[:, col0:col0 + 128], in_=pT[:, col0:col0 + 128],
                                                         pattern=[[1, 128]], compare_op=ALU.is_ge, fill=0.0, base=0,
                                                         channel_multiplier=-1), [pT_b], [pT_b])
                op("pe", lambda e: e.matmul(po[0:65, col0:512], vaug[:, jt, hh, :], pT[:, col0:512], start=(jt == 0),
                                            stop=(jt == nkt - 1)), [vaug_b, pT_b], [po_b])
            op("act", lambda e: e.activation(out=osb[:], in_=po[0:65, :], func=AF.Copy), [po_b], [osb_b])
            op("dve", lambda e: e.reciprocal(out=rec[64:65, :], in_=osb[64:65, :]), [osb_b], [rec_b])
            pbc, pbc_b = bank[2]
            op("pe", lambda e: e.matmul(pbc[0:64, :], ones_f[64:65, :], rec[64:65, :], start=True, stop=True),
               [ones_f_b, rec_b], [pbc_b])
            op("dve", lambda e: e.tensor_tensor(out=yt32[:], in0=osb[0:64, :], in1=pbc[0:64, :], op=ALU.mult),
               [osb_b, pbc_b], [yt32_b])
            yfT, yfT_b = yfT2[yf_rot]
            yf_rot = 1 - yf_rot
            op("dve", lambda e: e.tensor_tensor(out=yfT[:], in0=yt32[:], in1=sgT[:, hh, :], op=ALU.mult),
               [yt32_b, sgT_b], [yfT_b])
            t0 = qb * 512
            sch.dma(ybuf[4 + hh][:, t0:t0 + 512], yfT[:], [yfT_b], [ybuf_b[4 + hh]])

    sch.barrier()
    P2.close()
    if stop_after <= 2:
        sch.finish()
        return nc

    for c in range(4, 8):
        gather(c)
    P3 = Scope()

    def sb3(name, shape, dt=F32):
        return P3.sb(name, shape, dt)

    Wo, Wo_b = sb3("Wo", [128, KC, D_MODEL], BF16)
    onesg, onesg_b = sb3("onesg", [128, KC])
    op("dve", lambda e: e.memset(onesg[:], 1.0), [], [onesg_b])
    gcol, gcol_b = onesg, onesg_b
    load_weight(Wo, Wo_b, w_outp, D_MODEL)
    fg, fg_b = sb3("fg", [128, D_MODEL])
    sch.dma(fg[:], pb3_d[:, :], [], [fg_b])
    bank = [P3.ps("bank3_%d" % i, [128, 512], F32) for i in range(4)]
    yT2 = [sb3("yT%d" % i, [128, KC, 512], BF16) for i in range(2)]
    yA, yA_b = sb3("yA", [128, KC, 512], BF16)
    yB, yB_b = sb3("yB", [128, KC, 512], BF16)
    sel, sel_b = sb3("sel", [128, 2])
    sch.dma(sel[:], sel_d[:, :], [], [sel_b])
    xr2 = [sb3("xr%d" % i, [128, D_MODEL]) for i in range(2)]
    z2 = [sb3("z%d" % i, [128, D_MODEL]) for i in range(2)]
    junk, junk_b = sb3("junk3", [128, D_MODEL], BF16)
    ss3, ss3_b = sb3("ss3", [128, 4])
    for blk in range(SH // 512):
        yT, yT_b = yT2[blk % 2]
        for c in range(KC):
            sch.dma(yA[:, c, :], yall[c][:, blk * 512:(blk + 1) * 512], [yall_b[c]], [yA_b])
            sch.dma(yB[:, c, :], yall[c][:, SH + blk * 512:SH + (blk + 1) * 512], [yall_b[c]], [yB_b])
        op("pool", lambda e: e.tensor_scalar(out=yT[:], in0=yA[:], scalar1=sel[:, 0:1], scalar2=None, op0=ALU.mult),
           [yA_b, sel_b], [yT_b])
        op("dve", lambda e: e.scalar_tensor_tensor(out=yT[:], in0=yB[:], scalar=sel[:, 1:2], in1=yT[:], op0=ALU.mult,
                                                   op1=ALU.add), [yB_b, sel_b, yT_b], [yT_b])
        for ti in range(4):
            i = blk * 4 + ti
            xr, xr_b = xr2[i % 2]
            z, z_b = z2[i % 2]
            sch.dma(xr[:], xres[i * 128:(i + 1) * 128, :], [], [xr_b])
            for nn in range(2):
                po, po_b = bank[(i % 2) * 2 + nn]
                for c in range(KC):
                    op("pe", lambda e: e.matmul(po[:], yT[:, c, ti * 128:(ti + 1) * 128], Wo[:, c, nn * 512:(nn + 1) * 512],
                                                start=(c == 0), stop=(c == KC - 1)), [yT_b, Wo_b], [po_b])
                op("dve", lambda e: e.tensor_tensor(out=z[:, nn * 512:(nn + 1) * 512], in0=po[:], in1=xr[:, nn * 512:(nn + 1) * 512],
                                                    op=ALU.add), [po_b, xr_b], [z_b])
            op("act", lambda e: e.activation(out=junk[:], in_=z[:], func=AF.Square, accum_out=ss3[:, 0:1]), [z_b], [junk_b, ss3_b])
            op("act", lambda e: e.activation(out=ss3[:, 1:2], in_=ss3[:, 0:1], func=AF.Ln, scale=1.0 / D_MODEL, bias=NORM_EPS),
               [ss3_b], [ss3_b])
            op("act", lambda e: e.activation(out=ss3[:, 2:3], in_=ss3[:, 1:2], func=AF.Exp, scale=-0.5), [ss3_b], [ss3_b])
            op("dve", lambda e: e.scalar_tensor_tensor(out=z[:], in0=z[:], scalar=ss3[:, 2:3], in1=fg[:], op0=ALU.mult,
                                                       op1=ALU.mult), [z_b, ss3_b, fg_b], [z_b])
            sch.dma(out[i * 128:(i + 1) * 128, :], z[:], [z_b], [out_b])
    sch.finish()
    return nc


def make_core_inputs(inputs, b, g, S):
    f = lambda a: np.ascontiguousarray(np.asarray(a, dtype=np.float32))
    w_in = np.asarray(inputs["w_in"])[0]
    SH = S // 2
    hs = slice(g * 256, (g + 1) * 256)

    def cols(base):
        return np.arange(base + g * 256, base + (g + 1) * 256)

    rw_cols = np.concatenate([cols(0), cols(512), cols(1024), np.arange(1536, 1664)])
    fx0 = 1664
    fox_cols = np.concatenate([cols(fx0), cols(fx0 + 512), cols(fx0 + 1024), np.arange(fx0 + 1536 + 4 * g, fx0 + 1536 + 4 * g + 4)])
    g0 = 1664 + 1544
    mu = np.asarray(inputs["rw_mu"])[0][rw_cols]
    rep = lambda v: np.broadcast_to(np.asarray(v, np.float32).reshape(1, -1), (128, np.asarray(v).size))
    pb1 = np.concatenate([rep(mu), rep(np.asarray(inputs["rw_w0"])[0][hs]), rep(np.asarray(inputs["rw_a0"])[0][hs]),
                          rep(np.asarray(inputs["rw_k_k"])[0][hs]), rep(np.asarray(inputs["rw_k_a"])[0][hs]),
                          rep(np.asarray(inputs["rw_r_k"])[0][4 * g:4 * g + 4].reshape(-1)),
                          rep(np.asarray(inputs["rw_gn_g"])[0][hs]), rep(np.asarray(inputs["rw_gn_b"])[0][hs])], axis=1)
    pb2 = np.concatenate([rep(np.asarray(inputs["fox_q_g"])[0]), rep(np.asarray(inputs["fox_k_g"])[0]),
                          rep(np.asarray(inputs["fox_b_f"])[0][4 * g:4 * g + 4])], axis=1)
    w_out = np.asarray(inputs["w_out"])[0]
    rows = []
    for c in range(8):
        for gg in range(2):
            base = (4 * gg + c) * 64 if c < 4 else 512 + (4 * gg + c - 4) * 64
            rows.append(np.arange(base, base + 64))
    w_outp = w_out[np.concatenate(rows)]
    xb = np.asarray(inputs["x"])[b]
    return {
        "x": f(xb),
        "xres": f(xb[g * SH:(g + 1) * SH]),
        "w_rw": f(w_in[:, rw_cols]),
        "w_fox": f(w_in[:, fox_cols]),
        "w_grw": f(w_in[:, cols(g0)]),
        "w_gfx": f(w_in[:, cols(g0 + 512)]),
        "w_outp": f(w_outp),
        "g_col": f(np.asarray(inputs["norm_g"])[0].reshape(KC, 128).T),
        "pb1": f(pb1),
        "pb2": f(pb2),
        "pb3": f(rep(np.asarray(inputs["final_g"]))),
        "w_up": f(np.asarray(inputs["rw_w_up"])[0][:, hs]),
        "a_up": f(np.asarray(inputs["rw_a_up"])[0][:, hs]),
        "sel": f(np.tile(np.array([[1.0 - g, float(g)]], np.float32), (128, 1))),
    }


def kernel(**inputs):
    x = np.asarray(inputs["x"])
    B, S, D = x.shape
    nc = build_nc(S)
    in_maps = [make_core_inputs(inputs, c // 2, c % 2, S) for c in range(2 * B)]
    res = run_bass_kernel_spmd(nc, in_maps, core_ids=list(range(2 * B)))
    out = np.empty((B, S, D), np.float32)
    SH = S // 2
    for c in range(2 * B):
        out[c // 2, (c % 2) * SH:(c % 2 + 1) * SH] = res.results[c]["out"]
    return out
```

```python
import numpy as np
import concourse.bass as bass
import concourse.mybir as mybir
from concourse.bass_utils import run_bass_kernel_spmd

F32 = mybir.dt.float32
BF16 = mybir.dt.bfloat16
AF = mybir.ActivationFunctionType
ALU = mybir.AluOpType
AX = mybir.AxisListType

D_MODEL = 1024
KC = 8
HD = 64
NH = 4
RWC = 896
FXC = 772
NORM_EPS = 1e-6
GN_EPS = 64e-5
N_DCH = 6


class Buf:
    __slots__ = ("name", "lw", "rd", "psum")

    def __init__(self, name, psum=False):
        self.name = name
        self.lw = None
        self.rd = {}
        self.psum = psum


class Sched:
    def __init__(self, nc):
        self.nc = nc
        self.eng = {"pe": nc.tensor, "act": nc.scalar, "dve": nc.vector, "pool": nc.gpsimd, "sp": nc.sync}
        self.sem = {}
        self.cnt = {}
        for k in self.eng:
            self.sem[k] = nc.semaphore("sem_" + k).__enter__()
            self.cnt[k] = 0
        for c in range(N_DCH):
            k = "d%d" % c
            self.sem[k] = nc.semaphore("sem_" + k).__enter__()
            self.cnt[k] = 0
        self.sem["cc"] = nc.semaphore("sem_cc").__enter__()
        self.cnt["cc"] = 0
        self.waited = {k: {} for k in self.eng}
        self.next_dch = 0
        self.limit = None
        self.total = 0
        self.log = []

    def _lim(self):
        self.total += 1
        if self.log is not None:
            import sys as _s
            fr = _s._getframe(2)
            self.log.append((self.total, fr.f_lineno))
        return self.limit is not None and self.total > self.limit

    def _val(self, k, i):
        return i * 16 if (k[0] == "d" and k != "dve") else i

    def _deps(self, reads, writes):
        deps = {}

        def need(k, i):
            if deps.get(k, 0) < i:
                deps[k] = i

        for b in reads:
            if b.lw is not None:
                need(*b.lw)
            if b.psum:
                for k, i in b.rd.items():
                    need(k, i)
        for b in writes:
            if b.lw is not None:
                need(*b.lw)
            for k, i in b.rd.items():
                need(k, i)
        return deps

    def _wait(self, e, deps):
        w = self.waited[e]
        for k, i in deps.items():
            if w.get(k, 0) < i:
                self.eng[e].wait_ge(self.sem[k], self._val(k, i))
                w[k] = i

    def _commit(self, me, reads, writes):
        k, i = me
        for b in reads:
            b.rd[k] = i
        for b in writes:
            b.lw = me
            b.rd = {}

    def op(self, e, fn, reads=(), writes=()):
        if self._lim():
            return
        deps = self._deps(reads, writes)
        if e == "pe":
            deps.pop("pe", None)
        self._wait(e, deps)
        ins = fn(self.eng[e])
        self.cnt[e] += 1
        ins.then_inc(self.sem[e], 1)
        self._commit((e, self.cnt[e]), reads, writes)

    def dma(self, out, in_, reads=(), writes=(), q="sp"):
        if self._lim():
            return
        c = "d%d" % self.next_dch
        self.next_dch = (self.next_dch + 1) % N_DCH
        deps = self._deps(reads, writes)
        if self.cnt[c] > 0:
            deps[c] = max(deps.get(c, 0), self.cnt[c])
        self._wait(q, deps)
        ins = self.eng[q].dma_start(out=out, in_=in_)
        self.cnt[c] += 1
        ins.then_inc(self.sem[c], 16)
        self._commit((c, self.cnt[c]), reads, writes)

    def collective(self, fn, reads=(), writes=()):
        if self._lim():
            return
        deps = self._deps(reads, writes)
        self._wait("pool", deps)
        ins = fn(self.eng["pool"])
        self.cnt["cc"] += 1
        ins.then_inc(self.sem["cc"], 1)
        self._commit(("cc", self.cnt["cc"]), reads, writes)

    def barrier(self):
        for e in self.eng:
            deps = {k: c for k, c in self.cnt.items() if c > 0 and k != e}
            self._wait(e, deps)

    def finish(self):
        deps = {k: c for k, c in self.cnt.items() if c > 0 and k != "sp"}
        self._wait("sp", deps)


def build_nc(S, debug=False, stop_after=9, limit=None):
    NT = S // 128
    NB = S // 512
    SH = S // 2
    nc = bass.Bass("TRN2", target_bir_lowering=False)

    def din(name, shape, dt=F32):
        return nc.dram_tensor(name, list(shape), dt, kind="ExternalInput").ap()

    x = din("x", [S, D_MODEL])
    xres = din("xres", [SH, D_MODEL])
    w_rw = din("w_rw", [D_MODEL, RWC])
    w_fox = din("w_fox", [D_MODEL, FXC])
    w_grw = din("w_grw", [D_MODEL, 256])
    w_gfx = din("w_gfx", [D_MODEL, 256])
    w_outp = din("w_outp", [D_MODEL, D_MODEL])
    g_col = din("g_col", [128, KC])
    pb1_d = din("pb1", [128, 2688])
    pb2_d = din("pb2", [128, 132])
    pb3_d = din("pb3", [128, D_MODEL])
    w_up_d = din("w_up", [64, 256])
    a_up_d = din("a_up", [64, 256])
    out = nc.dram_tensor("out", [SH, D_MODEL], F32, kind="ExternalOutput").ap()
    dbg_kind = "ExternalOutput" if debug else "Internal"
    ybuf = [nc.dram_tensor("ybuf%d" % c, [64, S], BF16, kind=dbg_kind).ap() for c in range(8)]
    yall = [nc.dram_tensor("yall%d" % c, [128, S], BF16, kind=dbg_kind).ap() for c in range(8)]
    sel_d = din("sel", [128, 2])
    ybuf_b = [Buf("ybuf%d" % c) for c in range(8)]
    yall_b = [Buf("yall%d" % c) for c in range(8)]
    out_b = Buf("out")

    sch = Sched(nc)
    sch.limit = limit
    op = sch.op

    class Scope:
        def __init__(self):
            self.stack = []

        def sb(self, name, shape, dt=F32):
            cm = nc.sbuf_tensor("s_" + name, list(shape), dt)
            t = cm.__enter__()
            self.stack.append(cm)
            return t, Buf(name)

        def ps(self, name, shape, dt=F32):
            cm = nc.psum_tensor("p_" + name, list(shape), dt)
            t = cm.__enter__()
            self.stack.append(cm)
            return t, Buf(name, psum=True)

        def close(self):
            while self.stack:
                self.stack.pop().__exit__(None, None, None)

    G = Scope()
    ident_bf, ident_bf_b = G.sb("ident_bf", [128, 128], BF16)
    ident_f, ident_f_b = G.sb("ident_f", [128, 128], F32)
    gcol, gcol_b = G.sb("gcol", [128, KC], F32)
    stage, stage_b = G.sb("stage", [128, 1024], F32)
    stage2, stage2_b = G.sb("stage2", [128, 1024], F32)

    def mk_ident(t, b):
        op("pool", lambda e: e.memset(t[:], 0.0), [], [b])
        op("pool", lambda e: e.affine_select(out=t[:], in_=t[:], pattern=[[-1, 128]], compare_op=ALU.not_equal,
                                             fill=1.0, base=0, channel_multiplier=1), [b], [b])

    mk_ident(ident_bf, ident_bf_b)
    mk_ident(ident_f, ident_f_b)
    sch.dma(gcol[:], g_col[:, :], [], [gcol_b])

    def load_weight(dst, dst_b, src, ncols, post=None):
        for c in range(KC):
            st, stb = (stage, stage_b) if c % 2 == 0 else (stage2, stage2_b)
            sch.dma(st[:, 0:ncols], src[c * 128:(c + 1) * 128, :], [], [stb])
            if post is None:
                op("dve", lambda e: e.tensor_scalar(out=dst[:, c, :], in0=st[:, 0:ncols], scalar1=gcol[:, c:c + 1],
                                                    scalar2=None, op0=ALU.mult), [stb, gcol_b], [dst_b])
            else:
                for (d2, d2b, pt, ptb) in post:
                    op("dve", lambda e: e.scalar_tensor_tensor(out=d2[:, c, :], in0=st[:, 0:ncols],
                                                               scalar=gcol[:, c:c + 1], in1=pt, op0=ALU.mult,
                                                               op1=ALU.mult), [stb, gcol_b, ptb], [d2b])

    def rmsnorm_to_hT(xt, xt_b, h, h_b, junk, junk_b, ss, ss_b, tpb, tpb_b, dst_ap_fn, dst_b):
        op("act", lambda e: e.activation(out=junk[:], in_=xt[:], func=AF.Square, accum_out=ss[:, 0:1]),
           [xt_b], [junk_b, ss_b])
        op("act", lambda e: e.activation(out=ss[:, 1:2], in_=ss[:, 0:1], func=AF.Ln, scale=1.0 / D_MODEL,
                                         bias=NORM_EPS), [ss_b], [ss_b])
        op("act", lambda e: e.activation(out=ss[:, 2:3], in_=ss[:, 1:2], func=AF.Exp, scale=-0.5), [ss_b], [ss_b])
        op("act", lambda e: e.activation(out=h[:], in_=xt[:], func=AF.Copy, scale=ss[:, 2:3]), [xt_b, ss_b], [h_b])
        tpv = tpb[:].rearrange("p (c t) -> p c t", c=KC)
        for c in range(KC):
            op("pe", lambda e: e.transpose(out=tpv[:, c, :], in_=h[:, c * 128:(c + 1) * 128], identity=ident_bf[:]),
               [h_b, ident_bf_b], [tpb_b])
        op("dve", lambda e: e.tensor_copy(out=dst_ap_fn(), in_=tpv), [tpb_b], [dst_b])

    P1 = Scope()
    Wa, Wa_b = P1.sb("Wa", [128, KC, RWC], BF16)
    Wb, Wb_b = P1.sb("Wb", [128, KC, RWC], BF16)
    Wg, Wg_b = P1.sb("Wg1", [128, KC, 256], BF16)
    pb1, pb1_b = P1.sb("pb1", [128, 2688], F32)
    omm, omm_b = P1.sb("omm", [128, RWC], F32)
    wup, wup_b = P1.sb("wup", [64, 256], F32)
    aup, aup_b = P1.sb("aup", [64, 256], F32)
    sch.dma(pb1[:], pb1_d[:, :], [], [pb1_b])
    sch.dma(wup[:], w_up_d[:, :], [], [wup_b])
    sch.dma(aup[:], a_up_d[:, :], [], [aup_b])
    MU = pb1[:, 0:896]
    W0 = pb1[:, 896:1152]
    A0 = pb1[:, 1152:1408]
    KK_ = pb1[:, 1408:1664]
    KA_ = pb1[:, 1664:1920]
    RK_ = pb1[:, 1920:2176]
    GNG = pb1[:, 2176:2432]
    GNB = pb1[:, 2432:2688]
    op("dve", lambda e: e.tensor_scalar(out=omm[:], in0=MU, scalar1=-1.0, scalar2=1.0, op0=ALU.mult, op1=ALU.add),
       [pb1_b], [omm_b])
    load_weight(None, None, w_rw, RWC, post=[(Wa, Wa_b, omm[:], omm_b), (Wb, Wb_b, MU, pb1_b)])
    load_weight(Wg, Wg_b, w_grw, 256)

    tri_i, tri_i_b = P1.sb("tri_i", [128, 128], F32)
    tri_s, tri_s_b = P1.sb("tri_s", [128, 128], F32)
    tri_r, tri_r_b = P1.sb("tri_r", [128, 128], F32)
    mask4, mask4_b = P1.sb("mask4", [128, 512], F32)
    maskr4, maskr4_b = P1.sb("maskr4", [128, 512], F32)
    ones_c, ones_c_b = P1.sb("ones_c", [128, 1], F32)

    def mk_tri(t, b, cm, pat, cmp_op):
        op("pool", lambda e: e.memset(t[:], 1.0), [], [b])
        op("pool", lambda e: e.affine_select(out=t[:], in_=t[:], pattern=[[pat, 128]], compare_op=cmp_op, fill=0.0,
                                             base=0, channel_multiplier=cm), [b], [b])
        op("pool", lambda e: e.memset(t[0:64, 64:128], 0.0), [b], [b])
        op("pool", lambda e: e.memset(t[64:128, 0:64], 0.0), [b], [b])

    mk_tri(tri_i, tri_i_b, -1, 1, ALU.is_ge)
    mk_tri(tri_s, tri_s_b, -1, 1, ALU.is_gt)
    mk_tri(tri_r, tri_r_b, 1, -1, ALU.is_gt)
    op("pool", lambda e: e.memset(ones_c[:], 1.0), [], [ones_c_b])
    for q_, (src, srcb) in enumerate([(tri_s, tri_s_b), (tri_i, tri_i_b), (tri_s, tri_s_b), (tri_i, tri_i_b)]):
        op("pool", lambda e: e.tensor_copy(out=mask4[:, q_ * 128:(q_ + 1) * 128], in_=src[:]), [srcb], [mask4_b])
        op("pool", lambda e: e.tensor_copy(out=maskr4[:, q_ * 128:(q_ + 1) * 128], in_=tri_r[:]), [tri_r_b], [maskr4_b])

    bank = [P1.ps("bank%d" % i, [128, 512], F32) for i in range(7)]
    bankT, bankT_b = P1.ps("bankT", [128, 1024], BF16)

    def sb(name, shape, dt=F32):
        return P1.sb(name, shape, dt)

    xt2 = [sb("xt%d" % i, [128, 1024]) for i in range(2)]
    h, h_b = sb("h", [128, 1024], BF16)
    junk, junk_b = sb("junk", [128, 1024], BF16)
    ss, ss_b = sb("ss", [128, 4])
    hT2 = [sb("hT%d" % i, [128, KC, 129], BF16) for i in range(2)]
    r32, r32_b = sb("r32", [128, 256])
    k32, k32_b = sb("k32", [128, 256])
    v32, v32_b = sb("v32", [128, 256])
    lt32, lt32_b = sb("lt32", [128, 128])
    lT, lT_b = sb("lT", [64, 2, 128])
    lwn, lwn_b = sb("lwn", [128, 256])
    t1, t1_b = sb("t1", [128, 256])
    t2, t2_b = sb("t2", [128, 256])
    a_, a_b = sb("a_", [128, 256])
    kk, kk_b = sb("kk", [128, 256])
    km, km_b = sb("km", [128, 256])
    ba, ba_b = sb("ba", [128, 256])
    s4, s4_b = sb("s4", [128, 16])
    ecum, ecum_b = sb("ecum", [128, 256])
    encum, encum_b = sb("encum", [128, 256])
    eprev, eprev_b = sb("eprev", [128, 256])
    erc, erc_b = sb("erc", [128, 256])
    wc, wc_b = sb("wc", [64, 8])
    tl4 = sb("tl4", [128, 4, 256], BF16)
    bh, bh_b = sb("bh", [128, 256], BF16)
    kh, kh_b = sb("kh", [128, 256], BF16)
    vb, vb_b = sb("vb", [128, 256], BF16)
    bhm, bhm_b = sb("bhm", [128, 2, 256], BF16)
    khm, khm_b = sb("khm", [128, 2, 256], BF16)
    cmask, cmask_b = sb("cmask", [128, 2])
    op("pool", lambda e: e.memset(cmask[:], 0.0), [], [cmask_b])
    op("pool", lambda e: e.memset(cmask[0:64, 0:1], 1.0), [cmask_b], [cmask_b])
    op("pool", lambda e: e.memset(cmask[64:128, 1:2], 1.0), [cmask_b], [cmask_b])
    TT, TT_b = sb("TT", [64, 16, 128], BF16)
    A_sb = [sb("A%d" % hh, [128, 512], BF16) for hh in range(NH)]
    Ni2 = [sb("Ni%d" % i, [128, 512], BF16) for i in range(2)]
    NiT2 = [sb("NiT%d" % i, [128, 512], BF16) for i in range(2)]
    X32, X32_b = sb("X32", [128, 512])
    Xb, Xb_b = sb("Xb", [128, 512], BF16)
    GTs, GTs_b = sb("GTs", [64, 1024])
    M0T, M0T_b = sb("M0T", [64, 512])
    Nst, Nst_b = sb("Nst", [64, 512])
    ST2 = [sb("ST%d" % i, [64, 256]) for i in range(2)]
    stt, stt_b = sb("stt", [64, 256])
    y32, y32_b = sb("y32", [128, 256])
    sg, sg_b = sb("sg", [128, 256])
    bon, bon_b = sb("bon", [128, 256])
    yg, yg_b = sb("yg", [128, 256], BF16)
    ygT, ygT_b = sb("ygT", [128, 2, 512], BF16)

    op("dve", lambda e: e.memset(ST2[0][0][:], 0.0), [], [ST2[0][1]])
    op("dve", lambda e: e.memset(GTs[:], 0.0), [], [GTs_b])
    op("dve", lambda e: e.memset(hT2[0][0][:, :, 0:1], 0.0), [], [hT2[0][1]])

    def v3(ap, hh=NH):
        return ap.rearrange("p (h d) -> p h d", h=hh)

    def bc(ap_small, hh=NH, d=HD):
        return ap_small.unsqueeze(2).to_broadcast([128, hh, d])

    st_cur = 0
    for i in range(NT):
        xt, xt_b = xt2[i % 2]
        hT, hT_b = hT2[i % 2]
        hTp, hTp_b = hT2[(i + 1) % 2]
        if i == 0:
            sch.dma(xt[:], x[0:128, :], [], [xt_b])
        if i + 1 < NT:
            sch.dma(xt2[(i + 1) % 2][0][:], x[(i + 1) * 128:(i + 2) * 128, :], [], [xt2[(i + 1) % 2][1]])
        rmsnorm_to_hT(xt, xt_b, h, h_b, junk, junk_b, ss, ss_b, bankT, bankT_b, lambda: hT[:, :, 1:129], hT_b)
        if i > 0:
            op("pool", lambda e: e.tensor_copy(out=hT[:, :, 0:1], in_=hTp[:, :, 128:129]), [hTp_b], [hT_b])
        pr0, pr0_b = bank[0]
        pr1, pr1_b = bank[1]
        for (pt, ptb, n0, n1) in ((pr0, pr0_b, 0, 512), (pr1, pr1_b, 512, 896)):
            for c in range(KC):
                op("pe", lambda e: e.matmul(pt[:, 0:n1 - n0], hT[:, c, 1:129], Wa[:, c, n0:n1], start=(c == 0), stop=False),
                   [hT_b, Wa_b], [ptb])
            for c in range(KC):
                op("pe", lambda e: e.matmul(pt[:, 0:n1 - n0], hT[:, c, 0:128], Wb[:, c, n0:n1], start=False, stop=(c == KC - 1)),
                   [hT_b, Wb_b], [ptb])
        pg, pg_b = bank[2]
        for c in range(KC):
            op("pe", lambda e: e.matmul(pg[:, 0:256], hT[:, c, 1:129], Wg[:, c, :], start=(c == 0), stop=(c == KC - 1)),
               [hT_b, Wg_b], [pg_b])
        op("act", lambda e: e.activation(out=r32[:], in_=pr0[:, 0:256], func=AF.Copy), [pr0_b], [r32_b])
        op("dve", lambda e: e.tensor_copy(out=k32[:], in_=pr0[:, 256:512]), [pr0_b], [k32_b])
        op("act", lambda e: e.activation(out=v32[:], in_=pr1[:, 0:256], func=AF.Copy), [pr1_b], [v32_b])
        op("act", lambda e: e.activation(out=lt32[:, 0:64], in_=pr1[:, 256:320], func=AF.Exp, scale=-2.0), [pr1_b], [lt32_b])
        op("dve", lambda e: e.tensor_scalar(out=lt32[:, 0:64], in0=lt32[:, 0:64], scalar1=1.0, scalar2=None, op0=ALU.add),
           [lt32_b], [lt32_b])
        op("dve", lambda e: e.reciprocal(out=lt32[:, 0:64], in_=lt32[:, 0:64]), [lt32_b], [lt32_b])
        op("dve", lambda e: e.tensor_scalar(out=lt32[:, 0:64], in0=lt32[:, 0:64], scalar1=2.0, scalar2=-1.0, op0=ALU.mult,
                                            op1=ALU.add), [lt32_b], [lt32_b])
        op("dve", lambda e: e.tensor_copy(out=lt32[:, 64:128], in_=pr1[:, 320:384]), [pr1_b, lt32_b], [lt32_b])
        ptl, ptl_b = bank[3]
        for q_ in range(2):
            op("pe", lambda e: e.transpose(out=ptl[0:64, q_ * 128:(q_ + 1) * 128], in_=lt32[:, q_ * 64:(q_ + 1) * 64],
                                           identity=ident_f[:]), [lt32_b, ident_f_b], [ptl_b])
        op("dve", lambda e: e.tensor_copy(out=lT[:].rearrange("p a t -> p (a t)"), in_=ptl[0:64, 0:256]), [ptl_b], [lT_b])
        pl, pl_b = bank[4]
        op("pe", lambda e: e.matmul(pl[:, 0:256], lT[:, 0, :], wup[:], start=True, stop=True), [lT_b, wup_b], [pl_b])
        op("pe", lambda e: e.matmul(pl[:, 256:512], lT[:, 1, :], aup[:], start=True, stop=True), [lT_b, aup_b], [pl_b])
        op("dve", lambda e: e.tensor_tensor(out=t1[:], in0=pl[:, 0:256], in1=W0, op=ALU.add), [pl_b, pb1_b], [t1_b])
        op("act", lambda e: e.activation(out=t1[:], in_=t1[:], func=AF.Exp, scale=-1.0), [t1_b], [t1_b])
        op("act", lambda e: e.activation(out=t1[:], in_=t1[:], func=AF.Ln, bias=1.0), [t1_b], [t1_b])
        op("act", lambda e: e.activation(out=lwn[:], in_=t1[:], func=AF.Exp, scale=-1.0, bias=-0.5), [t1_b], [lwn_b])
        op("dve", lambda e: e.tensor_tensor(out=t2[:], in0=pl[:, 256:512], in1=A0, op=ALU.add), [pl_b, pb1_b], [t2_b])
        op("act", lambda e: e.activation(out=t2[:], in_=t2[:], func=AF.Exp, scale=-1.0), [t2_b], [t2_b])
        op("dve", lambda e: e.tensor_scalar(out=t2[:], in0=t2[:], scalar1=1.0, scalar2=None, op0=ALU.add), [t2_b], [t2_b])
        op("dve", lambda e: e.reciprocal(out=a_[:], in_=t2[:]), [t2_b], [a_b])
        op("dve", lambda e: e.tensor_tensor(out=kk[:], in0=k32[:], in1=KK_, op=ALU.mult), [k32_b, pb1_b], [kk_b])
        op("dve", lambda e: e.tensor_tensor(out=t1[:], in0=kk[:], in1=kk[:], op=ALU.mult), [kk_b], [t1_b])
        op("dve", lambda e: e.tensor_reduce(out=s4[:, 0:4], in_=v3(t1[:]), axis=AX.X, op=ALU.add), [t1_b], [s4_b])
        op("dve", lambda e: e.tensor_scalar(out=s4[:, 0:4], in0=s4[:, 0:4], scalar1=1e-24, scalar2=None, op0=ALU.max),
           [s4_b], [s4_b])
        op("act", lambda e: e.activation(out=s4[:, 0:4], in_=s4[:, 0:4], func=AF.Ln), [s4_b], [s4_b])
        op("act", lambda e: e.activation(out=s4[:, 0:4], in_=s4[:, 0:4], func=AF.Exp, scale=-0.5), [s4_b], [s4_b])
        op("dve", lambda e: e.tensor_tensor(out=v3(kk[:]), in0=v3(kk[:]), in1=bc(s4[:, 0:4]), op=ALU.mult),
           [kk_b, s4_b], [kk_b])
        op("dve", lambda e: e.scalar_tensor_tensor(out=t1[:], in0=a_[:], scalar=-1.0, in1=KA_, op0=ALU.add, op1=ALU.mult),
           [a_b, pb1_b], [t1_b])
        op("dve", lambda e: e.scalar_tensor_tensor(out=km[:], in0=t1[:], scalar=1.0, in1=k32[:], op0=ALU.add, op1=ALU.mult),
           [t1_b, k32_b], [km_b])
        op("dve", lambda e: e.tensor_tensor(out=ba[:], in0=kk[:], in1=a_[:], op=ALU.mult), [kk_b, a_b], [ba_b])
        op("dve", lambda e: e.tensor_tensor(out=t1[:], in0=r32[:], in1=km[:], op=ALU.mult), [r32_b, km_b], [t1_b])
        op("dve", lambda e: e.tensor_tensor(out=t1[:], in0=t1[:], in1=RK_, op=ALU.mult), [t1_b, pb1_b], [t1_b])
        op("dve", lambda e: e.tensor_reduce(out=s4[:, 4:8], in_=v3(t1[:]), axis=AX.X, op=ALU.add), [t1_b], [s4_b])
        op("dve", lambda e: e.tensor_tensor(out=v3(bon[:]), in0=v3(v32[:]), in1=bc(s4[:, 4:8]), op=ALU.mult),
           [v32_b, s4_b], [bon_b])
        op("act", lambda e: e.activation(out=t2[:], in_=pg[:, 0:256], func=AF.Exp, scale=-1.0), [pg_b], [t2_b])
        op("dve", lambda e: e.tensor_scalar(out=t2[:], in0=t2[:], scalar1=1.0, scalar2=None, op0=ALU.add), [t2_b], [t2_b])
        op("dve", lambda e: e.reciprocal(out=t2[:], in_=t2[:]), [t2_b], [t2_b])
        op("dve", lambda e: e.tensor_tensor(out=sg[:], in0=t2[:], in1=pg[:, 0:256], op=ALU.mult), [t2_b, pg_b], [sg_b])
        pc0, pc0_b = bank[5]
        pc1, pc1_b = bank[6]
        op("pe", lambda e: e.matmul(pc0[:, 0:256], tri_i[:], lwn[:], start=True, stop=True), [tri_i_b, lwn_b], [pc0_b])
        op("pe", lambda e: e.matmul(pc0[:, 256:512], tri_s[:], lwn[:], start=True, stop=True), [tri_s_b, lwn_b], [pc0_b])
        op("pe", lambda e: e.matmul(pc1[:, 0:256], tri_r[:], lwn[:], start=True, stop=True), [tri_r_b, lwn_b], [pc1_b])
        for cch in range(2):
            for hh in range(NH):
                col = 256 + cch * 4 + hh
                op("pe", lambda e: e.matmul(pc1[0:64, col:col + 1], lwn[cch * 64:(cch + 1) * 64, hh * 64:(hh + 1) * 64],
                                            ones_c[cch * 64:(cch + 1) * 64, :], start=True, stop=True),
                   [lwn_b, ones_c_b], [pc1_b])
        op("act", lambda e: e.activation(out=ecum[:], in_=pc0[:, 0:256], func=AF.Exp, scale=-1.0), [pc0_b], [ecum_b])
        op("act", lambda e: e.activation(out=encum[:], in_=pc0[:, 0:256], func=AF.Exp), [pc0_b], [encum_b])
        op("act", lambda e: e.activation(out=eprev[:], in_=pc0[:, 256:512], func=AF.Exp, scale=-1.0), [pc0_b], [eprev_b])
        op("act", lambda e: e.activation(out=erc[:], in_=pc1[:, 0:256], func=AF.Exp, scale=-1.0), [pc1_b], [erc_b])
        op("act", lambda e: e.activation(out=wc[:], in_=pc1[0:64, 256:264], func=AF.Exp, scale=-1.0), [pc1_b], [wc_b])
        tl, tl_b = tl4
        op("dve", lambda e: e.scalar_tensor_tensor(out=tl[:, 0, :], in0=kk[:], scalar=-1.0, in1=eprev[:],
                                                   op0=ALU.mult, op1=ALU.mult), [kk_b, eprev_b], [tl_b])
        op("dve", lambda e: e.tensor_tensor(out=tl[:, 1, :], in0=r32[:], in1=ecum[:], op=ALU.mult), [r32_b, ecum_b], [tl_b])
        op("dve", lambda e: e.tensor_tensor(out=tl[:, 2, :], in0=ba[:], in1=encum[:], op=ALU.mult), [ba_b, encum_b], [tl_b])
        op("dve", lambda e: e.tensor_tensor(out=tl[:, 3, :], in0=km[:], in1=encum[:], op=ALU.mult), [km_b, encum_b], [tl_b])
        op("pool", lambda e: e.tensor_tensor(out=bh[:], in0=ba[:], in1=erc[:], op=ALU.mult), [ba_b, erc_b], [bh_b])
        op("pool", lambda e: e.tensor_tensor(out=kh[:], in0=km[:], in1=erc[:], op=ALU.mult), [km_b, erc_b], [kh_b])
        op("pool", lambda e: e.tensor_copy(out=vb[:], in_=v32[:]), [v32_b], [vb_b])
        TTp = bankT[0:64, :].rearrange("p (a t) -> p a t", a=8)
        for rnd in range(2):
            for tq in range(2):
                ty = rnd * 2 + tq
                for hh in range(NH):
                    op("pe", lambda e: e.transpose(out=TTp[:, tq * 4 + hh, :], in_=tl[:, ty, hh * 64:(hh + 1) * 64],
                                                   identity=ident_bf[:]), [tl_b, ident_bf_b], [bankT_b])
            op("act", lambda e: e.activation(out=TT[:, rnd * 8:(rnd + 1) * 8, :], in_=TTp, func=AF.Copy), [bankT_b], [TT_b])

        def kT(ty, hh):
            return TT[:, ty * 4 + hh, :]

        pn, pn_b = bank[2]
        for hh in range(NH):
            pa, pa_b = bank[hh % 2]
            rhs2 = TT[:, hh:hh + 5:4, :]
            op("pe", lambda e: e.matmul(pa[:, 0:256], kT(2, hh), rhs2, start=True, stop=True), [TT_b], [pa_b])
            op("pe", lambda e: e.matmul(pa[:, 256:512], kT(3, hh), rhs2, start=True, stop=True), [TT_b], [pa_b])
            op("pe", lambda e: e.matmul(pn[:, hh * 128:(hh + 1) * 128], kT(0, hh), kT(2, hh), start=True, stop=True),
               [TT_b], [pn_b])
            A, A_b = A_sb[hh]
            op("dve", lambda e: e.tensor_tensor(out=A[:], in0=pa[:], in1=mask4[:], op=ALU.mult), [pa_b, mask4_b], [A_b])
        Ni, Ni_b = Ni2[0]
        op("dve", lambda e: e.tensor_tensor(out=Ni[:], in0=pn[:], in1=maskr4[:], op=ALU.mult), [pn_b, maskr4_b], [Ni_b])
        px, px_b = bank[4]
        for hh in range(NH):
            A, A_b = A_sb[hh]
            op("pe", lambda e: e.matmul(px[:, hh * 128:hh * 128 + 64], ident_bf[:], tl[:, 0, hh * 64:(hh + 1) * 64],
                                        start=(hh == 0), stop=False, skip_group_check=True), [tl_b, ident_bf_b], [px_b])
            op("pe", lambda e: e.matmul(px[:, hh * 128 + 64:(hh + 1) * 128], A[:, 256:384], vb[:, hh * 64:(hh + 1) * 64],
                                        start=False, stop=False, skip_group_check=True), [A_b, vb_b], [px_b])
        op("dve", lambda e: e.tensor_copy(out=Xb[:], in_=px[:]), [px_b], [Xb_b])
        for s_ in range(6):
            cur = s_ % 2
            Ni, Ni_b = Ni2[cur]
            NiT, NiT_b = NiT2[cur]
            for hh in range(NH):
                if s_ == 0:
                    lt_, ltb = A_sb[hh][0][:, 0:128], A_sb[hh][1]
                else:
                    lt_, ltb = NiT[:, hh * 128:(hh + 1) * 128], NiT_b
                op("pe", lambda e: e.matmul(px[:, hh * 128:(hh + 1) * 128], lt_, Xb[:, hh * 128:(hh + 1) * 128],
                                            start=False, stop=(s_ == 5), skip_group_check=True), [ltb, Xb_b], [px_b])
            if s_ < 5:
                ps1, ps1_b = bank[5]
                ps2, ps2_b = bank[6]
                Nn_, Nn_b = Ni2[1 - cur]
                NnT, NnT_b = NiT2[1 - cur]
                for hh in range(NH):
                    if s_ == 0:
                        lt_, ltb = A_sb[hh][0][:, 0:128], A_sb[hh][1]
                    else:
                        lt_, ltb = NiT[:, hh * 128:(hh + 1) * 128], NiT_b
                    n_ = Ni[:, hh * 128:(hh + 1) * 128]
                    op("pe", lambda e: e.matmul(ps2[:, hh * 128:(hh + 1) * 128], n_, lt_, start=True, stop=True),
                       [ltb, Ni_b], [ps2_b])
                    op("pe", lambda e: e.matmul(ps1[:, hh * 128:(hh + 1) * 128], lt_, n_, start=True, stop=True),
                       [ltb, Ni_b], [ps1_b])
            op("dve", lambda e: e.tensor_copy(out=Xb[:], in_=px[:]), [px_b], [Xb_b])
            if s_ < 5:
                op("dve", lambda e: e.tensor_copy(out=NnT[:], in_=ps2[:]), [ps2_b], [NnT_b])
                op("act", lambda e: e.activation(out=Nn_[:], in_=ps1[:], func=AF.Copy), [ps1_b], [Nn_b])
        Xb3 = Xb[:].rearrange("p (h c) -> p h c", h=NH)
        pgt, pgt_b = bank[0]
        tlr = tl[:, 1, :]
        for hh in range(NH):
            A, A_b = A_sb[hh]
            op("pe", lambda e: e.matmul(pgt[0:64, hh * 128:(hh + 1) * 128], Xb3[:, hh, 0:64], A[:, 128:256],
                                        start=True, stop=False), [Xb_b, A_b], [pgt_b])
            op("pe", lambda e: e.matmul(pgt[0:64, hh * 128:(hh + 1) * 128], tlr[:, hh * 64:(hh + 1) * 64], ident_bf[:],
                                        start=False, stop=True), [tl_b, ident_bf_b], [pgt_b])
        pgt5 = pgt[0:64, :].rearrange("p (h c t) -> p h c t", h=NH, c=2)
        GT5 = GTs[:].rearrange("p (a h c t) -> p a h c t", a=2, h=NH, c=2)
        for cch in range(2):
            op("act", lambda e: e.activation(out=GT5[:, cch, :, cch, :], in_=pgt5[:, :, cch, :], func=AF.Copy), [pgt_b], [GTs_b])
        pm, pm_b = bank[1]
        pn2, pn2_b = bank[5]
        for cch in range(2):
            op("pool", lambda e: e.tensor_scalar(out=bhm[:, cch, :], in0=bh[:], scalar1=cmask[:, cch:cch + 1], scalar2=None,
                                                 op0=ALU.mult), [bh_b, cmask_b], [bhm_b])
            op("pool", lambda e: e.tensor_scalar(out=khm[:, cch, :], in0=kh[:], scalar1=cmask[:, cch:cch + 1], scalar2=None,
                                                 op0=ALU.mult), [kh_b, cmask_b], [khm_b])
        for cch in range(2):
            for hh in range(NH):
                cs = slice((cch * 4 + hh) * 64, (cch * 4 + hh + 1) * 64)
                hs = slice(hh * 64, (hh + 1) * 64)
                op("pe", lambda e: e.matmul(pm[0:64, cs], Xb3[:, hh, 0:64], bhm[:, cch, hs], start=True, stop=True),
                   [Xb_b, bhm_b], [pm_b])
                op("pe", lambda e: e.matmul(pn2[0:64, cs], bhm[:, cch, hs], Xb3[:, hh, 64:128], start=True, stop=False),
                   [Xb_b, bhm_b], [pn2_b])
                op("pe", lambda e: e.matmul(pn2[0:64, cs], khm[:, cch, hs], vb[:, hs], start=False, stop=True),
                   [khm_b, vb_b], [pn2_b])
        op("act", lambda e: e.activation(out=M0T[:], in_=pm[0:64, :], func=AF.Copy), [pm_b], [M0T_b])
        op("dve", lambda e: e.tensor_copy(out=Nst[:], in_=pn2[0:64, :]), [pn2_b], [Nst_b])
        py, py_b = bank[3]
        for hh in range(NH):
            A, A_b = A_sb[hh]
            hs = slice(hh * 64, (hh + 1) * 64)
            op("pe", lambda e: e.matmul(py[:, hs], A[:, 128:256], Xb3[:, hh, 64:128], start=(hh == 0), stop=False,
                                        skip_group_check=True), [A_b, Xb_b], [py_b])
            op("pe", lambda e: e.matmul(py[:, hs], A[:, 384:512], vb[:, hs], start=False, stop=False, skip_group_check=True),
               [A_b, vb_b], [py_b])
        pst, pst_b = bank[6]
        for cch in range(2):
            ST, ST_b = ST2[st_cur]
            STn, STn_b = ST2[1 - st_cur]
            for hh in range(NH):
                hs = slice(hh * 64, (hh + 1) * 64)
                op("pe", lambda e: e.matmul(py[:, hs], GTs[:, cch * 512 + hh * 128: cch * 512 + (hh + 1) * 128],
                                            ST[:, hs], start=False, stop=True, skip_group_check=True), [GTs_b, ST_b], [py_b])
            for hh in range(NH):
                hs = slice(hh * 64, (hh + 1) * 64)
                cs = slice((cch * 4 + hh) * 64, (cch * 4 + hh + 1) * 64)
                op("pe", lambda e: e.matmul(pst[0:64, hs], M0T[:, cs], ST[:, hs], start=True, stop=True), [M0T_b, ST_b], [pst_b])
            op("dve", lambda e: e.tensor_tensor(out=v3(stt[:]), in0=v3(ST[:]),
                                                in1=wc[:, cch * 4:(cch + 1) * 4].unsqueeze(2).to_broadcast([64, NH, HD]),
                                                op=ALU.mult), [ST_b, wc_b], [stt_b])
            op("dve", lambda e: e.tensor_tensor(out=stt[:], in0=stt[:], in1=Nst[:, cch * 256:(cch + 1) * 256], op=ALU.add),
               [stt_b, Nst_b], [stt_b])
            op("dve", lambda e: e.tensor_tensor(out=STn[:], in0=stt[:], in1=pst[0:64, 0:256], op=ALU.add),
               [stt_b, pst_b], [STn_b])
            st_cur = 1 - st_cur
        op("act", lambda e: e.activation(out=y32[:], in_=py[:, 0:256], func=AF.Copy), [py_b], [y32_b])
        op("dve", lambda e: e.tensor_reduce(out=s4[:, 8:12], in_=v3(y32[:]), axis=AX.X, op=ALU.add), [y32_b], [s4_b])
        op("dve", lambda e: e.tensor_scalar(out=s4[:, 8:12], in0=s4[:, 8:12], scalar1=1.0 / HD, scalar2=None, op0=ALU.mult),
           [s4_b], [s4_b])
        op("dve", lambda e: e.tensor_tensor(out=v3(y32[:]), in0=v3(y32[:]), in1=bc(s4[:, 8:12]), op=ALU.subtract),
           [y32_b, s4_b], [y32_b])
        op("dve", lambda e: e.tensor_tensor(out=t1[:], in0=y32[:], in1=y32[:], op=ALU.mult), [y32_b], [t1_b])
        op("dve", lambda e: e.tensor_reduce(out=s4[:, 12:16], in_=v3(t1[:]), axis=AX.X, op=ALU.add), [t1_b], [s4_b])
        op("act", lambda e: e.activation(out=s4[:, 12:16], in_=s4[:, 12:16], func=AF.Ln, scale=1.0 / HD, bias=GN_EPS),
           [s4_b], [s4_b])
        op("act", lambda e: e.activation(out=s4[:, 12:16], in_=s4[:, 12:16], func=AF.Exp, scale=-0.5), [s4_b], [s4_b])
        op("dve", lambda e: e.tensor_tensor(out=v3(y32[:]), in0=v3(y32[:]), in1=bc(s4[:, 12:16]), op=ALU.mult),
           [y32_b, s4_b], [y32_b])
        op("dve", lambda e: e.tensor_tensor(out=y32[:], in0=y32[:], in1=GNG, op=ALU.mult), [y32_b, pb1_b], [y32_b])
        op("dve", lambda e: e.tensor_tensor(out=y32[:], in0=y32[:], in1=GNB, op=ALU.add), [y32_b, pb1_b], [y32_b])
        op("dve", lambda e: e.tensor_tensor(out=y32[:], in0=y32[:], in1=bon[:], op=ALU.add), [y32_b, bon_b], [y32_b])
        op("dve", lambda e: e.tensor_tensor(out=yg[:], in0=y32[:], in1=sg[:], op=ALU.mult), [y32_b, sg_b], [yg_b])
        ygp = bankT[:, 0:256].rearrange("p (a t) -> p a t", a=2)
        for q_ in range(2):
            op("pe", lambda e: e.transpose(out=ygp[:, q_, :], in_=yg[:, q_ * 128:(q_ + 1) * 128], identity=ident_bf[:]),
               [yg_b, ident_bf_b], [bankT_b])
        ti = i % 4
        op("act", lambda e: e.activation(out=ygT[:, :, ti * 128:(ti + 1) * 128], in_=ygp, func=AF.Copy), [bankT_b], [ygT_b])
        if ti == 3:
            t0 = (i // 4) * 512
            for hd in range(NH):
                sch.dma(ybuf[hd][:, t0:t0 + 512], ygT[(hd % 2) * 64:(hd % 2 + 1) * 64, hd // 2, :], [ygT_b], [ybuf_b[hd]])

    sch.barrier()
    P1.close()

    def gather(c):
        sch.collective(lambda e: e.collective_compute("AllGather", ALU.bypass, replica_groups=[[0, 1], [2, 3], [4, 5], [6, 7]],
                                                      ins=[ybuf[c][:, :]], outs=[yall[c][:, :]]), [ybuf_b[c]], [yall_b[c]])

    if stop_after > 2:
        for c in range(4):
            gather(c)
    if stop_after <= 1:
        sch.finish()
        return nc

    P2 = Scope()

    def sb2(name, shape, dt=F32):
        return P2.sb(name, shape, dt)

    Wf, Wf_b = sb2("Wf", [128, KC, FXC], BF16)
    Wg2, Wg2_b = sb2("Wg2", [128, KC, 256], BF16)
    pb2, pb2_b = sb2("pb2", [128, 132])
    sch.dma(pb2[:], pb2_d[:, :], [], [pb2_b])
    load_weight(Wf, Wf_b, w_fox, FXC)
    load_weight(Wg2, Wg2_b, w_gfx, 256)
    QG = pb2[:, 0:64]
    KG = pb2[:, 64:128]
    BFB = pb2[:, 128:132]
    qgs, qgs_b = sb2("qgs", [128, 64])
    op("dve", lambda e: e.tensor_scalar(out=qgs[:], in0=QG, scalar1=HD ** -0.5, scalar2=None, op0=ALU.mult), [pb2_b], [qgs_b])
    tri_f, tri_f_b = sb2("tri_f", [128, 128])
    sel127, sel127_b = sb2("sel127", [128, 128])
    ones_f, ones_f_b = sb2("ones_f", [128, 64])
    op("pool", lambda e: e.memset(tri_f[:], 1.0), [], [tri_f_b])
    op("pool", lambda e: e.affine_select(out=tri_f[:], in_=tri_f[:], pattern=[[1, 128]], compare_op=ALU.is_ge, fill=0.0,
                                         base=0, channel_multiplier=-1), [tri_f_b], [tri_f_b])
    op("pool", lambda e: e.memset(sel127[:], 1.0), [], [sel127_b])
    op("pool", lambda e: e.affine_select(out=sel127[:], in_=sel127[:], pattern=[[0, 128]], compare_op=ALU.is_ge, fill=0.0,
                                         base=-127, channel_multiplier=1), [sel127_b], [sel127_b])
    op("pool", lambda e: e.memset(ones_f[:], 1.0), [], [ones_f_b])

    KT, KT_b = sb2("KT", [70, NH, S], BF16)
    vaug, vaug_b = sb2("vaug", [128, NT, NH, 65], BF16)
    op("pool", lambda e: e.memset(vaug[:], 1.0), [], [vaug_b])
    bank = [P2.ps("bank2_%d" % i, [128, 512], F32) for i in range(7)]
    bankT, bankT_b = P2.ps("bankT2", [128, 1024], BF16)
    xt2 = [sb2("xtb%d" % i, [128, 1024]) for i in range(2)]
    h, h_b = sb2("hb", [128, 1024], BF16)
    junk, junk_b = sb2("junkb", [128, 1024], BF16)
    ss, ss_b = sb2("ssb", [128, 4])
    hTb, hTb_b = sb2("hTb", [128, KC, 512], BF16)
    qk32, qk32_b = sb2("qk32", [128, 512])
    sq, sq_b = sb2("sq", [128, 512])
    s8, s8_b = sb2("s8", [128, 8])
    cn2 = [sb2("cn%d" % i, [128, 4]) for i in range(2)]
    z4, z4_b = sb2("z4", [128, 4])
    spl, spl_b = sb2("spl", [128, 3, 4], BF16)
    rr, rr_b = sb2("rr", [128, 4])
    qaug2 = [sb2("qaug%d" % i, [128, NH, 70], BF16) for i in range(2)]
    kaug2 = [sb2("kaug%d" % i, [128, NH, 70], BF16) for i in range(2)]
    QT, QT_b = sb2("QT", [70, NH, 512], BF16)
    sgT, sgT_b = sb2("sgT", [64, NH, 512], BF16)
    eg, eg_b = sb2("eg", [64, 512])
    pT3 = [sb2("pT%d" % i, [128, 512], BF16) for i in range(3)]
    osb, osb_b = sb2("osb", [65, 512])
    rec, rec_b = sb2("rec", [65, 512])
    yt32, yt32_b = sb2("yt32", [64, 512])
    yfT2 = [sb2("yfT%d" % i, [64, 512], BF16) for i in range(2)]
    for i_ in range(2):
        op("pool", lambda e: e.memset(qaug2[i_][0][:], 1.0), [], [qaug2[i_][1]])
        op("pool", lambda e: e.memset(kaug2[i_][0][:], 1.0), [], [kaug2[i_][1]])
    op("dve", lambda e: e.memset(cn2[1][0][:], 0.0), [], [cn2[1][1]])

    pt_rot = 0
    sc_rot = 0
    yf_rot = 0
    for qb in range(NB):
        for ti in range(4):
            i = qb * 4 + ti
            xt, xt_b = xt2[i % 2]
            if i == 0:
                sch.dma(xt[:], x[0:128, :], [], [xt_b])
            if i + 1 < NT:
                sch.dma(xt2[(i + 1) % 2][0][:], x[(i + 1) * 128:(i + 2) * 128, :], [], [xt2[(i + 1) % 2][1]])
            rmsnorm_to_hT(xt, xt_b, h, h_b, junk, junk_b, ss, ss_b, bankT, bankT_b,
                          lambda: hTb[:, :, ti * 128:(ti + 1) * 128], hTb_b)
            pf0, pf0_b = bank[0]
            pf1, pf1_b = bank[1]
            for (pt, ptb, n0, n1) in ((pf0, pf0_b, 0, 512), (pf1, pf1_b, 512, FXC)):
                for c in range(KC):
                    op("pe", lambda e: e.matmul(pt[:, 0:n1 - n0], hTb[:, c, ti * 128:(ti + 1) * 128], Wf[:, c, n0:n1],
                                                start=(c == 0), stop=(c == KC - 1)), [hTb_b, Wf_b], [ptb])
            qa, qa_b = qaug2[i % 2]
            ka, ka_b = kaug2[i % 2]
            op("act", lambda e: e.activation(out=qk32[:], in_=pf0[:], func=AF.Copy), [pf0_b], [qk32_b])
            op("dve", lambda e: e.tensor_tensor(out=sq[:], in0=qk32[:], in1=qk32[:], op=ALU.mult), [qk32_b], [sq_b])
            op("dve", lambda e: e.tensor_reduce(out=s8[:], in_=v3(sq[:], 8), axis=AX.X, op=ALU.add), [sq_b], [s8_b])
            op("act", lambda e: e.activation(out=s8[:], in_=s8[:], func=AF.Ln, scale=1.0 / HD, bias=NORM_EPS), [s8_b], [s8_b])
            op("act", lambda e: e.activation(out=s8[:], in_=s8[:], func=AF.Exp, scale=-0.5), [s8_b], [s8_b])
            op("dve", lambda e: e.tensor_tensor(out=v3(qk32[:], 8), in0=v3(qk32[:], 8), in1=bc(s8[:], 8), op=ALU.mult),
               [qk32_b, s8_b], [qk32_b])
            op("dve", lambda e: e.tensor_tensor(out=qa[:, :, 0:64], in0=v3(qk32[:, 0:256]),
                                                in1=qgs[:].unsqueeze(1).to_broadcast([128, NH, HD]), op=ALU.mult),
               [qk32_b, qgs_b], [qa_b])
            op("dve", lambda e: e.tensor_tensor(out=ka[:, :, 0:64], in0=v3(qk32[:, 256:512]),
                                                in1=KG.unsqueeze(1).to_broadcast([128, NH, HD]), op=ALU.mult),
               [qk32_b, pb2_b], [ka_b])
            op("act", lambda e: e.activation(out=vaug[:, i, :, 0:64], in_=v3(pf1[:, 0:256]), func=AF.Copy), [pf1_b], [vaug_b])
            op("dve", lambda e: e.tensor_tensor(out=z4[:], in0=pf1[:, 256:260], in1=BFB, op=ALU.add), [pf1_b, pb2_b], [z4_b])
            op("act", lambda e: e.activation(out=z4[:], in_=z4[:], func=AF.Exp, scale=-1.0), [z4_b], [z4_b])
            op("act", lambda e: e.activation(out=z4[:], in_=z4[:], func=AF.Ln, bias=1.0), [z4_b], [z4_b])
            cn, cn_b = cn2[i % 2]
            cnp, cnp_b = cn2[(i + 1) % 2]
            pcs, pcs_b = bank[2]
            op("pe", lambda e: e.matmul(pcs[:, 0:4], tri_f[:], z4[:], start=True, stop=False), [tri_f_b, z4_b], [pcs_b])
            op("pe", lambda e: e.matmul(pcs[:, 0:4], sel127[:], cnp[:], start=False, stop=True), [sel127_b, cnp_b], [pcs_b])
            op("dve", lambda e: e.tensor_copy(out=cn[:], in_=pcs[:, 0:4]), [pcs_b], [cn_b])
            op("dve", lambda e: e.tensor_copy(out=spl[:, 0, :], in_=cn[:]), [cn_b], [spl_b])
            op("dve", lambda e: e.tensor_tensor(out=rr[:], in0=cn[:], in1=spl[:, 0, :], op=ALU.subtract), [cn_b, spl_b], [rr_b])
            op("dve", lambda e: e.tensor_copy(out=spl[:, 1, :], in_=rr[:]), [rr_b], [spl_b])
            op("dve", lambda e: e.tensor_tensor(out=rr[:], in0=rr[:], in1=spl[:, 1, :], op=ALU.subtract), [rr_b, spl_b], [rr_b])
            op("dve", lambda e: e.tensor_copy(out=spl[:, 2, :], in_=rr[:]), [rr_b], [spl_b])
            op("dve", lambda e: e.tensor_copy(out=ka[:, :, 67:70], in_=spl[:].rearrange("p s h -> p h s")), [spl_b], [ka_b])
            op("dve", lambda e: e.tensor_scalar(out=qa[:, :, 64:67], in0=spl[:].rearrange("p s h -> p h s"), scalar1=-1.0,
                                                scalar2=None, op0=ALU.mult), [spl_b], [qa_b])
            tqp = bankT[:].rearrange("p (a t) -> p a t", a=8)
            for hh in range(NH):
                op("pe", lambda e: e.transpose(out=tqp[0:70, hh, :], in_=qa[:, hh, :], identity=ident_bf[:]),
                   [qa_b, ident_bf_b], [bankT_b])
                op("pe", lambda e: e.transpose(out=tqp[0:70, 4 + hh, :], in_=ka[:, hh, :], identity=ident_bf[:]),
                   [ka_b, ident_bf_b], [bankT_b])
            op("dve", lambda e: e.tensor_copy(out=QT[:, :, ti * 128:(ti + 1) * 128], in_=tqp[0:70, 0:4, :]), [bankT_b], [QT_b])
            op("act", lambda e: e.activation(out=KT[:, :, i * 128:(i + 1) * 128], in_=tqp[0:70, 4:8, :], func=AF.Copy),
               [bankT_b], [KT_b])
        for hh in range(NH):
            pgt, pgt_b = bank[2]
            for c in range(KC):
                op("pe", lambda e: e.matmul(pgt[0:64, :], Wg2[:, c, hh * 64:(hh + 1) * 64], hTb[:, c, :], start=(c == 0),
                                            stop=(c == KC - 1)), [Wg2_b, hTb_b], [pgt_b])
            op("act", lambda e: e.activation(out=eg[:], in_=pgt[0:64, :], func=AF.Exp, scale=-1.0), [pgt_b], [eg_b])
            op("dve", lambda e: e.tensor_scalar(out=eg[:], in0=eg[:], scalar1=1.0, scalar2=None, op0=ALU.add), [eg_b], [eg_b])
            op("dve", lambda e: e.reciprocal(out=eg[:], in_=eg[:]), [eg_b], [eg_b])
            op("dve", lambda e: e.tensor_tensor(out=sgT[:, hh, :], in0=eg[:], in1=pgt[0:64, :], op=ALU.mult), [eg_b, pgt_b], [sgT_b])
        nkt = 4 * (qb + 1)
        for hh in range(NH):
            po, po_b = bank[3]
            for jt in range(nkt):
                jl = jt - 4 * qb
                col0 = 128 * jl if jl > 0 else 0
                sc, sc_b = bank[4 + sc_rot]
                sc_rot = (sc_rot + 1) % 3
                pT, pT_b = pT3[pt_rot]
                pt_rot = (pt_rot + 1) % 3
                op("pe", lambda e: e.matmul(sc[:, col0:512], KT[:, hh, jt * 128:(jt + 1) * 128], QT[:, hh, col0:512],
                                            start=True, stop=True), [KT_b, QT_b], [sc_b])
                op("act", lambda e: e.activation(out=pT[:, col0:512], in_=sc[:, col0:512], func=AF.Exp), [sc_b], [pT_b])
                if jl >= 0:
                    op("pool", lambda e: e.affine_select(out=pT[:, col0:col0 + 128], in_=pT[:, col0:col0 + 128],
                                                         pattern=[[1, 128]], compare_op=ALU.is_ge, fill=0.0, base=0,
                                                         channel_multiplier=-1), [pT_b], [pT_b])
                op("pe", lambda e: e.matmul(po[0:65, col0:512], vaug[:, jt, hh, :], pT[:, col0:512], start=(jt == 0),
                                            stop=(jt == nkt - 1)), [vaug_b, pT_b], [po_b])
            op("act", lambda e: e.activation(out=osb[:], in_=po[0:65, :], func=AF.Copy), [po_b], [osb_b])
            op("dve", lambda e: e.reciprocal(out=rec[64:65, :], in_=osb[64:65, :]), [osb_b], [rec_b])
            pbc, pbc_b = bank[2]
            op("pe", lambda e: e.matmul(pbc[0:64, :], ones_f[64:65, :], rec[64:65, :], start=True, stop=True),
               [ones_f_b, rec_b], [pbc_b])
            op("dve", lambda e: e.tensor_tensor(out=yt32[:], in0=osb[0:64, :], in1=pbc[0:64, :], op=ALU.mult),
               [osb_b, pbc_b], [yt32_b])
            yfT, yfT_b = yfT2[yf_rot]
            yf_rot = 1 - yf_rot
            op("dve", lambda e: e.tensor_tensor(out=yfT[:], in0=yt32[:], in1=sgT[:, hh, :], op=ALU.mult),
               [yt32_b, sgT_b], [yfT_b])
            t0 = qb * 512
            sch.dma(ybuf[4 + hh][:, t0:t0 + 512], yfT[:], [yfT_b], [ybuf_b[4 + hh]])

    sch.barrier()
    P2.close()
    if stop_after <= 2:
        sch.finish()
        return nc

    for c in range(4, 8):
        gather(c)
    P3 = Scope()

    def sb3(name, shape, dt=F32):
        return P3.sb(name, shape, dt)

    Wo, Wo_b = sb3("Wo", [128, KC, D_MODEL], BF16)
    onesg, onesg_b = sb3("onesg", [128, KC])
    op("dve", lambda e: e.memset(onesg[:], 1.0), [], [onesg_b])
    gcol, gcol_b = onesg, onesg_b
    load_weight(Wo, Wo_b, w_outp, D_MODEL)
    fg, fg_b = sb3("fg", [128, D_MODEL])
    sch.dma(fg[:], pb3_d[:, :], [], [fg_b])
    bank = [P3.ps("bank3_%d" % i, [128, 512], F32) for i in range(4)]
    yT2 = [sb3("yT%d" % i, [128, KC, 512], BF16) for i in range(2)]
    yA, yA_b = sb3("yA", [128, KC, 512], BF16)
    yB, yB_b = sb3("yB", [128, KC, 512], BF16)
    sel, sel_b = sb3("sel", [128, 2])
    sch.dma(sel[:], sel_d[:, :], [], [sel_b])
    xr2 = [sb3("xr%d" % i, [128, D_MODEL]) for i in range(2)]
    z2 = [sb3("z%d" % i, [128, D_MODEL]) for i in range(2)]
    junk, junk_b = sb3("junk3", [128, D_MODEL], BF16)
    ss3, ss3_b = sb3("ss3", [128, 4])
    def load_y(blk):
        for c in range(KC):
            sch.dma(yA[:, c, :], yall[c][:, blk * 512:(blk + 1) * 512], [yall_b[c]], [yA_b])
            sch.dma(yB[:, c, :], yall[c][:, SH + blk * 512:SH + (blk + 1) * 512], [yall_b[c]], [yB_b])

    NT3 = SH // 128
    sch.dma(xr2[0][0][:], xres[0:128, :], [], [xr2[0][1]])
    load_y(0)
    for blk in range(SH // 512):
        yT, yT_b = yT2[blk % 2]
        op("pool", lambda e: e.tensor_scalar(out=yT[:], in0=yA[:], scalar1=sel[:, 0:1], scalar2=None, op0=ALU.mult),
           [yA_b, sel_b], [yT_b])
        op("dve", lambda e: e.scalar_tensor_tensor(out=yT[:], in0=yB[:], scalar=sel[:, 1:2], in1=yT[:], op0=ALU.mult,
                                                   op1=ALU.add), [yB_b, sel_b, yT_b], [yT_b])
        if blk + 1 < SH // 512:
            load_y(blk + 1)
        for ti in range(4):
            i = blk * 4 + ti
            xr, xr_b = xr2[i % 2]
            z, z_b = z2[i % 2]
            if i + 1 < NT3:
                sch.dma(xr2[(i + 1) % 2][0][:], xres[(i + 1) * 128:(i + 2) * 128, :], [], [xr2[(i + 1) % 2][1]])
            for nn in range(2):
                po, po_b = bank[(i % 2) * 2 + nn]
                for c in range(KC):
                    op("pe", lambda e: e.matmul(po[:], yT[:, c, ti * 128:(ti + 1) * 128], Wo[:, c, nn * 512:(nn + 1) * 512],
                                                start=(c == 0), stop=(c == KC - 1)), [yT_b, Wo_b], [po_b])
                op("dve", lambda e: e.tensor_tensor(out=z[:, nn * 512:(nn + 1) * 512], in0=po[:], in1=xr[:, nn * 512:(nn + 1) * 512],
                                                    op=ALU.add), [po_b, xr_b], [z_b])
            op("act", lambda e: e.activation(out=junk[:], in_=z[:], func=AF.Square, accum_out=ss3[:, 0:1]), [z_b], [junk_b, ss3_b])
            op("act", lambda e: e.activation(out=ss3[:, 1:2], in_=ss3[:, 0:1], func=AF.Ln, scale=1.0 / D_MODEL, bias=NORM_EPS),
               [ss3_b], [ss3_b])
            op("act", lambda e: e.activation(out=ss3[:, 2:3], in_=ss3[:, 1:2], func=AF.Exp, scale=-0.5), [ss3_b], [ss3_b])
            op("dve", lambda e: e.scalar_tensor_tensor(out=z[:], in0=z[:], scalar=ss3[:, 2:3], in1=fg[:], op0=ALU.mult,
                                                       op1=ALU.mult), [z_b, ss3_b, fg_b], [z_b])
            sch.dma(out[i * 128:(i + 1) * 128, :], z[:], [z_b], [out_b])
    sch.finish()
    return nc


def make_core_inputs(inputs, b, g, S):
    f = lambda a: np.ascontiguousarray(np.asarray(a, dtype=np.float32))
    w_in = np.asarray(inputs["w_in"])[0]
    SH = S // 2
    hs = slice(g * 256, (g + 1) * 256)

    def cols(base):
        return np.arange(base + g * 256, base + (g + 1) * 256)

    rw_cols = np.concatenate([cols(0), cols(512), cols(1024), np.arange(1536, 1664)])
    fx0 = 1664
    fox_cols = np.concatenate([cols(fx0), cols(fx0 + 512), cols(fx0 + 1024), np.arange(fx0 + 1536 + 4 * g, fx0 + 1536 + 4 * g + 4)])
    g0 = 1664 + 1544
    mu = np.asarray(inputs["rw_mu"])[0][rw_cols]
    rep = lambda v: np.broadcast_to(np.asarray(v, np.float32).reshape(1, -1), (128, np.asarray(v).size))
    pb1 = np.concatenate([rep(mu), rep(np.asarray(inputs["rw_w0"])[0][hs]), rep(np.asarray(inputs["rw_a0"])[0][hs]),
                          rep(np.asarray(inputs["rw_k_k"])[0][hs]), rep(np.asarray(inputs["rw_k_a"])[0][hs]),
                          rep(np.asarray(inputs["rw_r_k"])[0][4 * g:4 * g + 4].reshape(-1)),
                          rep(np.asarray(inputs["rw_gn_g"])[0][hs]), rep(np.asarray(inputs["rw_gn_b"])[0][hs])], axis=1)
    pb2 = np.concatenate([rep(np.asarray(inputs["fox_q_g"])[0]), rep(np.asarray(inputs["fox_k_g"])[0]),
                          rep(np.asarray(inputs["fox_b_f"])[0][4 * g:4 * g + 4])], axis=1)
    w_out = np.asarray(inputs["w_out"])[0]
    rows = []
    for c in range(8):
        for gg in range(2):
            base = (4 * gg + c) * 64 if c < 4 else 512 + (4 * gg + c - 4) * 64
            rows.append(np.arange(base, base + 64))
    w_outp = w_out[np.concatenate(rows)]
    xb = np.asarray(inputs["x"])[b]
    return {
        "x": f(xb),
        "xres": f(xb[g * SH:(g + 1) * SH]),
        "w_rw": f(w_in[:, rw_cols]),
        "w_fox": f(w_in[:, fox_cols]),
        "w_grw": f(w_in[:, cols(g0)]),
        "w_gfx": f(w_in[:, cols(g0 + 512)]),
        "w_outp": f(w_outp),
        "g_col": f(np.asarray(inputs["norm_g"])[0].reshape(KC, 128).T),
        "pb1": f(pb1),
        "pb2": f(pb2),
        "pb3": f(rep(np.asarray(inputs["final_g"]))),
        "w_up": f(np.asarray(inputs["rw_w_up"])[0][:, hs]),
        "a_up": f(np.asarray(inputs["rw_a_up"])[0][:, hs]),
        "sel": f(np.tile(np.array([[1.0 - g, float(g)]], np.float32), (128, 1))),
    }


def kernel(**inputs):
    x = np.asarray(inputs["x"])
    B, S, D = x.shape
    nc = build_nc(S)
    in_maps = [make_core_inputs(inputs, c // 2, c % 2, S) for c in range(2 * B)]
    res = run_bass_kernel_spmd(nc, in_maps, core_ids=list(range(2 * B)))
    out = np.empty((B, S, D), np.float32)
    SH = S // 2
    for c in range(2 * B):
        out[c // 2, (c % 2) * SH:(c % 2 + 1) * SH] = res.results[c]["out"]
    return out
```

```python
import numpy as np
import concourse.bass as bass
import concourse.mybir as mybir
from concourse.bass_utils import run_bass_kernel_spmd

F32 = mybir.dt.float32
BF16 = mybir.dt.bfloat16
AF = mybir.ActivationFunctionType
ALU = mybir.AluOpType
AX = mybir.AxisListType

D_MODEL = 1024
KC = 8
HD = 64
NH = 4
RWC = 896
FXC = 772
NORM_EPS = 1e-6
GN_EPS = 64e-5
N_DCH = 6


class Buf:
    __slots__ = ("name", "lw", "rd", "psum")

    def __init__(self, name, psum=False):
        self.name = name
        self.lw = None
        self.rd = {}
        self.psum = psum


class Sched:
    def __init__(self, nc):
        self.nc = nc
        self.eng = {"pe": nc.tensor, "act": nc.scalar, "dve": nc.vector, "pool": nc.gpsimd, "sp": nc.sync}
        self.sem = {}
        self.cnt = {}
        for k in self.eng:
            self.sem[k] = nc.semaphore("sem_" + k).__enter__()
            self.cnt[k] = 0
        for c in range(N_DCH):
            k = "d%d" % c
            self.sem[k] = nc.semaphore("sem_" + k).__enter__()
            self.cnt[k] = 0
        self.sem["cc"] = nc.semaphore("sem_cc").__enter__()
        self.cnt["cc"] = 0
        self.waited = {k: {} for k in self.eng}
        self.next_dch = 0
        self.limit = None
        self.total = 0
        self.log = []

    def _lim(self):
        self.total += 1
        if self.log is not None:
            import sys as _s
            fr = _s._getframe(2)
            self.log.append((self.total, fr.f_lineno))
        return self.limit is not None and self.total > self.limit

    def _val(self, k, i):
        return i * 16 if (k[0] == "d" and k != "dve") else i

    def _deps(self, reads, writes):
        deps = {}

        def need(k, i):
            if deps.get(k, 0) < i:
                deps[k] = i

        for b in reads:
            if b.lw is not None:
                need(*b.lw)
            if b.psum:
                for k, i in b.rd.items():
                    need(k, i)
        for b in writes:
            if b.lw is not None:
                need(*b.lw)
            for k, i in b.rd.items():
                need(k, i)
        return deps

    def _wait(self, e, deps):
        w = self.waited[e]
        for k, i in deps.items():
            if w.get(k, 0) < i:
                self.eng[e].wait_ge(self.sem[k], self._val(k, i))
                w[k] = i

    def _commit(self, me, reads, writes):
        k, i = me
        for b in reads:
            b.rd[k] = i
        for b in writes:
            b.lw = me
            b.rd = {}

    def op(self, e, fn, reads=(), writes=()):
        if self._lim():
            return
        deps = self._deps(reads, writes)
        if e == "pe":
            deps.pop("pe", None)
        self._wait(e, deps)
        ins = fn(self.eng[e])
        self.cnt[e] += 1
        ins.then_inc(self.sem[e], 1)
        self._commit((e, self.cnt[e]), reads, writes)

    def dma(self, out, in_, reads=(), writes=(), q="sp"):
        if self._lim():
            return
        c = "d%d" % self.next_dch
        self.next_dch = (self.next_dch + 1) % N_DCH
        deps = self._deps(reads, writes)
        if self.cnt[c] > 0:
            deps[c] = max(deps.get(c, 0), self.cnt[c])
        self._wait(q, deps)
        ins = self.eng[q].dma_start(out=out, in_=in_)
        self.cnt[c] += 1
        ins.then_inc(self.sem[c], 16)
        self._commit((c, self.cnt[c]), reads, writes)

    def collective(self, fn, reads=(), writes=()):
        if self._lim():
            return
        deps = self._deps(reads, writes)
        self._wait("pool", deps)
        ins = fn(self.eng["pool"])
        self.cnt["cc"] += 1
        ins.then_inc(self.sem["cc"], 1)
        self._commit(("cc", self.cnt["cc"]), reads, writes)

    def barrier(self):
        for e in self.eng:
            deps = {k: c for k, c in self.cnt.items() if c > 0 and k != e}
            self._wait(e, deps)

    def finish(self):
        deps = {k: c for k, c in self.cnt.items() if c > 0 and k != "sp"}
        self._wait("sp", deps)


def build_nc(S, debug=False, stop_after=9, limit=None):
    NT = S // 128
    NB = S // 512
    SH = S // 2
    nc = bass.Bass("TRN2", target_bir_lowering=False)

    def din(name, shape, dt=F32):
        return nc.dram_tensor(name, list(shape), dt, kind="ExternalInput").ap()

    x = din("x", [S, D_MODEL])
    xres = din("xres", [SH, D_MODEL])
    w_rw = din("w_rw", [D_MODEL, RWC])
    w_fox = din("w_fox", [D_MODEL, FXC])
    w_grw = din("w_grw", [D_MODEL, 256])
    w_gfx = din("w_gfx", [D_MODEL, 256])
    w_outp = din("w_outp", [D_MODEL, D_MODEL])
    g_col = din("g_col", [128, KC])
    pb1_d = din("pb1", [128, 2688])
    pb2_d = din("pb2", [128, 132])
    pb3_d = din("pb3", [128, D_MODEL])
    w_up_d = din("w_up", [64, 256])
    a_up_d = din("a_up", [64, 256])
    out = nc.dram_tensor("out", [SH, D_MODEL], F32, kind="ExternalOutput").ap()
    dbg_kind = "ExternalOutput" if debug else "Internal"
    ybuf = [nc.dram_tensor("ybuf%d" % c, [64, S], BF16, kind=dbg_kind).ap() for c in range(8)]
    yall = [nc.dram_tensor("yall%d" % c, [128, S], BF16, kind=dbg_kind).ap() for c in range(8)]
    sel_d = din("sel", [128, 2])
    ybuf_b = [Buf("ybuf%d" % c) for c in range(8)]
    yall_b = [Buf("yall%d" % c) for c in range(8)]
    out_b = Buf("out")

    sch = Sched(nc)
    sch.limit = limit
    op = sch.op

    class Scope:
        def __init__(self):
            self.stack = []

        def sb(self, name, shape, dt=F32):
            cm = nc.sbuf_tensor("s_" + name, list(shape), dt)
            t = cm.__enter__()
            self.stack.append(cm)
            return t, Buf(name)

        def ps(self, name, shape, dt=F32):
            cm = nc.psum_tensor("p_" + name, list(shape), dt)
            t = cm.__enter__()
            self.stack.append(cm)
            return t, Buf(name, psum=True)

        def close(self):
            while self.stack:
                self.stack.pop().__exit__(None, None, None)

    G = Scope()
    ident_bf, ident_bf_b = G.sb("ident_bf", [128, 128], BF16)
    ident_f, ident_f_b = G.sb("ident_f", [128, 128], F32)
    gcol, gcol_b = G.sb("gcol", [128, KC], F32)
    stage, stage_b = G.sb("stage", [128, 1024], F32)
    stage2, stage2_b = G.sb("stage2", [128, 1024], F32)

    def mk_ident(t, b):
        op("pool", lambda e: e.memset(t[:], 0.0), [], [b])
        op("pool", lambda e: e.affine_select(out=t[:], in_=t[:], pattern=[[-1, 128]], compare_op=ALU.not_equal,
                                             fill=1.0, base=0, channel_multiplier=1), [b], [b])

    mk_ident(ident_bf, ident_bf_b)
    mk_ident(ident_f, ident_f_b)
    sch.dma(gcol[:], g_col[:, :], [], [gcol_b])

    def load_weight(dst, dst_b, src, ncols, post=None):
        for c in range(KC):
            st, stb = (stage, stage_b) if c % 2 == 0 else (stage2, stage2_b)
            sch.dma(st[:, 0:ncols], src[c * 128:(c + 1) * 128, :], [], [stb])
            if post is None:
                op("dve", lambda e: e.tensor_scalar(out=dst[:, c, :], in0=st[:, 0:ncols], scalar1=gcol[:, c:c + 1],
                                                    scalar2=None, op0=ALU.mult), [stb, gcol_b], [dst_b])
            else:
                for (d2, d2b, pt, ptb) in post:
                    op("dve", lambda e: e.scalar_tensor_tensor(out=d2[:, c, :], in0=st[:, 0:ncols],
                                                               scalar=gcol[:, c:c + 1], in1=pt, op0=ALU.mult,
                                                               op1=ALU.mult), [stb, gcol_b, ptb], [d2b])

    def rmsnorm_to_hT(xt, xt_b, h, h_b, junk, junk_b, ss, ss_b, tpb, tpb_b, dst_ap_fn, dst_b):
        op("act", lambda e: e.activation(out=junk[:], in_=xt[:], func=AF.Square, accum_out=ss[:, 0:1]),
           [xt_b], [junk_b, ss_b])
        op("act", lambda e: e.activation(out=ss[:, 1:2], in_=ss[:, 0:1], func=AF.Ln, scale=1.0 / D_MODEL,
                                         bias=NORM_EPS), [ss_b], [ss_b])
        op("act", lambda e: e.activation(out=ss[:, 2:3], in_=ss[:, 1:2], func=AF.Exp, scale=-0.5), [ss_b], [ss_b])
        op("act", lambda e: e.activation(out=h[:], in_=xt[:], func=AF.Copy, scale=ss[:, 2:3]), [xt_b, ss_b], [h_b])
        tpv = tpb[:].rearrange("p (c t) -> p c t", c=KC)
        for c in range(KC):
            op("pe", lambda e: e.transpose(out=tpv[:, c, :], in_=h[:, c * 128:(c + 1) * 128], identity=ident_bf[:]),
               [h_b, ident_bf_b], [tpb_b])
        op("dve", lambda e: e.tensor_copy(out=dst_ap_fn(), in_=tpv), [tpb_b], [dst_b])

    P1 = Scope()
    Wa, Wa_b = P1.sb("Wa", [128, KC, RWC], BF16)
    Wb, Wb_b = P1.sb("Wb", [128, KC, RWC], BF16)
    Wg, Wg_b = P1.sb("Wg1", [128, KC, 256], BF16)
    pb1, pb1_b = P1.sb("pb1", [128, 2688], F32)
    omm, omm_b = P1.sb("omm", [128, RWC], F32)
    wup, wup_b = P1.sb("wup", [64, 256], F32)
    aup, aup_b = P1.sb("aup", [64, 256], F32)
    sch.dma(pb1[:], pb1_d[:, :], [], [pb1_b])
    sch.dma(wup[:], w_up_d[:, :], [], [wup_b])
    sch.dma(aup[:], a_up_d[:, :], [], [aup_b])
    MU = pb1[:, 0:896]
    W0 = pb1[:, 896:1152]
    A0 = pb1[:, 1152:1408]
    KK_ = pb1[:, 1408:1664]
    KA_ = pb1[:, 1664:1920]
    RK_ = pb1[:, 1920:2176]
    GNG = pb1[:, 2176:2432]
    GNB = pb1[:, 2432:2688]
    op("dve", lambda e: e.tensor_scalar(out=omm[:], in0=MU, scalar1=-1.0, scalar2=1.0, op0=ALU.mult, op1=ALU.add),
       [pb1_b], [omm_b])
    load_weight(None, None, w_rw, RWC, post=[(Wa, Wa_b, omm[:], omm_b), (Wb, Wb_b, MU, pb1_b)])
    load_weight(Wg, Wg_b, w_grw, 256)

    tri_i, tri_i_b = P1.sb("tri_i", [128, 128], F32)
    tri_s, tri_s_b = P1.sb("tri_s", [128, 128], F32)
    tri_r, tri_r_b = P1.sb("tri_r", [128, 128], F32)
    mask4, mask4_b = P1.sb("mask4", [128, 512], F32)
    maskr4, maskr4_b = P1.sb("maskr4", [128, 512], F32)
    ones_c, ones_c_b = P1.sb("ones_c", [128, 1], F32)

    def mk_tri(t, b, cm, pat, cmp_op):
        op("pool", lambda e: e.memset(t[:], 1.0), [], [b])
        op("pool", lambda e: e.affine_select(out=t[:], in_=t[:], pattern=[[pat, 128]], compare_op=cmp_op, fill=0.0,
                                             base=0, channel_multiplier=cm), [b], [b])
        op("pool", lambda e: e.memset(t[0:64, 64:128], 0.0), [b], [b])
        op("pool", lambda e: e.memset(t[64:128, 0:64], 0.0), [b], [b])

    mk_tri(tri_i, tri_i_b, -1, 1, ALU.is_ge)
    mk_tri(tri_s, tri_s_b, -1, 1, ALU.is_gt)
    mk_tri(tri_r, tri_r_b, 1, -1, ALU.is_gt)
    op("pool", lambda e: e.memset(ones_c[:], 1.0), [], [ones_c_b])
    for q_, (src, srcb) in enumerate([(tri_s, tri_s_b), (tri_i, tri_i_b), (tri_s, tri_s_b), (tri_i, tri_i_b)]):
        op("pool", lambda e: e.tensor_copy(out=mask4[:, q_ * 128:(q_ + 1) * 128], in_=src[:]), [srcb], [mask4_b])
        op("pool", lambda e: e.tensor_copy(out=maskr4[:, q_ * 128:(q_ + 1) * 128], in_=tri_r[:]), [tri_r_b], [maskr4_b])

    bank = [P1.ps("bank%d" % i, [128, 512], F32) for i in range(7)]
    bankT, bankT_b = P1.ps("bankT", [128, 1024], BF16)

    def sb(name, shape, dt=F32):
        return P1.sb(name, shape, dt)

    xt2 = [sb("xt%d" % i, [128, 1024]) for i in range(2)]
    h, h_b = sb("h", [128, 1024], BF16)
    junk, junk_b = sb("junk", [128, 1024], BF16)
    ss, ss_b = sb("ss", [128, 4])
    hT2 = [sb("hT%d" % i, [128, KC, 129], BF16) for i in range(2)]
    r32, r32_b = sb("r32", [128, 256])
    k32, k32_b = sb("k32", [128, 256])
    v32, v32_b = sb("v32", [128, 256])
    lt32, lt32_b = sb("lt32", [128, 128])
    lT, lT_b = sb("lT", [64, 2, 128])
    lwn, lwn_b = sb("lwn", [128, 256])
    t1, t1_b = sb("t1", [128, 256])
    t2, t2_b = sb("t2", [128, 256])
    a_, a_b = sb("a_", [128, 256])
    kk, kk_b = sb("kk", [128, 256])
    km, km_b = sb("km", [128, 256])
    ba, ba_b = sb("ba", [128, 256])
    s4, s4_b = sb("s4", [128, 16])
    ecum, ecum_b = sb("ecum", [128, 256])
    encum, encum_b = sb("encum", [128, 256])
    eprev, eprev_b = sb("eprev", [128, 256])
    erc, erc_b = sb("erc", [128, 256])
    wc, wc_b = sb("wc", [64, 8])
    tl4 = sb("tl4", [128, 4, 256], BF16)
    bh, bh_b = sb("bh", [128, 256], BF16)
    kh, kh_b = sb("kh", [128, 256], BF16)
    vb, vb_b = sb("vb", [128, 256], BF16)
    bhm, bhm_b = sb("bhm", [128, 2, 256], BF16)
    khm, khm_b = sb("khm", [128, 2, 256], BF16)
    cmask, cmask_b = sb("cmask", [128, 2])
    op("pool", lambda e: e.memset(cmask[:], 0.0), [], [cmask_b])
    op("pool", lambda e: e.memset(cmask[0:64, 0:1], 1.0), [cmask_b], [cmask_b])
    op("pool", lambda e: e.memset(cmask[64:128, 1:2], 1.0), [cmask_b], [cmask_b])
    TT, TT_b = sb("TT", [64, 16, 128], BF16)
    A_sb = [sb("A%d" % hh, [128, 512], BF16) for hh in range(NH)]
    Ni2 = [sb("Ni%d" % i, [128, 512], BF16) for i in range(2)]
    NiT2 = [sb("NiT%d" % i, [128, 512], BF16) for i in range(2)]
    X32, X32_b = sb("X32", [128, 512])
    Xb, Xb_b = sb("Xb", [128, 512], BF16)
    GTs, GTs_b = sb("GTs", [64, 1024])
    M0T, M0T_b = sb("M0T", [64, 512])
    Nst, Nst_b = sb("Nst", [64, 512])
    ST2 = [sb("ST%d" % i, [64, 256]) for i in range(2)]
    stt, stt_b = sb("stt", [64, 256])
    y32, y32_b = sb("y32", [128, 256])
    sg, sg_b = sb("sg", [128, 256])
    bon, bon_b = sb("bon", [128, 256])
    yg, yg_b = sb("yg", [128, 256], BF16)
    ygT, ygT_b = sb("ygT", [128, 2, 512], BF16)

    op("dve", lambda e: e.memset(ST2[0][0][:], 0.0), [], [ST2[0][1]])
    op("dve", lambda e: e.memset(GTs[:], 0.0), [], [GTs_b])
    op("dve", lambda e: e.memset(hT2[0][0][:, :, 0:1], 0.0), [], [hT2[0][1]])

    def v3(ap, hh=NH):
        return ap.rearrange("p (h d) -> p h d", h=hh)

    def bc(ap_small, hh=NH, d=HD):
        return ap_small.unsqueeze(2).to_broadcast([128, hh, d])

    st_cur = 0
    for i in range(NT):
        xt, xt_b = xt2[i % 2]
        hT, hT_b = hT2[i % 2]
        hTp, hTp_b = hT2[(i + 1) % 2]
        if i == 0:
            sch.dma(xt[:], x[0:128, :], [], [xt_b])
        if i + 1 < NT:
            sch.dma(xt2[(i + 1) % 2][0][:], x[(i + 1) * 128:(i + 2) * 128, :], [], [xt2[(i + 1) % 2][1]])
        rmsnorm_to_hT(xt, xt_b, h, h_b, junk, junk_b, ss, ss_b, bankT, bankT_b, lambda: hT[:, :, 1:129], hT_b)
        if i > 0:
            op("pool", lambda e: e.tensor_copy(out=hT[:, :, 0:1], in_=hTp[:, :, 128:129]), [hTp_b], [hT_b])
        pr0, pr0_b = bank[0]
        pr1, pr1_b = bank[1]
        for (pt, ptb, n0, n1) in ((pr0, pr0_b, 0, 512), (pr1, pr1_b, 512, 896)):
            for c in range(KC):
                op("pe", lambda e: e.matmul(pt[:, 0:n1 - n0], hT[:, c, 1:129], Wa[:, c, n0:n1], start=(c == 0), stop=False),
                   [hT_b, Wa_b], [ptb])
            for c in range(KC):
                op("pe", lambda e: e.matmul(pt[:, 0:n1 - n0], hT[:, c, 0:128], Wb[:, c, n0:n1], start=False, stop=(c == KC - 1)),
                   [hT_b, Wb_b], [ptb])
        pg, pg_b = bank[2]
        for c in range(KC):
            op("pe", lambda e: e.matmul(pg[:, 0:256], hT[:, c, 1:129], Wg[:, c, :], start=(c == 0), stop=(c == KC - 1)),
               [hT_b, Wg_b], [pg_b])
        op("act", lambda e: e.activation(out=r32[:], in_=pr0[:, 0:256], func=AF.Copy), [pr0_b], [r32_b])
        op("dve", lambda e: e.tensor_copy(out=k32[:], in_=pr0[:, 256:512]), [pr0_b], [k32_b])
        op("act", lambda e: e.activation(out=v32[:], in_=pr1[:, 0:256], func=AF.Copy), [pr1_b], [v32_b])
        op("act", lambda e: e.activation(out=lt32[:, 0:64], in_=pr1[:, 256:320], func=AF.Exp, scale=-2.0), [pr1_b], [lt32_b])
        op("dve", lambda e: e.tensor_scalar(out=lt32[:, 0:64], in0=lt32[:, 0:64], scalar1=1.0, scalar2=None, op0=ALU.add),
           [lt32_b], [lt32_b])
        op("dve", lambda e: e.reciprocal(out=lt32[:, 0:64], in_=lt32[:, 0:64]), [lt32_b], [lt32_b])
        op("dve", lambda e: e.tensor_scalar(out=lt32[:, 0:64], in0=lt32[:, 0:64], scalar1=2.0, scalar2=-1.0, op0=ALU.mult,
                                            op1=ALU.add), [lt32_b], [lt32_b])
        op("dve", lambda e: e.tensor_copy(out=lt32[:, 64:128], in_=pr1[:, 320:384]), [pr1_b, lt32_b], [lt32_b])
        ptl, ptl_b = bank[3]
        for q_ in range(2):
            op("pe", lambda e: e.transpose(out=ptl[0:64, q_ * 128:(q_ + 1) * 128], in_=lt32[:, q_ * 64:(q_ + 1) * 64],
                                           identity=ident_f[:]), [lt32_b, ident_f_b], [ptl_b])
        op("dve", lambda e: e.tensor_copy(out=lT[:].rearrange("p a t -> p (a t)"), in_=ptl[0:64, 0:256]), [ptl_b], [lT_b])
        pl, pl_b = bank[4]
        op("pe", lambda e: e.matmul(pl[:, 0:256], lT[:, 0, :], wup[:], start=True, stop=True), [lT_b, wup_b], [pl_b])
        op("pe", lambda e: e.matmul(pl[:, 256:512], lT[:, 1, :], aup[:], start=True, stop=True), [lT_b, aup_b], [pl_b])
        op("dve", lambda e: e.tensor_tensor(out=t1[:], in0=pl[:, 0:256], in1=W0, op=ALU.add), [pl_b, pb1_b], [t1_b])
        op("act", lambda e: e.activation(out=t1[:], in_=t1[:], func=AF.Exp, scale=-1.0), [t1_b], [t1_b])
        op("act", lambda e: e.activation(out=t1[:], in_=t1[:], func=AF.Ln, bias=1.0), [t1_b], [t1_b])
        op("act", lambda e: e.activation(out=lwn[:], in_=t1[:], func=AF.Exp, scale=-1.0, bias=-0.5), [t1_b], [lwn_b])
        op("dve", lambda e: e.tensor_tensor(out=t2[:], in0=pl[:, 256:512], in1=A0, op=ALU.add), [pl_b, pb1_b], [t2_b])
        op("act", lambda e: e.activation(out=t2[:], in_=t2[:], func=AF.Exp, scale=-1.0), [t2_b], [t2_b])
        op("dve", lambda e: e.tensor_scalar(out=t2[:], in0=t2[:], scalar1=1.0, scalar2=None, op0=ALU.add), [t2_b], [t2_b])
        op("dve", lambda e: e.reciprocal(out=a_[:], in_=t2[:]), [t2_b], [a_b])
        op("dve", lambda e: e.tensor_tensor(out=kk[:], in0=k32[:], in1=KK_, op=ALU.mult), [k32_b, pb1_b], [kk_b])
        op("dve", lambda e: e.tensor_tensor(out=t1[:], in0=kk[:], in1=kk[:], op=ALU.mult), [kk_b], [t1_b])
        op("dve", lambda e: e.tensor_reduce(out=s4[:, 0:4], in_=v3(t1[:]), axis=AX.X, op=ALU.add), [t1_b], [s4_b])
        op("dve", lambda e: e.tensor_scalar(out=s4[:, 0:4], in0=s4[:, 0:4], scalar1=1e-24, scalar2=None, op0=ALU.max),
           [s4_b], [s4_b])
        op("act", lambda e: e.activation(out=s4[:, 0:4], in_=s4[:, 0:4], func=AF.Ln), [s4_b], [s4_b])
        op("act", lambda e: e.activation(out=s4[:, 0:4], in_=s4[:, 0:4], func=AF.Exp, scale=-0.5), [s4_b], [s4_b])
        op("dve", lambda e: e.tensor_tensor(out=v3(kk[:]), in0=v3(kk[:]), in1=bc(s4[:, 0:4]), op=ALU.mult),
           [kk_b, s4_b], [kk_b])
        op("dve", lambda e: e.scalar_tensor_tensor(out=t1[:], in0=a_[:], scalar=-1.0, in1=KA_, op0=ALU.add, op1=ALU.mult),
           [a_b, pb1_b], [t1_b])
        op("dve", lambda e: e.scalar_tensor_tensor(out=km[:], in0=t1[:], scalar=1.0, in1=k32[:], op0=ALU.add, op1=ALU.mult),
           [t1_b, k32_b], [km_b])
        op("dve", lambda e: e.tensor_tensor(out=ba[:], in0=kk[:], in1=a_[:], op=ALU.mult), [kk_b, a_b], [ba_b])
        op("dve", lambda e: e.tensor_tensor(out=t1[:], in0=r32[:], in1=km[:], op=ALU.mult), [r32_b, km_b], [t1_b])
        op("dve", lambda e: e.tensor_tensor(out=t1[:], in0=t1[:], in1=RK_, op=ALU.mult), [t1_b, pb1_b], [t1_b])
        op("dve", lambda e: e.tensor_reduce(out=s4[:, 4:8], in_=v3(t1[:]), axis=AX.X, op=ALU.add), [t1_b], [s4_b])
        op("dve", lambda e: e.tensor_tensor(out=v3(bon[:]), in0=v3(v32[:]), in1=bc(s4[:, 4:8]), op=ALU.mult),
           [v32_b, s4_b], [bon_b])
        op("act", lambda e: e.activation(out=t2[:], in_=pg[:, 0:256], func=AF.Exp, scale=-1.0), [pg_b], [t2_b])
        op("dve", lambda e: e.tensor_scalar(out=t2[:], in0=t2[:], scalar1=1.0, scalar2=None, op0=ALU.add), [t2_b], [t2_b])
        op("dve", lambda e: e.reciprocal(out=t2[:], in_=t2[:]), [t2_b], [t2_b])
        op("dve", lambda e: e.tensor_tensor(out=sg[:], in0=t2[:], in1=pg[:, 0:256], op=ALU.mult), [t2_b, pg_b], [sg_b])
        pc0, pc0_b = bank[5]
        pc1, pc1_b = bank[6]
        op("pe", lambda e: e.matmul(pc0[:, 0:256], tri_i[:], lwn[:], start=True, stop=True), [tri_i_b, lwn_b], [pc0_b])
        op("pe", lambda e: e.matmul(pc0[:, 256:512], tri_s[:], lwn[:], start=True, stop=True), [tri_s_b, lwn_b], [pc0_b])
        op("pe", lambda e: e.matmul(pc1[:, 0:256], tri_r[:], lwn[:], start=True, stop=True), [tri_r_b, lwn_b], [pc1_b])
        for cch in range(2):
            for hh in range(NH):
                col = 256 + cch * 4 + hh
                op("pe", lambda e: e.matmul(pc1[0:64, col:col + 1], lwn[cch * 64:(cch + 1) * 64, hh * 64:(hh + 1) * 64],
                                            ones_c[cch * 64:(cch + 1) * 64, :], start=True, stop=True),
                   [lwn_b, ones_c_b], [pc1_b])
        op("act", lambda e: e.activation(out=ecum[:], in_=pc0[:, 0:256], func=AF.Exp, scale=-1.0), [pc0_b], [ecum_b])
        op("act", lambda e: e.activation(out=encum[:], in_=pc0[:, 0:256], func=AF.Exp), [pc0_b], [encum_b])
        op("act", lambda e: e.activation(out=eprev[:], in_=pc0[:, 256:512], func=AF.Exp, scale=-1.0), [pc0_b], [eprev_b])
        op("act", lambda e: e.activation(out=erc[:], in_=pc1[:, 0:256], func=AF.Exp, scale=-1.0), [pc1_b], [erc_b])
        op("act", lambda e: e.activation(out=wc[:], in_=pc1[0:64, 256:264], func=AF.Exp, scale=-1.0), [pc1_b], [wc_b])
        tl, tl_b = tl4
        op("dve", lambda e: e.scalar_tensor_tensor(out=tl[:, 0, :], in0=kk[:], scalar=-1.0, in1=eprev[:],
                                                   op0=ALU.mult, op1=ALU.mult), [kk_b, eprev_b], [tl_b])
        op("dve", lambda e: e.tensor_tensor(out=tl[:, 1, :], in0=r32[:], in1=ecum[:], op=ALU.mult), [r32_b, ecum_b], [tl_b])
        op("dve", lambda e: e.tensor_tensor(out=tl[:, 2, :], in0=ba[:], in1=encum[:], op=ALU.mult), [ba_b, encum_b], [tl_b])
        op("dve", lambda e: e.tensor_tensor(out=tl[:, 3, :], in0=km[:], in1=encum[:], op=ALU.mult), [km_b, encum_b], [tl_b])
        op("pool", lambda e: e.tensor_tensor(out=bh[:], in0=ba[:], in1=erc[:], op=ALU.mult), [ba_b, erc_b], [bh_b])
        op("pool", lambda e: e.tensor_tensor(out=kh[:], in0=km[:], in1=erc[:], op=ALU.mult), [km_b, erc_b], [kh_b])
        op("pool", lambda e: e.tensor_copy(out=vb[:], in_=v32[:]), [v32_b], [vb_b])
        TTp = bankT[0:64, :].rearrange("p (a t) -> p a t", a=8)
        for rnd in range(2):
            for tq in range(2):
                ty = rnd * 2 + tq
                for hh in range(NH):
                    op("pe", lambda e: e.transpose(out=TTp[:, tq * 4 + hh, :], in_=tl[:, ty, hh * 64:(hh + 1) * 64],
                                                   identity=ident_bf[:]), [tl_b, ident_bf_b], [bankT_b])
            op("act", lambda e: e.activation(out=TT[:, rnd * 8:(rnd + 1) * 8, :], in_=TTp, func=AF.Copy), [bankT_b], [TT_b])

        def kT(ty, hh):
            return TT[:, ty * 4 + hh, :]

        pn, pn_b = bank[2]
        for hh in range(NH):
            pa, pa_b = bank[hh % 2]
            rhs2 = TT[:, hh:hh + 5:4, :]
            op("pe", lambda e: e.matmul(pa[:, 0:256], kT(2, hh), rhs2, start=True, stop=True), [TT_b], [pa_b])
            op("pe", lambda e: e.matmul(pa[:, 256:512], kT(3, hh), rhs2, start=True, stop=True), [TT_b], [pa_b])
            op("pe", lambda e: e.matmul(pn[:, hh * 128:(hh + 1) * 128], kT(0, hh), kT(2, hh), start=True, stop=True),
               [TT_b], [pn_b])
            A, A_b = A_sb[hh]
            op("dve", lambda e: e.tensor_tensor(out=A[:], in0=pa[:], in1=mask4[:], op=ALU.mult), [pa_b, mask4_b], [A_b])
        Ni, Ni_b = Ni2[0]
        op("dve", lambda e: e.tensor_tensor(out=Ni[:], in0=pn[:], in1=maskr4[:], op=ALU.mult), [pn_b, maskr4_b], [Ni_b])
        px, px_b = bank[4]
        for hh in range(NH):
            A, A_b = A_sb[hh]
            op("pe", lambda e: e.matmul(px[:, hh * 128:hh * 128 + 64], ident_bf[:], tl[:, 0, hh * 64:(hh + 1) * 64],
                                        start=(hh == 0), stop=False, skip_group_check=True), [tl_b, ident_bf_b], [px_b])
            op("pe", lambda e: e.matmul(px[:, hh * 128 + 64:(hh + 1) * 128], A[:, 256:384], vb[:, hh * 64:(hh + 1) * 64],
                                        start=False, stop=False, skip_group_check=True), [A_b, vb_b], [px_b])
        op("dve", lambda e: e.tensor_copy(out=Xb[:], in_=px[:]), [px_b], [Xb_b])
        for s_ in range(6):
            cur = s_ % 2
            Ni, Ni_b = Ni2[cur]
            NiT, NiT_b = NiT2[cur]
            for hh in range(NH):
                if s_ == 0:
                    lt_, ltb = A_sb[hh][0][:, 0:128], A_sb[hh][1]
                else:
                    lt_, ltb = NiT[:, hh * 128:(hh + 1) * 128], NiT_b
                op("pe", lambda e: e.matmul(px[:, hh * 128:(hh + 1) * 128], lt_, Xb[:, hh * 128:(hh + 1) * 128],
                                            start=False, stop=(s_ == 5), skip_group_check=True), [ltb, Xb_b], [px_b])
            if s_ < 5:
                ps1, ps1_b = bank[5]
                ps2, ps2_b = bank[6]
                Nn_, Nn_b = Ni2[1 - cur]
                NnT, NnT_b = NiT2[1 - cur]
                for hh in range(NH):
                    if s_ == 0:
                        lt_, ltb = A_sb[hh][0][:, 0:128], A_sb[hh][1]
                    else:
                        lt_, ltb = NiT[:, hh * 128:(hh + 1) * 128], NiT_b
                    n_ = Ni[:, hh * 128:(hh + 1) * 128]
                    op("pe", lambda e: e.matmul(ps2[:, hh * 128:(hh + 1) * 128], n_, lt_, start=True, stop=True),
                       [ltb, Ni_b], [ps2_b])
                    op("pe", lambda e: e.matmul(ps1[:, hh * 128:(hh + 1) * 128], lt_, n_, start=True, stop=True),
                       [ltb, Ni_b], [ps1_b])
            op("dve", lambda e: e.tensor_copy(out=Xb[:], in_=px[:]), [px_b], [Xb_b])
            if s_ < 5:
                op("dve", lambda e: e.tensor_copy(out=NnT[:], in_=ps2[:]), [ps2_b], [NnT_b])
                op("act", lambda e: e.activation(out=Nn_[:], in_=ps1[:], func=AF.Copy), [ps1_b], [Nn_b])
        Xb3 = Xb[:].rearrange("p (h c) -> p h c", h=NH)
        pgt, pgt_b = bank[0]
        tlr = tl[:, 1, :]
        for hh in range(NH):
            A, A_b = A_sb[hh]
            op("pe", lambda e: e.matmul(pgt[0:64, hh * 128:(hh + 1) * 128], Xb3[:, hh, 0:64], A[:, 128:256],
                                        start=True, stop=False), [Xb_b, A_b], [pgt_b])
            op("pe", lambda e: e.matmul(pgt[0:64, hh * 128:(hh + 1) * 128], tlr[:, hh * 64:(hh + 1) * 64], ident_bf[:],
                                        start=False, stop=True), [tl_b, ident_bf_b], [pgt_b])
        pgt5 = pgt[0:64, :].rearrange("p (h c t) -> p h c t", h=NH, c=2)
        GT5 = GTs[:].rearrange("p (a h c t) -> p a h c t", a=2, h=NH, c=2)
        for cch in range(2):
            op("act", lambda e: e.activation(out=GT5[:, cch, :, cch, :], in_=pgt5[:, :, cch, :], func=AF.Copy), [pgt_b], [GTs_b])
        pm, pm_b = bank[1]
        pn2, pn2_b = bank[5]
        for cch in range(2):
            op("pool", lambda e: e.tensor_scalar(out=bhm[:, cch, :], in0=bh[:], scalar1=cmask[:, cch:cch + 1], scalar2=None,
                                                 op0=ALU.mult), [bh_b, cmask_b], [bhm_b])
            op("pool", lambda e: e.tensor_scalar(out=khm[:, cch, :], in0=kh[:], scalar1=cmask[:, cch:cch + 1], scalar2=None,
                                                 op0=ALU.mult), [kh_b, cmask_b], [khm_b])
        for cch in range(2):
            for hh in range(NH):
                cs = slice((cch * 4 + hh) * 64, (cch * 4 + hh + 1) * 64)
                hs = slice(hh * 64, (hh + 1) * 64)
                op("pe", lambda e: e.matmul(pm[0:64, cs], Xb3[:, hh, 0:64], bhm[:, cch, hs], start=True, stop=True),
                   [Xb_b, bhm_b], [pm_b])
                op("pe", lambda e: e.matmul(pn2[0:64, cs], bhm[:, cch, hs], Xb3[:, hh, 64:128], start=True, stop=False),
                   [Xb_b, bhm_b], [pn2_b])
                op("pe", lambda e: e.matmul(pn2[0:64, cs], khm[:, cch, hs], vb[:, hs], start=False, stop=True),
                   [khm_b, vb_b], [pn2_b])
        op("act", lambda e: e.activation(out=M0T[:], in_=pm[0:64, :], func=AF.Copy), [pm_b], [M0T_b])
        op("dve", lambda e: e.tensor_copy(out=Nst[:], in_=pn2[0:64, :]), [pn2_b], [Nst_b])
        py, py_b = bank[3]
        for hh in range(NH):
            A, A_b = A_sb[hh]
            hs = slice(hh * 64, (hh + 1) * 64)
            op("pe", lambda e: e.matmul(py[:, hs], A[:, 128:256], Xb3[:, hh, 64:128], start=(hh == 0), stop=False,
                                        skip_group_check=True), [A_b, Xb_b], [py_b])
            op("pe", lambda e: e.matmul(py[:, hs], A[:, 384:512], vb[:, hs], start=False, stop=False, skip_group_check=True),
               [A_b, vb_b], [py_b])
        pst, pst_b = bank[6]
        for cch in range(2):
            ST, ST_b = ST2[st_cur]
            STn, STn_b = ST2[1 - st_cur]
            for hh in range(NH):
                hs = slice(hh * 64, (hh + 1) * 64)
                op("pe", lambda e: e.matmul(py[:, hs], GTs[:, cch * 512 + hh * 128: cch * 512 + (hh + 1) * 128],
                                            ST[:, hs], start=False, stop=True, skip_group_check=True), [GTs_b, ST_b], [py_b])
            for hh in range(NH):
                hs = slice(hh * 64, (hh + 1) * 64)
                cs = slice((cch * 4 + hh) * 64, (cch * 4 + hh + 1) * 64)
                op("pe", lambda e: e.matmul(pst[0:64, hs], M0T[:, cs], ST[:, hs], start=True, stop=True), [M0T_b, ST_b], [pst_b])
            op("dve", lambda e: e.tensor_tensor(out=v3(stt[:]), in0=v3(ST[:]),
                                                in1=wc[:, cch * 4:(cch + 1) * 4].unsqueeze(2).to_broadcast([64, NH, HD]),
                                                op=ALU.mult), [ST_b, wc_b], [stt_b])
            op("dve", lambda e: e.tensor_tensor(out=stt[:], in0=stt[:], in1=Nst[:, cch * 256:(cch + 1) * 256], op=ALU.add),
               [stt_b, Nst_b], [stt_b])
            op("dve", lambda e: e.tensor_tensor(out=STn[:], in0=stt[:], in1=pst[0:64, 0:256], op=ALU.add),
               [stt_b, pst_b], [STn_b])
            st_cur = 1 - st_cur
        op("act", lambda e: e.activation(out=y32[:], in_=py[:, 0:256], func=AF.Copy), [py_b], [y32_b])
        op("dve", lambda e: e.tensor_reduce(out=s4[:, 8:12], in_=v3(y32[:]), axis=AX.X, op=ALU.add), [y32_b], [s4_b])
        op("dve", lambda e: e.tensor_scalar(out=s4[:, 8:12], in0=s4[:, 8:12], scalar1=1.0 / HD, scalar2=None, op0=ALU.mult),
           [s4_b], [s4_b])
        op("dve", lambda e: e.tensor_tensor(out=v3(y32[:]), in0=v3(y32[:]), in1=bc(s4[:, 8:12]), op=ALU.subtract),
           [y32_b, s4_b], [y32_b])
        op("dve", lambda e: e.tensor_tensor(out=t1[:], in0=y32[:], in1=y32[:], op=ALU.mult), [y32_b], [t1_b])
        op("dve", lambda e: e.tensor_reduce(out=s4[:, 12:16], in_=v3(t1[:]), axis=AX.X, op=ALU.add), [t1_b], [s4_b])
        op("act", lambda e: e.activation(out=s4[:, 12:16], in_=s4[:, 12:16], func=AF.Ln, scale=1.0 / HD, bias=GN_EPS),
           [s4_b], [s4_b])
        op("act", lambda e: e.activation(out=s4[:, 12:16], in_=s4[:, 12:16], func=AF.Exp, scale=-0.5), [s4_b], [s4_b])
        op("dve", lambda e: e.tensor_tensor(out=v3(y32[:]), in0=v3(y32[:]), in1=bc(s4[:, 12:16]), op=ALU.mult),
           [y32_b, s4_b], [y32_b])
        op("dve", lambda e: e.tensor_tensor(out=y32[:], in0=y32[:], in1=GNG, op=ALU.mult), [y32_b, pb1_b], [y32_b])
        op("dve", lambda e: e.tensor_tensor(out=y32[:], in0=y32[:], in1=GNB, op=ALU.add), [y32_b, pb1_b], [y32_b])
        op("dve", lambda e: e.tensor_tensor(out=y32[:], in0=y32[:], in1=bon[:], op=ALU.add), [y32_b, bon_b], [y32_b])
        op("dve", lambda e: e.tensor_tensor(out=yg[:], in0=y32[:], in1=sg[:], op=ALU.mult), [y32_b, sg_b], [yg_b])
        ygp = bankT[:, 0:256].rearrange("p (a t) -> p a t", a=2)
        for q_ in range(2):
            op("pe", lambda e: e.transpose(out=ygp[:, q_, :], in_=yg[:, q_ * 128:(q_ + 1) * 128], identity=ident_bf[:]),
               [yg_b, ident_bf_b], [bankT_b])
        ti = i % 4
        op("act", lambda e: e.activation(out=ygT[:, :, ti * 128:(ti + 1) * 128], in_=ygp, func=AF.Copy), [bankT_b], [ygT_b])
        if ti == 3:
            t0 = (i // 4) * 512
            for hd in range(NH):
                sch.dma(ybuf[hd][:, t0:t0 + 512], ygT[(hd % 2) * 64:(hd % 2 + 1) * 64, hd // 2, :], [ygT_b], [ybuf_b[hd]])

    sch.barrier()
    P1.close()

    def gather(c):
        sch.collective(lambda e: e.collective_compute("AllGather", ALU.bypass, replica_groups=[[0, 1], [2, 3], [4, 5], [6, 7]],
                                                      ins=[ybuf[c][:, :]], outs=[yall[c][:, :]]), [ybuf_b[c]], [yall_b[c]])

    if stop_after > 2:
        for c in range(4):
            gather(c)
    if stop_after <= 1:
        sch.finish()
        return nc

    P2 = Scope()

    def sb2(name, shape, dt=F32):
        return P2.sb(name, shape, dt)

    Wf, Wf_b = sb2("Wf", [128, KC, FXC], BF16)
    Wg2, Wg2_b = sb2("Wg2", [128, KC, 256], BF16)
    pb2, pb2_b = sb2("pb2", [128, 132])
    sch.dma(pb2[:], pb2_d[:, :], [], [pb2_b])
    load_weight(Wf, Wf_b, w_fox, FXC)
    load_weight(Wg2, Wg2_b, w_gfx, 256)
    QG = pb2[:, 0:64]
    KG = pb2[:, 64:128]
    BFB = pb2[:, 128:132]
    qgs, qgs_b = sb2("qgs", [128, 64])
    op("dve", lambda e: e.tensor_scalar(out=qgs[:], in0=QG, scalar1=HD ** -0.5, scalar2=None, op0=ALU.mult), [pb2_b], [qgs_b])
    tri_f, tri_f_b = sb2("tri_f", [128, 128])
    sel127, sel127_b = sb2("sel127", [128, 128])
    ones_f, ones_f_b = sb2("ones_f", [128, 64])
    op("pool", lambda e: e.memset(tri_f[:], 1.0), [], [tri_f_b])
    op("pool", lambda e: e.affine_select(out=tri_f[:], in_=tri_f[:], pattern=[[1, 128]], compare_op=ALU.is_ge, fill=0.0,
                                         base=0, channel_multiplier=-1), [tri_f_b], [tri_f_b])
    op("pool", lambda e: e.memset(sel127[:], 1.0), [], [sel127_b])
    op("pool", lambda e: e.affine_select(out=sel127[:], in_=sel127[:], pattern=[[0, 128]], compare_op=ALU.is_ge, fill=0.0,
                                         base=-127, channel_multiplier=1), [sel127_b], [sel127_b])
    op("pool", lambda e: e.memset(ones_f[:], 1.0), [], [ones_f_b])

    KT, KT_b = sb2("KT", [70, NH, S], BF16)
    vaug, vaug_b = sb2("vaug", [128, NT, NH, 65], BF16)
    op("pool", lambda e: e.memset(vaug[:], 1.0), [], [vaug_b])
    bank = [P2.ps("bank2_%d" % i, [128, 512], F32) for i in range(7)]
    bankT, bankT_b = P2.ps("bankT2", [128, 1024], BF16)
    xt2 = [sb2("xtb%d" % i, [128, 1024]) for i in range(2)]
    h, h_b = sb2("hb", [128, 1024], BF16)
    junk, junk_b = sb2("junkb", [128, 1024], BF16)
    ss, ss_b = sb2("ssb", [128, 4])
    hTb, hTb_b = sb2("hTb", [128, KC, 512], BF16)
    qk32, qk32_b = sb2("qk32", [128, 512])
    sq, sq_b = sb2("sq", [128, 512])
    s8, s8_b = sb2("s8", [128, 8])
    cn2 = [sb2("cn%d" % i, [128, 4]) for i in range(2)]
    z4, z4_b = sb2("z4", [128, 4])
    spl, spl_b = sb2("spl", [128, 3, 4], BF16)
    rr, rr_b = sb2("rr", [128, 4])
    qaug2 = [sb2("qaug%d" % i, [128, NH, 70], BF16) for i in range(2)]
    kaug2 = [sb2("kaug%d" % i, [128, NH, 70], BF16) for i in range(2)]
    QT, QT_b = sb2("QT", [70, NH, 512], BF16)
    sgT, sgT_b = sb2("sgT", [64, NH, 512], BF16)
    eg, eg_b = sb2("eg", [64, 512])
    pT3 = [sb2("pT%d" % i, [128, 512], BF16) for i in range(3)]
    osb, osb_b = sb2("osb", [65, 512])
    rec, rec_b = sb2("rec", [65, 512])
    yt32, yt32_b = sb2("yt32", [64, 512])
    yfT2 = [sb2("yfT%d" % i, [64, 512], BF16) for i in range(2)]
    for i_ in range(2):
        op("pool", lambda e: e.memset(qaug2[i_][0][:], 1.0), [], [qaug2[i_][1]])
        op("pool", lambda e: e.memset(kaug2[i_][0][:], 1.0), [], [kaug2[i_][1]])
    op("dve", lambda e: e.memset(cn2[1][0][:], 0.0), [], [cn2[1][1]])

    pt_rot = 0
    sc_rot = 0
    yf_rot = 0
    for qb in range(NB):
        for ti in range(4):
            i = qb * 4 + ti
            xt, xt_b = xt2[i % 2]
            if i == 0:
                sch.dma(xt[:], x[0:128, :], [], [xt_b])
            if i + 1 < NT:
                sch.dma(xt2[(i + 1) % 2][0][:], x[(i + 1) * 128:(i + 2) * 128, :], [], [xt2[(i + 1) % 2][1]])
            rmsnorm_to_hT(xt, xt_b, h, h_b, junk, junk_b, ss, ss_b, bankT, bankT_b,
                          lambda: hTb[:, :, ti * 128:(ti + 1) * 128], hTb_b)
            pf0, pf0_b = bank[0]
            pf1, pf1_b = bank[1]
            for (pt, ptb, n0, n1) in ((pf0, pf0_b, 0, 512), (pf1, pf1_b, 512, FXC)):
                for c in range(KC):
                    op("pe", lambda e: e.matmul(pt[:, 0:n1 - n0], hTb[:, c, ti * 128:(ti + 1) * 128], Wf[:, c, n0:n1],
                                                start=(c == 0), stop=(c == KC - 1)), [hTb_b, Wf_b], [ptb])
            qa, qa_b = qaug2[i % 2]
            ka, ka_b = kaug2[i % 2]
            op("act", lambda e: e.activation(out=qk32[:], in_=pf0[:], func=AF.Copy), [pf0_b], [qk32_b])
            op("dve", lambda e: e.tensor_tensor(out=sq[:], in0=qk32[:], in1=qk32[:], op=ALU.mult), [qk32_b], [sq_b])
            op("dve", lambda e: e.tensor_reduce(out=s8[:], in_=v3(sq[:], 8), axis=AX.X, op=ALU.add), [sq_b], [s8_b])
            op("act", lambda e: e.activation(out=s8[:], in_=s8[:], func=AF.Ln, scale=1.0 / HD, bias=NORM_EPS), [s8_b], [s8_b])
            op("act", lambda e: e.activation(out=s8[:], in_=s8[:], func=AF.Exp, scale=-0.5), [s8_b], [s8_b])
            op("dve", lambda e: e.tensor_tensor(out=v3(qk32[:], 8), in0=v3(qk32[:], 8), in1=bc(s8[:], 8), op=ALU.mult),
               [qk32_b, s8_b], [qk32_b])
            op("dve", lambda e: e.tensor_tensor(out=qa[:, :, 0:64], in0=v3(qk32[:, 0:256]),
                                                in1=qgs[:].unsqueeze(1).to_broadcast([128, NH, HD]), op=ALU.mult),
               [qk32_b, qgs_b], [qa_b])
            op("dve", lambda e: e.tensor_tensor(out=ka[:, :, 0:64], in0=v3(qk32[:, 256:512]),
                                                in1=KG.unsqueeze(1).to_broadcast([128, NH, HD]), op=ALU.mult),
               [qk32_b, pb2_b], [ka_b])
            op("act", lambda e: e.activation(out=vaug[:, i, :, 0:64], in_=v3(pf1[:, 0:256]), func=AF.Copy), [pf1_b], [vaug_b])
            op("dve", lambda e: e.tensor_tensor(out=z4[:], in0=pf1[:, 256:260], in1=BFB, op=ALU.add), [pf1_b, pb2_b], [z4_b])
            op("act", lambda e: e.activation(out=z4[:], in_=z4[:], func=AF.Exp, scale=-1.0), [z4_b], [z4_b])
            op("act", lambda e: e.activation(out=z4[:], in_=z4[:], func=AF.Ln, bias=1.0), [z4_b], [z4_b])
            cn, cn_b = cn2[i % 2]
            cnp, cnp_b = cn2[(i + 1) % 2]
            pcs, pcs_b = bank[2]
            op("pe", lambda e: e.matmul(pcs[:, 0:4], tri_f[:], z4[:], start=True, stop=False), [tri_f_b, z4_b], [pcs_b])
            op("pe", lambda e: e.matmul(pcs[:, 0:4], sel127[:], cnp[:], start=False, stop=True), [sel127_b, cnp_b], [pcs_b])
            op("dve", lambda e: e.tensor_copy(out=cn[:], in_=pcs[:, 0:4]), [pcs_b], [cn_b])
            op("dve", lambda e: e.tensor_copy(out=spl[:, 0, :], in_=cn[:]), [cn_b], [spl_b])
            op("dve", lambda e: e.tensor_tensor(out=rr[:], in0=cn[:], in1=spl[:, 0, :], op=ALU.subtract), [cn_b, spl_b], [rr_b])
            op("dve", lambda e: e.tensor_copy(out=spl[:, 1, :], in_=rr[:]), [rr_b], [spl_b])
            op("dve", lambda e: e.tensor_tensor(out=rr[:], in0=rr[:], in1=spl[:, 1, :], op=ALU.subtract), [rr_b, spl_b], [rr_b])
            op("dve", lambda e: e.tensor_copy(out=spl[:, 2, :], in_=rr[:]), [rr_b], [spl_b])
            op("dve", lambda e: e.tensor_copy(out=ka[:, :, 67:70], in_=spl[:].rearrange("p s h -> p h s")), [spl_b], [ka_b])
            op("dve", lambda e: e.tensor_scalar(out=qa[:, :, 64:67], in0=spl[:].rearrange("p s h -> p h s"), scalar1=-1.0,
                                                scalar2=None, op0=ALU.mult), [spl_b], [qa_b])
            tqp = bankT[:].rearrange("p (a t) -> p a t", a=8)
            for hh in range(NH):
                op("pe", lambda e: e.transpose(out=tqp[0:70, hh, :], in_=qa[:, hh, :], identity=ident_bf[:]),
                   [qa_b, ident_bf_b], [bankT_b])
                op("pe", lambda e: e.transpose(out=tqp[0:70, 4 + hh, :], in_=ka[:, hh, :], identity=ident_bf[:]),
                   [ka_b, ident_bf_b], [bankT_b])
            op("dve", lambda e: e.tensor_copy(out=QT[:, :, ti * 128:(ti + 1) * 128], in_=tqp[0:70, 0:4, :]), [bankT_b], [QT_b])
            op("act", lambda e: e.activation(out=KT[:, :, i * 128:(i + 1) * 128], in_=tqp[0:70, 4:8, :], func=AF.Copy),
               [bankT_b], [KT_b])
        for hh in range(NH):
            pgt, pgt_b = bank[2]
            for c in range(KC):
                op("pe", lambda e: e.matmul(pgt[0:64, :], Wg2[:, c, hh * 64:(hh + 1) * 64], hTb[:, c, :], start=(c == 0),
                                            stop=(c == KC - 1)), [Wg2_b, hTb_b], [pgt_b])
            op("act", lambda e: e.activation(out=eg[:], in_=pgt[0:64, :], func=AF.Exp, scale=-1.0), [pgt_b], [eg_b])
            op("dve", lambda e: e.tensor_scalar(out=eg[:], in0=eg[:], scalar1=1.0, scalar2=None, op0=ALU.add), [eg_b], [eg_b])
            op("dve", lambda e: e.reciprocal(out=eg[:], in_=eg[:]), [eg_b], [eg_b])
            op("dve", lambda e: e.tensor_tensor(out=sgT[:, hh, :], in0=eg[:], in1=pgt[0:64, :], op=ALU.mult), [eg_b, pgt_b], [sgT_b])
        nkt = 4 * (qb + 1)
        for hh in range(NH):
            po, po_b = bank[3]
            items = []
            for jt in range(nkt):
                jl = jt - 4 * qb
                col0 = 128 * jl if jl > 0 else 0
                items.append((jt, jl, col0, bank[4 + sc_rot], pT3[pt_rot]))
                sc_rot = (sc_rot + 1) % 3
                pt_rot = (pt_rot + 1) % 3
            LA = 2
            for k in range(nkt + LA):
                if k < nkt:
                    jt, jl, col0, (sc, sc_b), (pT, pT_b) = items[k]
                    op("pe", lambda e: e.matmul(sc[:, col0:512], KT[:, hh, jt * 128:(jt + 1) * 128], QT[:, hh, col0:512],
                                                start=True, stop=True), [KT_b, QT_b], [sc_b])
                if k >= LA:
                    jt, jl, col0, (sc, sc_b), (pT, pT_b) = items[k - LA]
                    op("act", lambda e: e.activation(out=pT[:, col0:512], in_=sc[:, col0:512], func=AF.Exp), [sc_b], [pT_b])
                    if jl >= 0:
                        op("pool", lambda e: e.affine_select(out=pT[:, col0:col0 + 128], in_=pT[:, col0:col0 + 128],
                                                             pattern=[[1, 128]], compare_op=ALU.is_ge, fill=0.0, base=0,
                                                             channel_multiplier=-1), [pT_b], [pT_b])
                    op("pe", lambda e: e.matmul(po[0:65, col0:512], vaug[:, jt, hh, :], pT[:, col0:512], start=(jt == 0),
                                                stop=(jt == nkt - 1)), [vaug_b, pT_b], [po_b])
            op("act", lambda e: e.activation(out=osb[:], in_=po[0:65, :], func=AF.Copy), [po_b], [osb_b])
            op("dve", lambda e: e.reciprocal(out=rec[64:65, :], in_=osb[64:65, :]), [osb_b], [rec_b])
            pbc, pbc_b = bank[2]
            op("pe", lambda e: e.matmul(pbc[0:64, :], ones_f[64:65, :], rec[64:65, :], start=True, stop=True),
               [ones_f_b, rec_b], [pbc_b])
            op("dve", lambda e: e.tensor_tensor(out=yt32[:], in0=osb[0:64, :], in1=pbc[0:64, :], op=ALU.mult),
               [osb_b, pbc_b], [yt32_b])
            yfT, yfT_b = yfT2[yf_rot]
            yf_rot = 1 - yf_rot
            op("dve", lambda e: e.tensor_tensor(out=yfT[:], in0=yt32[:], in1=sgT[:, hh, :], op=ALU.mult),
               [yt32_b, sgT_b], [yfT_b])
            t0 = qb * 512
            sch.dma(ybuf[4 + hh][:, t0:t0 + 512], yfT[:], [yfT_b], [ybuf_b[4 + hh]])

    sch.barrier()
    P2.close()
    if stop_after <= 2:
        sch.finish()
        return nc

    for c in range(4, 8):
        gather(c)
    P3 = Scope()

    def sb3(name, shape, dt=F32):
        return P3.sb(name, shape, dt)

    Wo, Wo_b = sb3("Wo", [128, KC, D_MODEL], BF16)
    onesg, onesg_b = sb3("onesg", [128, KC])
    op("dve", lambda e: e.memset(onesg[:], 1.0), [], [onesg_b])
    gcol, gcol_b = onesg, onesg_b
    load_weight(Wo, Wo_b, w_outp, D_MODEL)
    fg, fg_b = sb3("fg", [128, D_MODEL])
    sch.dma(fg[:], pb3_d[:, :], [], [fg_b])
    bank = [P3.ps("bank3_%d" % i, [128, 512], F32) for i in range(4)]
    yT2 = [sb3("yT%d" % i, [128, KC, 512], BF16) for i in range(2)]
    yA, yA_b = sb3("yA", [128, KC, 512], BF16)
    yB, yB_b = sb3("yB", [128, KC, 512], BF16)
    sel, sel_b = sb3("sel", [128, 2])
    sch.dma(sel[:], sel_d[:, :], [], [sel_b])
    xr2 = [sb3("xr%d" % i, [128, D_MODEL]) for i in range(2)]
    z2 = [sb3("z%d" % i, [128, D_MODEL]) for i in range(2)]
    junk, junk_b = sb3("junk3", [128, D_MODEL], BF16)
    ss3, ss3_b = sb3("ss3", [128, 4])
    def load_y(blk):
        for c in range(KC):
            sch.dma(yA[:, c, :], yall[c][:, blk * 512:(blk + 1) * 512], [yall_b[c]], [yA_b])
            sch.dma(yB[:, c, :], yall[c][:, SH + blk * 512:SH + (blk + 1) * 512], [yall_b[c]], [yB_b])

    NT3 = SH // 128
    sch.dma(xr2[0][0][:], xres[0:128, :], [], [xr2[0][1]])
    load_y(0)
    for blk in range(SH // 512):
        yT, yT_b = yT2[blk % 2]
        op("pool", lambda e: e.tensor_scalar(out=yT[:], in0=yA[:], scalar1=sel[:, 0:1], scalar2=None, op0=ALU.mult),
           [yA_b, sel_b], [yT_b])
        op("dve", lambda e: e.scalar_tensor_tensor(out=yT[:], in0=yB[:], scalar=sel[:, 1:2], in1=yT[:], op0=ALU.mult,
                                                   op1=ALU.add), [yB_b, sel_b, yT_b], [yT_b])
        if blk + 1 < SH // 512:
            load_y(blk + 1)
        for ti in range(4):
            i = blk * 4 + ti
            xr, xr_b = xr2[i % 2]
            z, z_b = z2[i % 2]
            if i + 1 < NT3:
                sch.dma(xr2[(i + 1) % 2][0][:], xres[(i + 1) * 128:(i + 2) * 128, :], [], [xr2[(i + 1) % 2][1]])
            for nn in range(2):
                po, po_b = bank[(i % 2) * 2 + nn]
                for c in range(KC):
                    op("pe", lambda e: e.matmul(po[:], yT[:, c, ti * 128:(ti + 1) * 128], Wo[:, c, nn * 512:(nn + 1) * 512],
                                                start=(c == 0), stop=(c == KC - 1)), [yT_b, Wo_b], [po_b])
                op("dve", lambda e: e.tensor_tensor(out=z[:, nn * 512:(nn + 1) * 512], in0=po[:], in1=xr[:, nn * 512:(nn + 1) * 512],
                                                    op=ALU.add), [po_b, xr_b], [z_b])
            op("act", lambda e: e.activation(out=junk[:], in_=z[:], func=AF.Square, accum_out=ss3[:, 0:1]), [z_b], [junk_b, ss3_b])
            op("act", lambda e: e.activation(out=ss3[:, 1:2], in_=ss3[:, 0:1], func=AF.Ln, scale=1.0 / D_MODEL, bias=NORM_EPS),
               [ss3_b], [ss3_b])
            op("act", lambda e: e.activation(out=ss3[:, 2:3], in_=ss3[:, 1:2], func=AF.Exp, scale=-0.5), [ss3_b], [ss3_b])
            op("dve", lambda e: e.scalar_tensor_tensor(out=z[:], in0=z[:], scalar=ss3[:, 2:3], in1=fg[:], op0=ALU.mult,
                                                       op1=ALU.mult), [z_b, ss3_b, fg_b], [z_b])
            sch.dma(out[i * 128:(i + 1) * 128, :], z[:], [z_b], [out_b])
    sch.finish()
    return nc


def make_core_inputs(inputs, b, g, S):
    f = lambda a: np.ascontiguousarray(np.asarray(a, dtype=np.float32))
    w_in = np.asarray(inputs["w_in"])[0]
    SH = S // 2
    hs = slice(g * 256, (g + 1) * 256)

    def cols(base):
        return np.arange(base + g * 256, base + (g + 1) * 256)

    rw_cols = np.concatenate([cols(0), cols(512), cols(1024), np.arange(1536, 1664)])
    fx0 = 1664
    fox_cols = np.concatenate([cols(fx0), cols(fx0 + 512), cols(fx0 + 1024), np.arange(fx0 + 1536 + 4 * g, fx0 + 1536 + 4 * g + 4)])
    g0 = 1664 + 1544
    mu = np.asarray(inputs["rw_mu"])[0][rw_cols]
    rep = lambda v: np.broadcast_to(np.asarray(v, np.float32).reshape(1, -1), (128, np.asarray(v).size))
    pb1 = np.concatenate([rep(mu), rep(np.asarray(inputs["rw_w0"])[0][hs]), rep(np.asarray(inputs["rw_a0"])[0][hs]),
                          rep(np.asarray(inputs["rw_k_k"])[0][hs]), rep(np.asarray(inputs["rw_k_a"])[0][hs]),
                          rep(np.asarray(inputs["rw_r_k"])[0][4 * g:4 * g + 4].reshape(-1)),
                          rep(np.asarray(inputs["rw_gn_g"])[0][hs]), rep(np.asarray(inputs["rw_gn_b"])[0][hs])], axis=1)
    pb2 = np.concatenate([rep(np.asarray(inputs["fox_q_g"])[0]), rep(np.asarray(inputs["fox_k_g"])[0]),
                          rep(np.asarray(inputs["fox_b_f"])[0][4 * g:4 * g + 4])], axis=1)
    w_out = np.asarray(inputs["w_out"])[0]
    rows = []
    for c in range(8):
        for gg in range(2):
            base = (4 * gg + c) * 64 if c < 4 else 512 + (4 * gg + c - 4) * 64
            rows.append(np.arange(base, base + 64))
    w_outp = w_out[np.concatenate(rows)]
    xb = np.asarray(inputs["x"])[b]
    return {
        "x": f(xb),
        "xres": f(xb[g * SH:(g + 1) * SH]),
        "w_rw": f(w_in[:, rw_cols]),
        "w_fox": f(w_in[:, fox_cols]),
        "w_grw": f(w_in[:, cols(g0)]),
        "w_gfx": f(w_in[:, cols(g0 + 512)]),
        "w_outp": f(w_outp),
        "g_col": f(np.asarray(inputs["norm_g"])[0].reshape(KC, 128).T),
        "pb1": f(pb1),
        "pb2": f(pb2),
        "pb3": f(rep(np.asarray(inputs["final_g"]))),
        "w_up": f(np.asarray(inputs["rw_w_up"])[0][:, hs]),
        "a_up": f(np.asarray(inputs["rw_a_up"])[0][:, hs]),
        "sel": f(np.tile(np.array([[1.0 - g, float(g)]], np.float32), (128, 1))),
    }


def kernel(**inputs):
    x = np.asarray(inputs["x"])
    B, S, D = x.shape
    nc = build_nc(S)
    in_maps = [make_core_inputs(inputs, c // 2, c % 2, S) for c in range(2 * B)]
    res = run_bass_kernel_spmd(nc, in_maps, core_ids=list(range(2 * B)))
    out = np.empty((B, S, D), np.float32)
    SH = S // 2
    for c in range(2 * B):
        out[c // 2, (c % 2) * SH:(c % 2 + 1) * SH] = res.results[c]["out"]
    return out
```

```python
import numpy as np
import concourse.bass as bass
import concourse.mybir as mybir
from concourse.bass_utils import run_bass_kernel_spmd

F32 = mybir.dt.float32
BF16 = mybir.dt.bfloat16
AF = mybir.ActivationFunctionType
ALU = mybir.AluOpType
AX = mybir.AxisListType

D_MODEL = 1024
KC = 8
HD = 64
NH = 4
RWC = 896
FXC = 772
NORM_EPS = 1e-6
GN_EPS = 64e-5
N_DCH = 6


class Buf:
    __slots__ = ("name", "lw", "rd", "psum")

    def __init__(self, name, psum=False):
        self.name = name
        self.lw = None
        self.rd = {}
        self.psum = psum


class Sched:
    def __init__(self, nc):
        self.nc = nc
        self.eng = {"pe": nc.tensor, "act": nc.scalar, "dve": nc.vector, "pool": nc.gpsimd, "sp": nc.sync}
        self.sem = {}
        self.cnt = {}
        for k in self.eng:
            self.sem[k] = nc.semaphore("sem_" + k).__enter__()
            self.cnt[k] = 0
        for c in range(N_DCH):
            k = "d%d" % c
            self.sem[k] = nc.semaphore("sem_" + k).__enter__()
            self.cnt[k] = 0
        self.sem["cc"] = nc.semaphore("sem_cc").__enter__()
        self.cnt["cc"] = 0
        self.waited = {k: {} for k in self.eng}
        self.next_dch = 0
        self.limit = None
        self.total = 0
        self.log = []

    def _lim(self):
        self.total += 1
        if self.log is not None:
            import sys as _s
            fr = _s._getframe(2)
            self.log.append((self.total, fr.f_lineno))
        return self.limit is not None and self.total > self.limit

    def _val(self, k, i):
        return i * 16 if (k[0] == "d" and k != "dve") else i

    def _deps(self, reads, writes):
        deps = {}

        def need(k, i):
            if deps.get(k, 0) < i:
                deps[k] = i

        for b in reads:
            if b.lw is not None:
                need(*b.lw)
            if b.psum:
                for k, i in b.rd.items():
                    need(k, i)
        for b in writes:
            if b.lw is not None:
                need(*b.lw)
            for k, i in b.rd.items():
                need(k, i)
        return deps

    def _wait(self, e, deps):
        w = self.waited[e]
        for k, i in deps.items():
            if w.get(k, 0) < i:
                self.eng[e].wait_ge(self.sem[k], self._val(k, i))
                w[k] = i

    def _commit(self, me, reads, writes):
        k, i = me
        for b in reads:
            b.rd[k] = i
        for b in writes:
            b.lw = me
            b.rd = {}

    def op(self, e, fn, reads=(), writes=()):
        if self._lim():
            return
        deps = self._deps(reads, writes)
        if e == "pe":
            deps.pop("pe", None)
        self._wait(e, deps)
        ins = fn(self.eng[e])
        self.cnt[e] += 1
        ins.then_inc(self.sem[e], 1)
        self._commit((e, self.cnt[e]), reads, writes)

    def dma(self, out, in_, reads=(), writes=(), q="sp"):
        if self._lim():
            return
        c = "d%d" % self.next_dch
        self.next_dch = (self.next_dch + 1) % N_DCH
        deps = self._deps(reads, writes)
        if self.cnt[c] > 0:
            deps[c] = max(deps.get(c, 0), self.cnt[c])
        self._wait(q, deps)
        ins = self.eng[q].dma_start(out=out, in_=in_)
        self.cnt[c] += 1
        ins.then_inc(self.sem[c], 16)
        self._commit((c, self.cnt[c]), reads, writes)

    def collective(self, fn, reads=(), writes=()):
        if self._lim():
            return
        deps = self._deps(reads, writes)
        self._wait("pool", deps)
        ins = fn(self.eng["pool"])
        self.cnt["cc"] += 1
        ins.then_inc(self.sem["cc"], 1)
        self._commit(("cc", self.cnt["cc"]), reads, writes)

    def barrier(self):
        for e in self.eng:
            deps = {k: c for k, c in self.cnt.items() if c > 0 and k != e}
            self._wait(e, deps)

    def finish(self):
        deps = {k: c for k, c in self.cnt.items() if c > 0 and k != "sp"}
        self._wait("sp", deps)


def build_nc(S, debug=False, stop_after=9, limit=None):
    NT = S // 128
    NB = S // 512
    SH = S // 2
    nc = bass.Bass("TRN2", target_bir_lowering=False)

    def din(name, shape, dt=F32):
        return nc.dram_tensor(name, list(shape), dt, kind="ExternalInput").ap()

    x = din("x", [S, D_MODEL])
    xres = din("xres", [SH, D_MODEL])
    w_rw = din("w_rw", [D_MODEL, RWC])
    w_fox = din("w_fox", [D_MODEL, FXC])
    w_grw = din("w_grw", [D_MODEL, 256])
    w_gfx = din("w_gfx", [D_MODEL, 256])
    w_outp = din("w_outp", [D_MODEL, D_MODEL])
    g_col = din("g_col", [128, KC])
    pb1_d = din("pb1", [128, 2688])
    pb2_d = din("pb2", [128, 132])
    pb3_d = din("pb3", [128, D_MODEL])
    w_up_d = din("w_up", [64, 256])
    a_up_d = din("a_up", [64, 256])
    out = nc.dram_tensor("out", [SH, D_MODEL], F32, kind="ExternalOutput").ap()
    dbg_kind = "ExternalOutput" if debug else "Internal"
    ybuf = [nc.dram_tensor("ybuf%d" % c, [64, S], BF16, kind=dbg_kind).ap() for c in range(8)]
    yall = [nc.dram_tensor("yall%d" % c, [128, S], BF16, kind=dbg_kind).ap() for c in range(8)]
    sel_d = din("sel", [128, 2])
    ybuf_b = [Buf("ybuf%d" % c) for c in range(8)]
    yall_b = [Buf("yall%d" % c) for c in range(8)]
    out_b = Buf("out")

    sch = Sched(nc)
    sch.limit = limit
    op = sch.op

    class Scope:
        def __init__(self):
            self.stack = []

        def sb(self, name, shape, dt=F32):
            cm = nc.sbuf_tensor("s_" + name, list(shape), dt)
            t = cm.__enter__()
            self.stack.append(cm)
            return t, Buf(name)

        def ps(self, name, shape, dt=F32):
            cm = nc.psum_tensor("p_" + name, list(shape), dt)
            t = cm.__enter__()
            self.stack.append(cm)
            return t, Buf(name, psum=True)

        def close(self):
            while self.stack:
                self.stack.pop().__exit__(None, None, None)

    G = Scope()
    ident_bf, ident_bf_b = G.sb("ident_bf", [128, 128], BF16)
    ident_f, ident_f_b = G.sb("ident_f", [128, 128], F32)
    gcol, gcol_b = G.sb("gcol", [128, KC], F32)
    stage, stage_b = G.sb("stage", [128, 1024], F32)
    stage2, stage2_b = G.sb("stage2", [128, 1024], F32)

    def mk_ident(t, b):
        op("pool", lambda e: e.memset(t[:], 0.0), [], [b])
        op("pool", lambda e: e.affine_select(out=t[:], in_=t[:], pattern=[[-1, 128]], compare_op=ALU.not_equal,
                                             fill=1.0, base=0, channel_multiplier=1), [b], [b])

    mk_ident(ident_bf, ident_bf_b)
    mk_ident(ident_f, ident_f_b)
    sch.dma(gcol[:], g_col[:, :], [], [gcol_b])

    def load_weight(dst, dst_b, src, ncols, post=None):
        for c in range(KC):
            st, stb = (stage, stage_b) if c % 2 == 0 else (stage2, stage2_b)
            sch.dma(st[:, 0:ncols], src[c * 128:(c + 1) * 128, :], [], [stb])
            if post is None:
                op("dve", lambda e: e.tensor_scalar(out=dst[:, c, :], in0=st[:, 0:ncols], scalar1=gcol[:, c:c + 1],
                                                    scalar2=None, op0=ALU.mult), [stb, gcol_b], [dst_b])
            else:
                for (d2, d2b, pt, ptb) in post:
                    op("dve", lambda e: e.scalar_tensor_tensor(out=d2[:, c, :], in0=st[:, 0:ncols],
                                                               scalar=gcol[:, c:c + 1], in1=pt, op0=ALU.mult,
                                                               op1=ALU.mult), [stb, gcol_b, ptb], [d2b])

    def rmsnorm_to_hT(xt, xt_b, h, h_b, junk, junk_b, ss, ss_b, tpb, tpb_b, dst_ap_fn, dst_b):
        op("act", lambda e: e.activation(out=junk[:], in_=xt[:], func=AF.Square, accum_out=ss[:, 0:1]),
           [xt_b], [junk_b, ss_b])
        op("act", lambda e: e.activation(out=ss[:, 1:2], in_=ss[:, 0:1], func=AF.Ln, scale=1.0 / D_MODEL,
                                         bias=NORM_EPS), [ss_b], [ss_b])
        op("act", lambda e: e.activation(out=ss[:, 2:3], in_=ss[:, 1:2], func=AF.Exp, scale=-0.5), [ss_b], [ss_b])
        op("act", lambda e: e.activation(out=h[:], in_=xt[:], func=AF.Copy, scale=ss[:, 2:3]), [xt_b, ss_b], [h_b])
        tpv = tpb[:].rearrange("p (c t) -> p c t", c=KC)
        for c in range(KC):
            op("pe", lambda e: e.transpose(out=tpv[:, c, :], in_=h[:, c * 128:(c + 1) * 128], identity=ident_bf[:]),
               [h_b, ident_bf_b], [tpb_b])
        op("dve", lambda e: e.tensor_copy(out=dst_ap_fn(), in_=tpv), [tpb_b], [dst_b])

    P1 = Scope()
    Wa, Wa_b = P1.sb("Wa", [128, KC, RWC], BF16)
    Wb, Wb_b = P1.sb("Wb", [128, KC, RWC], BF16)
    Wg, Wg_b = P1.sb("Wg1", [128, KC, 256], BF16)
    pb1, pb1_b = P1.sb("pb1", [128, 2688], F32)
    omm, omm_b = P1.sb("omm", [128, RWC], F32)
    wup, wup_b = P1.sb("wup", [64, 256], F32)
    aup, aup_b = P1.sb("aup", [64, 256], F32)
    sch.dma(pb1[:], pb1_d[:, :], [], [pb1_b])
    sch.dma(wup[:], w_up_d[:, :], [], [wup_b])
    sch.dma(aup[:], a_up_d[:, :], [], [aup_b])
    MU = pb1[:, 0:896]
    W0 = pb1[:, 896:1152]
    A0 = pb1[:, 1152:1408]
    KK_ = pb1[:, 1408:1664]
    KA_ = pb1[:, 1664:1920]
    RK_ = pb1[:, 1920:2176]
    GNG = pb1[:, 2176:2432]
    GNB = pb1[:, 2432:2688]
    op("dve", lambda e: e.tensor_scalar(out=omm[:], in0=MU, scalar1=-1.0, scalar2=1.0, op0=ALU.mult, op1=ALU.add),
       [pb1_b], [omm_b])
    load_weight(None, None, w_rw, RWC, post=[(Wa, Wa_b, omm[:], omm_b), (Wb, Wb_b, MU, pb1_b)])
    load_weight(Wg, Wg_b, w_grw, 256)

    tri_i, tri_i_b = P1.sb("tri_i", [128, 128], F32)
    tri_s, tri_s_b = P1.sb("tri_s", [128, 128], F32)
    tri_r, tri_r_b = P1.sb("tri_r", [128, 128], F32)
    mask4, mask4_b = P1.sb("mask4", [128, 512], F32)
    maskr4, maskr4_b = P1.sb("maskr4", [128, 512], F32)
    ones_c, ones_c_b = P1.sb("ones_c", [128, 1], F32)

    def mk_tri(t, b, cm, pat, cmp_op):
        op("pool", lambda e: e.memset(t[:], 1.0), [], [b])
        op("pool", lambda e: e.affine_select(out=t[:], in_=t[:], pattern=[[pat, 128]], compare_op=cmp_op, fill=0.0,
                                             base=0, channel_multiplier=cm), [b], [b])
        op("pool", lambda e: e.memset(t[0:64, 64:128], 0.0), [b], [b])
        op("pool", lambda e: e.memset(t[64:128, 0:64], 0.0), [b], [b])

    mk_tri(tri_i, tri_i_b, -1, 1, ALU.is_ge)
    mk_tri(tri_s, tri_s_b, -1, 1, ALU.is_gt)
    mk_tri(tri_r, tri_r_b, 1, -1, ALU.is_gt)
    op("pool", lambda e: e.memset(ones_c[:], 1.0), [], [ones_c_b])
    for q_, (src, srcb) in enumerate([(tri_s, tri_s_b), (tri_i, tri_i_b), (tri_s, tri_s_b), (tri_i, tri_i_b)]):
        op("pool", lambda e: e.tensor_copy(out=mask4[:, q_ * 128:(q_ + 1) * 128], in_=src[:]), [srcb], [mask4_b])
        op("pool", lambda e: e.tensor_copy(out=maskr4[:, q_ * 128:(q_ + 1) * 128], in_=tri_r[:]), [tri_r_b], [maskr4_b])

    bank = [P1.ps("bank%d" % i, [128, 512], F32) for i in range(7)]
    bankT, bankT_b = P1.ps("bankT", [128, 1024], BF16)

    def sb(name, shape, dt=F32):
        return P1.sb(name, shape, dt)

    xt2 = [sb("xt%d" % i, [128, 1024]) for i in range(2)]
    h, h_b = sb("h", [128, 1024], BF16)
    junk, junk_b = sb("junk", [128, 1024], BF16)
    ss, ss_b = sb("ss", [128, 4])
    hT2 = [sb("hT%d" % i, [128, KC, 129], BF16) for i in range(2)]
    r32, r32_b = sb("r32", [128, 256])
    k32, k32_b = sb("k32", [128, 256])
    v32, v32_b = sb("v32", [128, 256])
    lt32, lt32_b = sb("lt32", [128, 128])
    lT, lT_b = sb("lT", [64, 2, 128])
    lwn, lwn_b = sb("lwn", [128, 256])
    t1, t1_b = sb("t1", [128, 256])
    t2, t2_b = sb("t2", [128, 256])
    a_, a_b = sb("a_", [128, 256])
    kk, kk_b = sb("kk", [128, 256])
    km, km_b = sb("km", [128, 256])
    ba, ba_b = sb("ba", [128, 256])
    s4, s4_b = sb("s4", [128, 16])
    ecum, ecum_b = sb("ecum", [128, 256])
    encum, encum_b = sb("encum", [128, 256])
    eprev, eprev_b = sb("eprev", [128, 256])
    erc, erc_b = sb("erc", [128, 256])
    wc, wc_b = sb("wc", [64, 8])
    tl4 = sb("tl4", [128, 4, 256], BF16)
    bh, bh_b = sb("bh", [128, 256], BF16)
    kh, kh_b = sb("kh", [128, 256], BF16)
    vb, vb_b = sb("vb", [128, 256], BF16)
    bhm, bhm_b = sb("bhm", [128, 2, 256], BF16)
    khm, khm_b = sb("khm", [128, 2, 256], BF16)
    cmask, cmask_b = sb("cmask", [128, 2])
    op("pool", lambda e: e.memset(bhm[:], 0.0), [], [bhm_b])
    op("pool", lambda e: e.memset(khm[:], 0.0), [], [khm_b])
    op("pool", lambda e: e.memset(cmask[:], 0.0), [], [cmask_b])
    op("pool", lambda e: e.memset(cmask[0:64, 0:1], 1.0), [cmask_b], [cmask_b])
    op("pool", lambda e: e.memset(cmask[64:128, 1:2], 1.0), [cmask_b], [cmask_b])
    TT, TT_b = sb("TT", [64, 16, 128], BF16)
    A_sb = [sb("A%d" % hh, [128, 512], BF16) for hh in range(NH)]
    Ni2 = [sb("Ni%d" % i, [128, 512], BF16) for i in range(2)]
    NiT2 = [sb("NiT%d" % i, [128, 512], BF16) for i in range(2)]
    X32, X32_b = sb("X32", [128, 512])
    Xb, Xb_b = sb("Xb", [128, 512], BF16)
    GTs, GTs_b = sb("GTs", [64, 1024])
    M0T, M0T_b = sb("M0T", [64, 512])
    Nst, Nst_b = sb("Nst", [64, 512])
    ST2 = [sb("ST%d" % i, [64, 256]) for i in range(2)]
    stt, stt_b = sb("stt", [64, 256])
    y32, y32_b = sb("y32", [128, 256])
    sg, sg_b = sb("sg", [128, 256])
    bon, bon_b = sb("bon", [128, 256])
    yg, yg_b = sb("yg", [128, 256], BF16)
    ygT, ygT_b = sb("ygT", [128, 2, 512], BF16)

    op("dve", lambda e: e.memset(ST2[0][0][:], 0.0), [], [ST2[0][1]])
    op("dve", lambda e: e.memset(GTs[:], 0.0), [], [GTs_b])
    op("dve", lambda e: e.memset(hT2[0][0][:, :, 0:1], 0.0), [], [hT2[0][1]])

    def v3(ap, hh=NH):
        return ap.rearrange("p (h d) -> p h d", h=hh)

    def bc(ap_small, hh=NH, d=HD):
        return ap_small.unsqueeze(2).to_broadcast([128, hh, d])

    st_cur = 0
    for i in range(NT):
        xt, xt_b = xt2[i % 2]
        hT, hT_b = hT2[i % 2]
        hTp, hTp_b = hT2[(i + 1) % 2]
        if i == 0:
            sch.dma(xt[:], x[0:128, :], [], [xt_b])
        if i + 1 < NT:
            sch.dma(xt2[(i + 1) % 2][0][:], x[(i + 1) * 128:(i + 2) * 128, :], [], [xt2[(i + 1) % 2][1]])
        rmsnorm_to_hT(xt, xt_b, h, h_b, junk, junk_b, ss, ss_b, bankT, bankT_b, lambda: hT[:, :, 1:129], hT_b)
        if i > 0:
            op("pool", lambda e: e.tensor_copy(out=hT[:, :, 0:1], in_=hTp[:, :, 128:129]), [hTp_b], [hT_b])
        pr0, pr0_b = bank[0]
        pr1, pr1_b = bank[1]
        for (pt, ptb, n0, n1) in ((pr0, pr0_b, 0, 512), (pr1, pr1_b, 512, 896)):
            for c in range(KC):
                op("pe", lambda e: e.matmul(pt[:, 0:n1 - n0], hT[:, c, 1:129], Wa[:, c, n0:n1], start=(c == 0), stop=False),
                   [hT_b, Wa_b], [ptb])
            for c in range(KC):
                op("pe", lambda e: e.matmul(pt[:, 0:n1 - n0], hT[:, c, 0:128], Wb[:, c, n0:n1], start=False, stop=(c == KC - 1)),
                   [hT_b, Wb_b], [ptb])
        pg, pg_b = bank[2]
        for c in range(KC):
            op("pe", lambda e: e.matmul(pg[:, 0:256], hT[:, c, 1:129], Wg[:, c, :], start=(c == 0), stop=(c == KC - 1)),
               [hT_b, Wg_b], [pg_b])
        op("act", lambda e: e.activation(out=r32[:], in_=pr0[:, 0:256], func=AF.Copy), [pr0_b], [r32_b])
        op("dve", lambda e: e.tensor_copy(out=k32[:], in_=pr0[:, 256:512]), [pr0_b], [k32_b])
        op("act", lambda e: e.activation(out=v32[:], in_=pr1[:, 0:256], func=AF.Copy), [pr1_b], [v32_b])
        op("act", lambda e: e.activation(out=lt32[:, 0:64], in_=pr1[:, 256:320], func=AF.Exp, scale=-2.0), [pr1_b], [lt32_b])
        op("act", lambda e: e.activation(out=lt32[:, 0:64], in_=lt32[:, 0:64], func=AF.Ln, bias=1.0), [lt32_b], [lt32_b])
        op("act", lambda e: e.activation(out=lt32[:, 0:64], in_=lt32[:, 0:64], func=AF.Exp, scale=-1.0), [lt32_b], [lt32_b])
        op("dve", lambda e: e.tensor_scalar(out=lt32[:, 0:64], in0=lt32[:, 0:64], scalar1=2.0, scalar2=-1.0, op0=ALU.mult,
                                            op1=ALU.add), [lt32_b], [lt32_b])
        op("dve", lambda e: e.tensor_copy(out=lt32[:, 64:128], in_=pr1[:, 320:384]), [pr1_b, lt32_b], [lt32_b])
        ptl, ptl_b = bank[3]
        for q_ in range(2):
            op("pe", lambda e: e.transpose(out=ptl[0:64, q_ * 128:(q_ + 1) * 128], in_=lt32[:, q_ * 64:(q_ + 1) * 64],
                                           identity=ident_f[:]), [lt32_b, ident_f_b], [ptl_b])
        op("dve", lambda e: e.tensor_copy(out=lT[:].rearrange("p a t -> p (a t)"), in_=ptl[0:64, 0:256]), [ptl_b], [lT_b])
        pl, pl_b = bank[4]
        op("pe", lambda e: e.matmul(pl[:, 0:256], lT[:, 0, :], wup[:], start=True, stop=True), [lT_b, wup_b], [pl_b])
        op("pe", lambda e: e.matmul(pl[:, 256:512], lT[:, 1, :], aup[:], start=True, stop=True), [lT_b, aup_b], [pl_b])
        op("dve", lambda e: e.tensor_tensor(out=t1[:], in0=pl[:, 0:256], in1=W0, op=ALU.add), [pl_b, pb1_b], [t1_b])
        op("act", lambda e: e.activation(out=t1[:], in_=t1[:], func=AF.Exp, scale=-1.0), [t1_b], [t1_b])
        op("act", lambda e: e.activation(out=t1[:], in_=t1[:], func=AF.Ln, bias=1.0), [t1_b], [t1_b])
        op("act", lambda e: e.activation(out=lwn[:], in_=t1[:], func=AF.Exp, scale=-1.0, bias=-0.5), [t1_b], [lwn_b])
        op("dve", lambda e: e.tensor_tensor(out=t2[:], in0=pl[:, 256:512], in1=A0, op=ALU.add), [pl_b, pb1_b], [t2_b])
        op("act", lambda e: e.activation(out=t2[:], in_=t2[:], func=AF.Exp, scale=-1.0), [t2_b], [t2_b])
        op("act", lambda e: e.activation(out=t2[:], in_=t2[:], func=AF.Ln, bias=1.0), [t2_b], [t2_b])
        op("act", lambda e: e.activation(out=a_[:], in_=t2[:], func=AF.Exp, scale=-1.0), [t2_b], [a_b])
        op("dve", lambda e: e.tensor_tensor(out=kk[:], in0=k32[:], in1=KK_, op=ALU.mult), [k32_b, pb1_b], [kk_b])
        op("dve", lambda e: e.tensor_tensor(out=t1[:], in0=kk[:], in1=kk[:], op=ALU.mult), [kk_b], [t1_b])
        op("dve", lambda e: e.tensor_reduce(out=s4[:, 0:4], in_=v3(t1[:]), axis=AX.X, op=ALU.add), [t1_b], [s4_b])
        op("dve", lambda e: e.tensor_scalar(out=s4[:, 0:4], in0=s4[:, 0:4], scalar1=1e-24, scalar2=None, op0=ALU.max),
           [s4_b], [s4_b])
        op("act", lambda e: e.activation(out=s4[:, 0:4], in_=s4[:, 0:4], func=AF.Ln), [s4_b], [s4_b])
        op("act", lambda e: e.activation(out=s4[:, 0:4], in_=s4[:, 0:4], func=AF.Exp, scale=-0.5), [s4_b], [s4_b])
        op("dve", lambda e: e.tensor_tensor(out=v3(kk[:]), in0=v3(kk[:]), in1=bc(s4[:, 0:4]), op=ALU.mult),
           [kk_b, s4_b], [kk_b])
        op("dve", lambda e: e.scalar_tensor_tensor(out=t1[:], in0=a_[:], scalar=-1.0, in1=KA_, op0=ALU.add, op1=ALU.mult),
           [a_b, pb1_b], [t1_b])
        op("dve", lambda e: e.scalar_tensor_tensor(out=km[:], in0=t1[:], scalar=1.0, in1=k32[:], op0=ALU.add, op1=ALU.mult),
           [t1_b, k32_b], [km_b])
        op("dve", lambda e: e.tensor_tensor(out=ba[:], in0=kk[:], in1=a_[:], op=ALU.mult), [kk_b, a_b], [ba_b])
        op("dve", lambda e: e.tensor_tensor(out=t1[:], in0=r32[:], in1=km[:], op=ALU.mult), [r32_b, km_b], [t1_b])
        op("dve", lambda e: e.tensor_tensor(out=t1[:], in0=t1[:], in1=RK_, op=ALU.mult), [t1_b, pb1_b], [t1_b])
        op("dve", lambda e: e.tensor_reduce(out=s4[:, 4:8], in_=v3(t1[:]), axis=AX.X, op=ALU.add), [t1_b], [s4_b])
        op("dve", lambda e: e.tensor_tensor(out=v3(bon[:]), in0=v3(v32[:]), in1=bc(s4[:, 4:8]), op=ALU.mult),
           [v32_b, s4_b], [bon_b])
        op("act", lambda e: e.activation(out=t2[:], in_=pg[:, 0:256], func=AF.Exp, scale=-1.0), [pg_b], [t2_b])
        op("act", lambda e: e.activation(out=t2[:], in_=t2[:], func=AF.Ln, bias=1.0), [t2_b], [t2_b])
        op("act", lambda e: e.activation(out=t2[:], in_=t2[:], func=AF.Exp, scale=-1.0), [t2_b], [t2_b])
        op("dve", lambda e: e.tensor_tensor(out=sg[:], in0=t2[:], in1=pg[:, 0:256], op=ALU.mult), [t2_b, pg_b], [sg_b])
        pc0, pc0_b = bank[5]
        pc1, pc1_b = bank[6]
        op("pe", lambda e: e.matmul(pc0[:, 0:256], tri_i[:], lwn[:], start=True, stop=True), [tri_i_b, lwn_b], [pc0_b])
        op("pe", lambda e: e.matmul(pc0[:, 256:512], tri_s[:], lwn[:], start=True, stop=True), [tri_s_b, lwn_b], [pc0_b])
        op("pe", lambda e: e.matmul(pc1[:, 0:256], tri_r[:], lwn[:], start=True, stop=True), [tri_r_b, lwn_b], [pc1_b])
        for cch in range(2):
            for hh in range(NH):
                col = 256 + cch * 4 + hh
                op("pe", lambda e: e.matmul(pc1[0:64, col:col + 1], lwn[cch * 64:(cch + 1) * 64, hh * 64:(hh + 1) * 64],
                                            ones_c[cch * 64:(cch + 1) * 64, :], start=True, stop=True),
                   [lwn_b, ones_c_b], [pc1_b])
        op("act", lambda e: e.activation(out=ecum[:], in_=pc0[:, 0:256], func=AF.Exp, scale=-1.0), [pc0_b], [ecum_b])
        op("act", lambda e: e.activation(out=encum[:], in_=pc0[:, 0:256], func=AF.Exp), [pc0_b], [encum_b])
        op("act", lambda e: e.activation(out=eprev[:], in_=pc0[:, 256:512], func=AF.Exp, scale=-1.0), [pc0_b], [eprev_b])
        op("act", lambda e: e.activation(out=erc[:], in_=pc1[:, 0:256], func=AF.Exp, scale=-1.0), [pc1_b], [erc_b])
        op("act", lambda e: e.activation(out=wc[:], in_=pc1[0:64, 256:264], func=AF.Exp, scale=-1.0), [pc1_b], [wc_b])
        tl, tl_b = tl4
        op("dve", lambda e: e.scalar_tensor_tensor(out=tl[:, 0, :], in0=kk[:], scalar=-1.0, in1=eprev[:],
                                                   op0=ALU.mult, op1=ALU.mult), [kk_b, eprev_b], [tl_b])
        op("dve", lambda e: e.tensor_tensor(out=tl[:, 1, :], in0=r32[:], in1=ecum[:], op=ALU.mult), [r32_b, ecum_b], [tl_b])
        op("dve", lambda e: e.tensor_tensor(out=tl[:, 2, :], in0=ba[:], in1=encum[:], op=ALU.mult), [ba_b, encum_b], [tl_b])
        op("dve", lambda e: e.tensor_tensor(out=tl[:, 3, :], in0=km[:], in1=encum[:], op=ALU.mult), [km_b, encum_b], [tl_b])
        for cch in range(2):
            rs = slice(cch * 64, (cch + 1) * 64)
            op("pool", lambda e: e.tensor_tensor(out=bhm[rs, cch, :], in0=ba[rs, :], in1=erc[rs, :], op=ALU.mult),
               [ba_b, erc_b], [bhm_b])
            op("pool", lambda e: e.tensor_tensor(out=khm[rs, cch, :], in0=km[rs, :], in1=erc[rs, :], op=ALU.mult),
               [km_b, erc_b], [khm_b])
        op("pool", lambda e: e.tensor_copy(out=vb[:], in_=v32[:]), [v32_b], [vb_b])
        TTp = bankT[0:64, :].rearrange("p (a t) -> p a t", a=8)
        for rnd in range(2):
            for tq in range(2):
                ty = rnd * 2 + tq
                for hh in range(NH):
                    op("pe", lambda e: e.transpose(out=TTp[:, tq * 4 + hh, :], in_=tl[:, ty, hh * 64:(hh + 1) * 64],
                                                   identity=ident_bf[:]), [tl_b, ident_bf_b], [bankT_b])
            op("act", lambda e: e.activation(out=TT[:, rnd * 8:(rnd + 1) * 8, :], in_=TTp, func=AF.Copy), [bankT_b], [TT_b])

        def kT(ty, hh):
            return TT[:, ty * 4 + hh, :]

        pn, pn_b = bank[2]
        for hh in range(NH):
            pa, pa_b = bank[hh % 2]
            rhs2 = TT[:, hh:hh + 5:4, :]
            op("pe", lambda e: e.matmul(pa[:, 0:256], kT(2, hh), rhs2, start=True, stop=True), [TT_b], [pa_b])
            op("pe", lambda e: e.matmul(pa[:, 256:512], kT(3, hh), rhs2, start=True, stop=True), [TT_b], [pa_b])
            op("pe", lambda e: e.matmul(pn[:, hh * 128:(hh + 1) * 128], kT(0, hh), kT(2, hh), start=True, stop=True),
               [TT_b], [pn_b])
            A, A_b = A_sb[hh]
            op("dve", lambda e: e.tensor_tensor(out=A[:], in0=pa[:], in1=mask4[:], op=ALU.mult), [pa_b, mask4_b], [A_b])
        Ni, Ni_b = Ni2[0]
        op("dve", lambda e: e.tensor_tensor(out=Ni[:], in0=pn[:], in1=maskr4[:], op=ALU.mult), [pn_b, maskr4_b], [Ni_b])
        px, px_b = bank[4]
        for hh in range(NH):
            A, A_b = A_sb[hh]
            op("pe", lambda e: e.matmul(px[:, hh * 128:hh * 128 + 64], ident_bf[:], tl[:, 0, hh * 64:(hh + 1) * 64],
                                        start=(hh == 0), stop=False, skip_group_check=True), [tl_b, ident_bf_b], [px_b])
            op("pe", lambda e: e.matmul(px[:, hh * 128 + 64:(hh + 1) * 128], A[:, 256:384], vb[:, hh * 64:(hh + 1) * 64],
                                        start=False, stop=False, skip_group_check=True), [A_b, vb_b], [px_b])
        op("dve", lambda e: e.tensor_copy(out=Xb[:], in_=px[:]), [px_b], [Xb_b])
        for s_ in range(6):
            cur = s_ % 2
            Ni, Ni_b = Ni2[cur]
            NiT, NiT_b = NiT2[cur]
            for hh in range(NH):
                if s_ == 0:
                    lt_, ltb = A_sb[hh][0][:, 0:128], A_sb[hh][1]
                else:
                    lt_, ltb = NiT[:, hh * 128:(hh + 1) * 128], NiT_b
                op("pe", lambda e: e.matmul(px[:, hh * 128:(hh + 1) * 128], lt_, Xb[:, hh * 128:(hh + 1) * 128],
                                            start=False, stop=(s_ == 5), skip_group_check=True), [ltb, Xb_b], [px_b])
            if s_ < 5:
                ps1, ps1_b = bank[5]
                ps2, ps2_b = bank[6]
                Nn_, Nn_b = Ni2[1 - cur]
                NnT, NnT_b = NiT2[1 - cur]
                for hh in range(NH):
                    if s_ == 0:
                        lt_, ltb = A_sb[hh][0][:, 0:128], A_sb[hh][1]
                    else:
                        lt_, ltb = NiT[:, hh * 128:(hh + 1) * 128], NiT_b
                    n_ = Ni[:, hh * 128:(hh + 1) * 128]
                    op("pe", lambda e: e.matmul(ps2[:, hh * 128:(hh + 1) * 128], n_, lt_, start=True, stop=True),
                       [ltb, Ni_b], [ps2_b])
                    op("pe", lambda e: e.matmul(ps1[:, hh * 128:(hh + 1) * 128], lt_, n_, start=True, stop=True),
                       [ltb, Ni_b], [ps1_b])
            op("dve", lambda e: e.tensor_copy(out=Xb[:], in_=px[:]), [px_b], [Xb_b])
            if s_ < 5:
                op("dve", lambda e: e.tensor_copy(out=NnT[:], in_=ps2[:]), [ps2_b], [NnT_b])
                op("act", lambda e: e.activation(out=Nn_[:], in_=ps1[:], func=AF.Copy), [ps1_b], [Nn_b])
        Xb3 = Xb[:].rearrange("p (h c) -> p h c", h=NH)
        pgt, pgt_b = bank[0]
        tlr = tl[:, 1, :]
        for hh in range(NH):
            A, A_b = A_sb[hh]
            op("pe", lambda e: e.matmul(pgt[0:64, hh * 128:(hh + 1) * 128], Xb3[:, hh, 0:64], A[:, 128:256],
                                        start=True, stop=False), [Xb_b, A_b], [pgt_b])
            op("pe", lambda e: e.matmul(pgt[0:64, hh * 128:(hh + 1) * 128], tlr[:, hh * 64:(hh + 1) * 64], ident_bf[:],
                                        start=False, stop=True), [tl_b, ident_bf_b], [pgt_b])
        pgt5 = pgt[0:64, :].rearrange("p (h c t) -> p h c t", h=NH, c=2)
        GT5 = GTs[:].rearrange("p (a h c t) -> p a h c t", a=2, h=NH, c=2)
        for cch in range(2):
            op("act", lambda e: e.activation(out=GT5[:, cch, :, cch, :], in_=pgt5[:, :, cch, :], func=AF.Copy), [pgt_b], [GTs_b])
        pm, pm_b = bank[1]
        pn2, pn2_b = bank[5]
        for cch in range(2):
            for hh in range(NH):
                cs = slice((cch * 4 + hh) * 64, (cch * 4 + hh + 1) * 64)
                hs = slice(hh * 64, (hh + 1) * 64)
                op("pe", lambda e: e.matmul(pm[0:64, cs], Xb3[:, hh, 0:64], bhm[:, cch, hs], start=True, stop=True),
                   [Xb_b, bhm_b], [pm_b])
                op("pe", lambda e: e.matmul(pn2[0:64, cs], bhm[:, cch, hs], Xb3[:, hh, 64:128], start=True, stop=False),
                   [Xb_b, bhm_b], [pn2_b])
                op("pe", lambda e: e.matmul(pn2[0:64, cs], khm[:, cch, hs], vb[:, hs], start=False, stop=True),
                   [khm_b, vb_b], [pn2_b])
        op("act", lambda e: e.activation(out=M0T[:], in_=pm[0:64, :], func=AF.Copy), [pm_b], [M0T_b])
        op("dve", lambda e: e.tensor_copy(out=Nst[:], in_=pn2[0:64, :]), [pn2_b], [Nst_b])
        py, py_b = bank[3]
        for hh in range(NH):
            A, A_b = A_sb[hh]
            hs = slice(hh * 64, (hh + 1) * 64)
            op("pe", lambda e: e.matmul(py[:, hs], A[:, 128:256], Xb3[:, hh, 64:128], start=(hh == 0), stop=False,
                                        skip_group_check=True), [A_b, Xb_b], [py_b])
            op("pe", lambda e: e.matmul(py[:, hs], A[:, 384:512], vb[:, hs], start=False, stop=False, skip_group_check=True),
               [A_b, vb_b], [py_b])
        pst, pst_b = bank[6]
        for cch in range(2):
            ST, ST_b = ST2[st_cur]
            STn, STn_b = ST2[1 - st_cur]
            for hh in range(NH):
                hs = slice(hh * 64, (hh + 1) * 64)
                op("pe", lambda e: e.matmul(py[:, hs], GTs[:, cch * 512 + hh * 128: cch * 512 + (hh + 1) * 128],
                                            ST[:, hs], start=False, stop=True, skip_group_check=True), [GTs_b, ST_b], [py_b])
            for hh in range(NH):
                hs = slice(hh * 64, (hh + 1) * 64)
                cs = slice((cch * 4 + hh) * 64, (cch * 4 + hh + 1) * 64)
                op("pe", lambda e: e.matmul(pst[0:64, hs], M0T[:, cs], ST[:, hs], start=True, stop=True), [M0T_b, ST_b], [pst_b])
            op("dve", lambda e: e.tensor_tensor(out=v3(stt[:]), in0=v3(ST[:]),
                                                in1=wc[:, cch * 4:(cch + 1) * 4].unsqueeze(2).to_broadcast([64, NH, HD]),
                                                op=ALU.mult), [ST_b, wc_b], [stt_b])
            op("dve", lambda e: e.tensor_tensor(out=stt[:], in0=stt[:], in1=Nst[:, cch * 256:(cch + 1) * 256], op=ALU.add),
               [stt_b, Nst_b], [stt_b])
            op("dve", lambda e: e.tensor_tensor(out=STn[:], in0=stt[:], in1=pst[0:64, 0:256], op=ALU.add),
               [stt_b, pst_b], [STn_b])
            st_cur = 1 - st_cur
        op("act", lambda e: e.activation(out=y32[:], in_=py[:, 0:256], func=AF.Copy), [py_b], [y32_b])
        op("dve", lambda e: e.tensor_reduce(out=s4[:, 8:12], in_=v3(y32[:]), axis=AX.X, op=ALU.add), [y32_b], [s4_b])
        op("dve", lambda e: e.tensor_scalar(out=s4[:, 8:12], in0=s4[:, 8:12], scalar1=1.0 / HD, scalar2=None, op0=ALU.mult),
           [s4_b], [s4_b])
        op("dve", lambda e: e.tensor_tensor(out=v3(y32[:]), in0=v3(y32[:]), in1=bc(s4[:, 8:12]), op=ALU.subtract),
           [y32_b, s4_b], [y32_b])
        op("dve", lambda e: e.tensor_tensor(out=t1[:], in0=y32[:], in1=y32[:], op=ALU.mult), [y32_b], [t1_b])
        op("dve", lambda e: e.tensor_reduce(out=s4[:, 12:16], in_=v3(t1[:]), axis=AX.X, op=ALU.add), [t1_b], [s4_b])
        op("act", lambda e: e.activation(out=s4[:, 12:16], in_=s4[:, 12:16], func=AF.Ln, scale=1.0 / HD, bias=GN_EPS),
           [s4_b], [s4_b])
        op("act", lambda e: e.activation(out=s4[:, 12:16], in_=s4[:, 12:16], func=AF.Exp, scale=-0.5), [s4_b], [s4_b])
        op("dve", lambda e: e.tensor_tensor(out=v3(y32[:]), in0=v3(y32[:]), in1=bc(s4[:, 12:16]), op=ALU.mult),
           [y32_b, s4_b], [y32_b])
        op("dve", lambda e: e.tensor_tensor(out=y32[:], in0=y32[:], in1=GNG, op=ALU.mult), [y32_b, pb1_b], [y32_b])
        op("dve", lambda e: e.tensor_tensor(out=y32[:], in0=y32[:], in1=GNB, op=ALU.add), [y32_b, pb1_b], [y32_b])
        op("dve", lambda e: e.tensor_tensor(out=y32[:], in0=y32[:], in1=bon[:], op=ALU.add), [y32_b, bon_b], [y32_b])
        op("dve", lambda e: e.tensor_tensor(out=yg[:], in0=y32[:], in1=sg[:], op=ALU.mult), [y32_b, sg_b], [yg_b])
        ygp = bankT[:, 0:256].rearrange("p (a t) -> p a t", a=2)
        for q_ in range(2):
            op("pe", lambda e: e.transpose(out=ygp[:, q_, :], in_=yg[:, q_ * 128:(q_ + 1) * 128], identity=ident_bf[:]),
               [yg_b, ident_bf_b], [bankT_b])
        ti = i % 4
        op("act", lambda e: e.activation(out=ygT[:, :, ti * 128:(ti + 1) * 128], in_=ygp, func=AF.Copy), [bankT_b], [ygT_b])
        if ti == 3:
            t0 = (i // 4) * 512
            for hd in range(NH):
                sch.dma(ybuf[hd][:, t0:t0 + 512], ygT[(hd % 2) * 64:(hd % 2 + 1) * 64, hd // 2, :], [ygT_b], [ybuf_b[hd]])

    sch.barrier()
    P1.close()

    def gather(c):
        sch.collective(lambda e: e.collective_compute("AllGather", ALU.bypass, replica_groups=[[0, 1], [2, 3], [4, 5], [6, 7]],
                                                      ins=[ybuf[c][:, :]], outs=[yall[c][:, :]]), [ybuf_b[c]], [yall_b[c]])

    if stop_after > 2:
        for c in range(4):
            gather(c)
    if stop_after <= 1:
        sch.finish()
        return nc

    P2 = Scope()

    def sb2(name, shape, dt=F32):
        return P2.sb(name, shape, dt)

    Wf, Wf_b = sb2("Wf", [128, KC, FXC], BF16)
    Wg2, Wg2_b = sb2("Wg2", [128, KC, 256], BF16)
    pb2, pb2_b = sb2("pb2", [128, 132])
    sch.dma(pb2[:], pb2_d[:, :], [], [pb2_b])
    load_weight(Wf, Wf_b, w_fox, FXC)
    load_weight(Wg2, Wg2_b, w_gfx, 256)
    QG = pb2[:, 0:64]
    KG = pb2[:, 64:128]
    BFB = pb2[:, 128:132]
    qgs, qgs_b = sb2("qgs", [128, 64])
    op("dve", lambda e: e.tensor_scalar(out=qgs[:], in0=QG, scalar1=HD ** -0.5, scalar2=None, op0=ALU.mult), [pb2_b], [qgs_b])
    tri_f, tri_f_b = sb2("tri_f", [128, 128])
    sel127, sel127_b = sb2("sel127", [128, 128])
    ones_f, ones_f_b = sb2("ones_f", [128, 64])
    op("pool", lambda e: e.memset(tri_f[:], 1.0), [], [tri_f_b])
    op("pool", lambda e: e.affine_select(out=tri_f[:], in_=tri_f[:], pattern=[[1, 128]], compare_op=ALU.is_ge, fill=0.0,
                                         base=0, channel_multiplier=-1), [tri_f_b], [tri_f_b])
    op("pool", lambda e: e.memset(sel127[:], 1.0), [], [sel127_b])
    op("pool", lambda e: e.affine_select(out=sel127[:], in_=sel127[:], pattern=[[0, 128]], compare_op=ALU.is_ge, fill=0.0,
                                         base=-127, channel_multiplier=1), [sel127_b], [sel127_b])
    op("pool", lambda e: e.memset(ones_f[:], 1.0), [], [ones_f_b])

    KT, KT_b = sb2("KT", [70, NH, S], BF16)
    vaug, vaug_b = sb2("vaug", [128, NT, NH, 65], BF16)
    op("pool", lambda e: e.memset(vaug[:], 1.0), [], [vaug_b])
    bank = [P2.ps("bank2_%d" % i, [128, 512], F32) for i in range(7)]
    bankT, bankT_b = P2.ps("bankT2", [128, 1024], BF16)
    xt2 = [sb2("xtb%d" % i, [128, 1024]) for i in range(2)]
    h, h_b = sb2("hb", [128, 1024], BF16)
    junk, junk_b = sb2("junkb", [128, 1024], BF16)
    ss, ss_b = sb2("ssb", [128, 4])
    hTb, hTb_b = sb2("hTb", [128, KC, 512], BF16)
    qk32, qk32_b = sb2("qk32", [128, 512])
    sq, sq_b = sb2("sq", [128, 512])
    s8, s8_b = sb2("s8", [128, 8])
    cn2 = [sb2("cn%d" % i, [128, 4]) for i in range(2)]
    z4, z4_b = sb2("z4", [128, 4])
    spl, spl_b = sb2("spl", [128, 3, 4], BF16)
    rr, rr_b = sb2("rr", [128, 4])
    qaug2 = [sb2("qaug%d" % i, [128, NH, 70], BF16) for i in range(2)]
    kaug2 = [sb2("kaug%d" % i, [128, NH, 70], BF16) for i in range(2)]
    QT, QT_b = sb2("QT", [70, NH, 512], BF16)
    sgT, sgT_b = sb2("sgT", [64, NH, 512], BF16)
    eg, eg_b = sb2("eg", [64, 512])
    pT3 = [sb2("pT%d" % i, [128, 512], BF16) for i in range(3)]
    osb, osb_b = sb2("osb", [65, 512])
    rec, rec_b = sb2("rec", [65, 512])
    yt32, yt32_b = sb2("yt32", [64, 512])
    yfT2 = [sb2("yfT%d" % i, [64, 512], BF16) for i in range(2)]
    for i_ in range(2):
        op("pool", lambda e: e.memset(qaug2[i_][0][:], 1.0), [], [qaug2[i_][1]])
        op("pool", lambda e: e.memset(kaug2[i_][0][:], 1.0), [], [kaug2[i_][1]])
    op("dve", lambda e: e.memset(cn2[1][0][:], 0.0), [], [cn2[1][1]])

    pt_rot = 0
    sc_rot = 0
    yf_rot = 0
    for qb in range(NB):
        for ti in range(4):
            i = qb * 4 + ti
            xt, xt_b = xt2[i % 2]
            if i == 0:
                sch.dma(xt[:], x[0:128, :], [], [xt_b])
            if i + 1 < NT:
                sch.dma(xt2[(i + 1) % 2][0][:], x[(i + 1) * 128:(i + 2) * 128, :], [], [xt2[(i + 1) % 2][1]])
            rmsnorm_to_hT(xt, xt_b, h, h_b, junk, junk_b, ss, ss_b, bankT, bankT_b,
                          lambda: hTb[:, :, ti * 128:(ti + 1) * 128], hTb_b)
            pf0, pf0_b = bank[0]
            pf1, pf1_b = bank[1]
            for (pt, ptb, n0, n1) in ((pf0, pf0_b, 0, 512), (pf1, pf1_b, 512, FXC)):
                for c in range(KC):
                    op("pe", lambda e: e.matmul(pt[:, 0:n1 - n0], hTb[:, c, ti * 128:(ti + 1) * 128], Wf[:, c, n0:n1],
                                                start=(c == 0), stop=(c == KC - 1)), [hTb_b, Wf_b], [ptb])
            qa, qa_b = qaug2[i % 2]
            ka, ka_b = kaug2[i % 2]
            op("act", lambda e: e.activation(out=qk32[:], in_=pf0[:], func=AF.Copy), [pf0_b], [qk32_b])
            op("dve", lambda e: e.tensor_tensor(out=sq[:], in0=qk32[:], in1=qk32[:], op=ALU.mult), [qk32_b], [sq_b])
            op("dve", lambda e: e.tensor_reduce(out=s8[:], in_=v3(sq[:], 8), axis=AX.X, op=ALU.add), [sq_b], [s8_b])
            op("act", lambda e: e.activation(out=s8[:], in_=s8[:], func=AF.Ln, scale=1.0 / HD, bias=NORM_EPS), [s8_b], [s8_b])
            op("act", lambda e: e.activation(out=s8[:], in_=s8[:], func=AF.Exp, scale=-0.5), [s8_b], [s8_b])
            op("dve", lambda e: e.tensor_tensor(out=v3(qk32[:], 8), in0=v3(qk32[:], 8), in1=bc(s8[:], 8), op=ALU.mult),
               [qk32_b, s8_b], [qk32_b])
            op("dve", lambda e: e.tensor_tensor(out=qa[:, :, 0:64], in0=v3(qk32[:, 0:256]),
                                                in1=qgs[:].unsqueeze(1).to_broadcast([128, NH, HD]), op=ALU.mult),
               [qk32_b, qgs_b], [qa_b])
            op("dve", lambda e: e.tensor_tensor(out=ka[:, :, 0:64], in0=v3(qk32[:, 256:512]),
                                                in1=KG.unsqueeze(1).to_broadcast([128, NH, HD]), op=ALU.mult),
               [qk32_b, pb2_b], [ka_b])
            op("act", lambda e: e.activation(out=vaug[:, i, :, 0:64], in_=v3(pf1[:, 0:256]), func=AF.Copy), [pf1_b], [vaug_b])
            op("dve", lambda e: e.tensor_tensor(out=z4[:], in0=pf1[:, 256:260], in1=BFB, op=ALU.add), [pf1_b, pb2_b], [z4_b])
            op("act", lambda e: e.activation(out=z4[:], in_=z4[:], func=AF.Exp, scale=-1.0), [z4_b], [z4_b])
            op("act", lambda e: e.activation(out=z4[:], in_=z4[:], func=AF.Ln, bias=1.0), [z4_b], [z4_b])
            cn, cn_b = cn2[i % 2]
            cnp, cnp_b = cn2[(i + 1) % 2]
            pcs, pcs_b = bank[2]
            op("pe", lambda e: e.matmul(pcs[:, 0:4], tri_f[:], z4[:], start=True, stop=False), [tri_f_b, z4_b], [pcs_b])
            op("pe", lambda e: e.matmul(pcs[:, 0:4], sel127[:], cnp[:], start=False, stop=True), [sel127_b, cnp_b], [pcs_b])
            op("dve", lambda e: e.tensor_copy(out=cn[:], in_=pcs[:, 0:4]), [pcs_b], [cn_b])
            op("dve", lambda e: e.tensor_copy(out=spl[:, 0, :], in_=cn[:]), [cn_b], [spl_b])
            op("dve", lambda e: e.tensor_tensor(out=rr[:], in0=cn[:], in1=spl[:, 0, :], op=ALU.subtract), [cn_b, spl_b], [rr_b])
            op("dve", lambda e: e.tensor_copy(out=spl[:, 1, :], in_=rr[:]), [rr_b], [spl_b])
            op("dve", lambda e: e.tensor_tensor(out=rr[:], in0=rr[:], in1=spl[:, 1, :], op=ALU.subtract), [rr_b, spl_b], [rr_b])
            op("dve", lambda e: e.tensor_copy(out=spl[:, 2, :], in_=rr[:]), [rr_b], [spl_b])
            op("dve", lambda e: e.tensor_copy(out=ka[:, :, 67:70], in_=spl[:].rearrange("p s h -> p h s")), [spl_b], [ka_b])
            op("dve", lambda e: e.tensor_scalar(out=qa[:, :, 64:67], in0=spl[:].rearrange("p s h -> p h s"), scalar1=-1.0,
                                                scalar2=None, op0=ALU.mult), [spl_b], [qa_b])
            tqp = bankT[:].rearrange("p (a t) -> p a t", a=8)
            for hh in range(NH):
                op("pe", lambda e: e.transpose(out=tqp[0:70, hh, :], in_=qa[:, hh, :], identity=ident_bf[:]),
                   [qa_b, ident_bf_b], [bankT_b])
                op("pe", lambda e: e.transpose(out=tqp[0:70, 4 + hh, :], in_=ka[:, hh, :], identity=ident_bf[:]),
                   [ka_b, ident_bf_b], [bankT_b])
            op("dve", lambda e: e.tensor_copy(out=QT[:, :, ti * 128:(ti + 1) * 128], in_=tqp[0:70, 0:4, :]), [bankT_b], [QT_b])
            op("act", lambda e: e.activation(out=KT[:, :, i * 128:(i + 1) * 128], in_=tqp[0:70, 4:8, :], func=AF.Copy),
               [bankT_b], [KT_b])
        for hh in range(NH):
            pgt, pgt_b = bank[2]
            for c in range(KC):
                op("pe", lambda e: e.matmul(pgt[0:64, :], Wg2[:, c, hh * 64:(hh + 1) * 64], hTb[:, c, :], start=(c == 0),
                                            stop=(c == KC - 1)), [Wg2_b, hTb_b], [pgt_b])
            op("act", lambda e: e.activation(out=eg[:], in_=pgt[0:64, :], func=AF.Exp, scale=-1.0), [pgt_b], [eg_b])
            op("act", lambda e: e.activation(out=eg[:], in_=eg[:], func=AF.Ln, bias=1.0), [eg_b], [eg_b])
            op("act", lambda e: e.activation(out=eg[:], in_=eg[:], func=AF.Exp, scale=-1.0), [eg_b], [eg_b])
            op("dve", lambda e: e.tensor_tensor(out=sgT[:, hh, :], in0=eg[:], in1=pgt[0:64, :], op=ALU.mult), [eg_b, pgt_b], [sgT_b])
        nkt = 4 * (qb + 1)
        for hh in range(NH):
            po, po_b = bank[3]
            items = []
            for jt in range(nkt):
                jl = jt - 4 * qb
                col0 = 128 * jl if jl > 0 else 0
                items.append((jt, jl, col0, bank[4 + sc_rot], pT3[pt_rot]))
                sc_rot = (sc_rot + 1) % 3
                pt_rot = (pt_rot + 1) % 3
            LA = 2
            for k in range(nkt + LA):
                if k < nkt:
                    jt, jl, col0, (sc, sc_b), (pT, pT_b) = items[k]
                    op("pe", lambda e: e.matmul(sc[:, col0:512], KT[:, hh, jt * 128:(jt + 1) * 128], QT[:, hh, col0:512],
                                                start=True, stop=True), [KT_b, QT_b], [sc_b])
                if k >= LA:
                    jt, jl, col0, (sc, sc_b), (pT, pT_b) = items[k - LA]
                    op("act", lambda e: e.activation(out=pT[:, col0:512], in_=sc[:, col0:512], func=AF.Exp), [sc_b], [pT_b])
                    if jl >= 0:
                        op("pool", lambda e: e.affine_select(out=pT[:, col0:col0 + 128], in_=pT[:, col0:col0 + 128],
                                                             pattern=[[1, 128]], compare_op=ALU.is_ge, fill=0.0, base=0,
                                                             channel_multiplier=-1), [pT_b], [pT_b])
                    op("pe", lambda e: e.matmul(po[0:65, col0:512], vaug[:, jt, hh, :], pT[:, col0:512], start=(jt == 0),
                                                stop=(jt == nkt - 1)), [vaug_b, pT_b], [po_b])
            op("act", lambda e: e.activation(out=osb[:], in_=po[0:65, :], func=AF.Copy), [po_b], [osb_b])
            op("dve", lambda e: e.reciprocal(out=rec[64:65, :], in_=osb[64:65, :]), [osb_b], [rec_b])
            pbc, pbc_b = bank[2]
            op("pe", lambda e: e.matmul(pbc[0:64, :], ones_f[64:65, :], rec[64:65, :], start=True, stop=True),
               [ones_f_b, rec_b], [pbc_b])
            op("dve", lambda e: e.tensor_tensor(out=yt32[:], in0=osb[0:64, :], in1=pbc[0:64, :], op=ALU.mult),
               [osb_b, pbc_b], [yt32_b])
            yfT, yfT_b = yfT2[yf_rot]
            yf_rot = 1 - yf_rot
            op("dve", lambda e: e.tensor_tensor(out=yfT[:], in0=yt32[:], in1=sgT[:, hh, :], op=ALU.mult),
               [yt32_b, sgT_b], [yfT_b])
            t0 = qb * 512
            sch.dma(ybuf[4 + hh][:, t0:t0 + 512], yfT[:], [yfT_b], [ybuf_b[4 + hh]])

    sch.barrier()
    P2.close()
    if stop_after <= 2:
        sch.finish()
        return nc

    for c in range(4, 8):
        gather(c)
    P3 = Scope()

    def sb3(name, shape, dt=F32):
        return P3.sb(name, shape, dt)

    Wo, Wo_b = sb3("Wo", [128, KC, D_MODEL], BF16)
    onesg, onesg_b = sb3("onesg", [128, KC])
    op("dve", lambda e: e.memset(onesg[:], 1.0), [], [onesg_b])
    gcol, gcol_b = onesg, onesg_b
    load_weight(Wo, Wo_b, w_outp, D_MODEL)
    fg, fg_b = sb3("fg", [128, D_MODEL])
    sch.dma(fg[:], pb3_d[:, :], [], [fg_b])
    bank = [P3.ps("bank3_%d" % i, [128, 512], F32) for i in range(4)]
    yT2 = [sb3("yT%d" % i, [128, KC, 512], BF16) for i in range(2)]
    yA, yA_b = sb3("yA", [128, KC, 512], BF16)
    yB, yB_b = sb3("yB", [128, KC, 512], BF16)
    sel, sel_b = sb3("sel", [128, 2])
    sch.dma(sel[:], sel_d[:, :], [], [sel_b])
    xr2 = [sb3("xr%d" % i, [128, D_MODEL]) for i in range(2)]
    z2 = [sb3("z%d" % i, [128, D_MODEL]) for i in range(2)]
    junk, junk_b = sb3("junk3", [128, D_MODEL], BF16)
    ss3, ss3_b = sb3("ss3", [128, 4])
    def load_y(blk):
        for c in range(KC):
            sch.dma(yA[:, c, :], yall[c][:, blk * 512:(blk + 1) * 512], [yall_b[c]], [yA_b])
            sch.dma(yB[:, c, :], yall[c][:, SH + blk * 512:SH + (blk + 1) * 512], [yall_b[c]], [yB_b])

    NT3 = SH // 128
    sch.dma(xr2[0][0][:], xres[0:128, :], [], [xr2[0][1]])
    load_y(0)
    for blk in range(SH // 512):
        yT, yT_b = yT2[blk % 2]
        op("pool", lambda e: e.tensor_scalar(out=yT[:], in0=yA[:], scalar1=sel[:, 0:1], scalar2=None, op0=ALU.mult),
           [yA_b, sel_b], [yT_b])
        op("dve", lambda e: e.scalar_tensor_tensor(out=yT[:], in0=yB[:], scalar=sel[:, 1:2], in1=yT[:], op0=ALU.mult,
                                                   op1=ALU.add), [yB_b, sel_b, yT_b], [yT_b])
        if blk + 1 < SH // 512:
            load_y(blk + 1)
        for ti in range(4):
            i = blk * 4 + ti
            xr, xr_b = xr2[i % 2]
            z, z_b = z2[i % 2]
            if i + 1 < NT3:
                sch.dma(xr2[(i + 1) % 2][0][:], xres[(i + 1) * 128:(i + 2) * 128, :], [], [xr2[(i + 1) % 2][1]])
            for nn in range(2):
                po, po_b = bank[(i % 2) * 2 + nn]
                for c in range(KC):
                    op("pe", lambda e: e.matmul(po[:], yT[:, c, ti * 128:(ti + 1) * 128], Wo[:, c, nn * 512:(nn + 1) * 512],
                                                start=(c == 0), stop=(c == KC - 1)), [yT_b, Wo_b], [po_b])
                op("dve", lambda e: e.tensor_tensor(out=z[:, nn * 512:(nn + 1) * 512], in0=po[:], in1=xr[:, nn * 512:(nn + 1) * 512],
                                                    op=ALU.add), [po_b, xr_b], [z_b])
            op("act", lambda e: e.activation(out=junk[:], in_=z[:], func=AF.Square, accum_out=ss3[:, 0:1]), [z_b], [junk_b, ss3_b])
            op("act", lambda e: e.activation(out=ss3[:, 1:2], in_=ss3[:, 0:1], func=AF.Ln, scale=1.0 / D_MODEL, bias=NORM_EPS),
               [ss3_b], [ss3_b])
            op("act", lambda e: e.activation(out=ss3[:, 2:3], in_=ss3[:, 1:2], func=AF.Exp, scale=-0.5), [ss3_b], [ss3_b])
            op("dve", lambda e: e.scalar_tensor_tensor(out=z[:], in0=z[:], scalar=ss3[:, 2:3], in1=fg[:], op0=ALU.mult,
                                                       op1=ALU.mult), [z_b, ss3_b, fg_b], [z_b])
            sch.dma(out[i * 128:(i + 1) * 128, :], z[:], [z_b], [out_b])
    sch.finish()
    return nc


def make_core_inputs(inputs, b, g, S):
    f = lambda a: np.ascontiguousarray(np.asarray(a, dtype=np.float32))
    w_in = np.asarray(inputs["w_in"])[0]
    SH = S // 2
    hs = slice(g * 256, (g + 1) * 256)

    def cols(base):
        return np.arange(base + g * 256, base + (g + 1) * 256)

    rw_cols = np.concatenate([cols(0), cols(512), cols(1024), np.arange(1536, 1664)])
    fx0 = 1664
    fox_cols = np.concatenate([cols(fx0), cols(fx0 + 512), cols(fx0 + 1024), np.arange(fx0 + 1536 + 4 * g, fx0 + 1536 + 4 * g + 4)])
    g0 = 1664 + 1544
    mu = np.asarray(inputs["rw_mu"])[0][rw_cols]
    rep = lambda v: np.broadcast_to(np.asarray(v, np.float32).reshape(1, -1), (128, np.asarray(v).size))
    pb1 = np.concatenate([rep(mu), rep(np.asarray(inputs["rw_w0"])[0][hs]), rep(np.asarray(inputs["rw_a0"])[0][hs]),
                          rep(np.asarray(inputs["rw_k_k"])[0][hs]), rep(np.asarray(inputs["rw_k_a"])[0][hs]),
                          rep(np.asarray(inputs["rw_r_k"])[0][4 * g:4 * g + 4].reshape(-1)),
                          rep(np.asarray(inputs["rw_gn_g"])[0][hs]), rep(np.asarray(inputs["rw_gn_b"])[0][hs])], axis=1)
    pb2 = np.concatenate([rep(np.asarray(inputs["fox_q_g"])[0]), rep(np.asarray(inputs["fox_k_g"])[0]),
                          rep(np.asarray(inputs["fox_b_f"])[0][4 * g:4 * g + 4])], axis=1)
    w_out = np.asarray(inputs["w_out"])[0]
    rows = []
    for c in range(8):
        for gg in range(2):
            base = (4 * gg + c) * 64 if c < 4 else 512 + (4 * gg + c - 4) * 64
            rows.append(np.arange(base, base + 64))
    w_outp = w_out[np.concatenate(rows)]
    xb = np.asarray(inputs["x"])[b]
    return {
        "x": f(xb),
        "xres": f(xb[g * SH:(g + 1) * SH]),
        "w_rw": f(w_in[:, rw_cols]),
        "w_fox": f(w_in[:, fox_cols]),
        "w_grw": f(w_in[:, cols(g0)]),
        "w_gfx": f(w_in[:, cols(g0 + 512)]),
        "w_outp": f(w_outp),
        "g_col": f(np.asarray(inputs["norm_g"])[0].reshape(KC, 128).T),
        "pb1": f(pb1),
        "pb2": f(pb2),
        "pb3": f(rep(np.asarray(inputs["final_g"]))),
        "w_up": f(np.asarray(inputs["rw_w_up"])[0][:, hs]),
        "a_up": f(np.asarray(inputs["rw_a_up"])[0][:, hs]),
        "sel": f(np.tile(np.array([[1.0 - g, float(g)]], np.float32), (128, 1))),
    }


def kernel(**inputs):
    x = np.asarray(inputs["x"])
    B, S, D = x.shape
    nc = build_nc(S)
    in_maps = [make_core_inputs(inputs, c // 2, c % 2, S) for c in range(2 * B)]
    res = run_bass_kernel_spmd(nc, in_maps, core_ids=list(range(2 * B)))
    out = np.empty((B, S, D), np.float32)
    SH = S // 2
    for c in range(2 * B):
        out[c // 2, (c % 2) * SH:(c % 2 + 1) * SH] = res.results[c]["out"]
    return out
```

```python
import numpy as np
import concourse.bass as bass
import concourse.mybir as mybir
from concourse.bass_utils import run_bass_kernel_spmd

F32 = mybir.dt.float32
BF16 = mybir.dt.bfloat16
AF = mybir.ActivationFunctionType
ALU = mybir.AluOpType
AX = mybir.AxisListType

D_MODEL = 1024
KC = 8
HD = 64
NH = 4
RWC = 896
FXC = 772
NORM_EPS = 1e-6
GN_EPS = 64e-5
N_DCH = 6


class Buf:
    __slots__ = ("name", "lw", "rd", "psum")

    def __init__(self, name, psum=False):
        self.name = name
        self.lw = None
        self.rd = {}
        self.psum = psum


class Sched:
    def __init__(self, nc):
        self.nc = nc
        self.eng = {"pe": nc.tensor, "act": nc.scalar, "dve": nc.vector, "pool": nc.gpsimd, "sp": nc.sync}
        self.sem = {}
        self.cnt = {}
        for k in self.eng:
            self.sem[k] = nc.semaphore("sem_" + k).__enter__()
            self.cnt[k] = 0
        for c in range(N_DCH):
            k = "d%d" % c
            self.sem[k] = nc.semaphore("sem_" + k).__enter__()
            self.cnt[k] = 0
        self.sem["cc"] = nc.semaphore("sem_cc").__enter__()
        self.cnt["cc"] = 0
        self.waited = {k: {} for k in self.eng}
        self.next_dch = 0
        self.limit = None
        self.total = 0
        self.log = []

    def _lim(self):
        self.total += 1
        if self.log is not None:
            import sys as _s
            fr = _s._getframe(2)
            self.log.append((self.total, fr.f_lineno))
        return self.limit is not None and self.total > self.limit

    def _val(self, k, i):
        return i * 16 if (k[0] == "d" and k != "dve") else i

    def _deps(self, reads, writes):
        deps = {}

        def need(k, i):
            if deps.get(k, 0) < i:
                deps[k] = i

        for b in reads:
            if b.lw is not None:
                need(*b.lw)
            if b.psum:
                for k, i in b.rd.items():
                    need(k, i)
        for b in writes:
            if b.lw is not None:
                need(*b.lw)
            for k, i in b.rd.items():
                need(k, i)
        return deps

    def _wait(self, e, deps):
        w = self.waited[e]
        for k, i in deps.items():
            if w.get(k, 0) < i:
                self.eng[e].wait_ge(self.sem[k], self._val(k, i))
                w[k] = i

    def _commit(self, me, reads, writes):
        k, i = me
        for b in reads:
            b.rd[k] = i
        for b in writes:
            b.lw = me
            b.rd = {}

    def op(self, e, fn, reads=(), writes=()):
        if self._lim():
            return
        deps = self._deps(reads, writes)
        if e == "pe":
            deps.pop("pe", None)
        self._wait(e, deps)
        ins = fn(self.eng[e])
        self.cnt[e] += 1
        ins.then_inc(self.sem[e], 1)
        self._commit((e, self.cnt[e]), reads, writes)

    def dma(self, out, in_, reads=(), writes=(), q="sp"):
        if self._lim():
            return
        c = "d%d" % self.next_dch
        self.next_dch = (self.next_dch + 1) % N_DCH
        deps = self._deps(reads, writes)
        if self.cnt[c] > 0:
            deps[c] = max(deps.get(c, 0), self.cnt[c])
        self._wait(q, deps)
        ins = self.eng[q].dma_start(out=out, in_=in_)
        self.cnt[c] += 1
        ins.then_inc(self.sem[c], 16)
        self._commit((c, self.cnt[c]), reads, writes)

    def collective(self, fn, reads=(), writes=()):
        if self._lim():
            return
        deps = self._deps(reads, writes)
        self._wait("pool", deps)
        ins = fn(self.eng["pool"])
        self.cnt["cc"] += 1
        ins.then_inc(self.sem["cc"], 1)
        self._commit(("cc", self.cnt["cc"]), reads, writes)

    def barrier(self):
        for e in self.eng:
            deps = {k: c for k, c in self.cnt.items() if c > 0 and k != e}
            self._wait(e, deps)

    def finish(self):
        deps = {k: c for k, c in self.cnt.items() if c > 0 and k != "sp"}
        self._wait("sp", deps)


def build_nc(S, debug=False, stop_after=9, limit=None):
    NT = S // 128
    NB = S // 512
    SH = S // 2
    nc = bass.Bass("TRN2", target_bir_lowering=False)

    def din(name, shape, dt=F32):
        return nc.dram_tensor(name, list(shape), dt, kind="ExternalInput").ap()

    x = din("x", [S, D_MODEL])
    xres = din("xres", [SH, D_MODEL])
    w_rw = din("w_rw", [D_MODEL, RWC])
    w_fox = din("w_fox", [D_MODEL, FXC])
    w_grw = din("w_grw", [D_MODEL, 256])
    w_gfx = din("w_gfx", [D_MODEL, 256])
    w_outp = din("w_outp", [D_MODEL, D_MODEL])
    g_col = din("g_col", [128, KC])
    pb1_d = din("pb1", [128, 2688])
    pb2_d = din("pb2", [128, 132])
    pb3_d = din("pb3", [128, D_MODEL])
    w_up_d = din("w_up", [64, 256])
    a_up_d = din("a_up", [64, 256])
    out = nc.dram_tensor("out", [SH, D_MODEL], F32, kind="ExternalOutput").ap()
    dbg_kind = "ExternalOutput" if debug else "Internal"
    ybuf = [nc.dram_tensor("ybuf%d" % c, [64, S], BF16, kind=dbg_kind).ap() for c in range(8)]
    yall = [nc.dram_tensor("yall%d" % c, [128, S], BF16, kind=dbg_kind).ap() for c in range(8)]
    sel_d = din("sel", [128, 2])
    ybuf_b = [Buf("ybuf%d" % c) for c in range(8)]
    yall_b = [Buf("yall%d" % c) for c in range(8)]
    out_b = Buf("out")

    sch = Sched(nc)
    sch.limit = limit
    op = sch.op

    class Scope:
        def __init__(self):
            self.stack = []

        def sb(self, name, shape, dt=F32):
            cm = nc.sbuf_tensor("s_" + name, list(shape), dt)
            t = cm.__enter__()
            self.stack.append(cm)
            return t, Buf(name)

        def ps(self, name, shape, dt=F32):
            cm = nc.psum_tensor("p_" + name, list(shape), dt)
            t = cm.__enter__()
            self.stack.append(cm)
            return t, Buf(name, psum=True)

        def close(self):
            while self.stack:
                self.stack.pop().__exit__(None, None, None)

    G = Scope()
    ident_bf, ident_bf_b = G.sb("ident_bf", [128, 128], BF16)
    ident_f, ident_f_b = G.sb("ident_f", [128, 128], F32)
    gcol, gcol_b = G.sb("gcol", [128, KC], F32)
    stage, stage_b = G.sb("stage", [128, 1024], F32)
    stage2, stage2_b = G.sb("stage2", [128, 1024], F32)

    def mk_ident(t, b):
        op("pool", lambda e: e.memset(t[:], 0.0), [], [b])
        op("pool", lambda e: e.affine_select(out=t[:], in_=t[:], pattern=[[-1, 128]], compare_op=ALU.not_equal,
                                             fill=1.0, base=0, channel_multiplier=1), [b], [b])

    mk_ident(ident_bf, ident_bf_b)
    mk_ident(ident_f, ident_f_b)
    sch.dma(gcol[:], g_col[:, :], [], [gcol_b])

    def load_weight(dst, dst_b, src, ncols, post=None):
        for c in range(KC):
            st, stb = (stage, stage_b) if c % 2 == 0 else (stage2, stage2_b)
            sch.dma(st[:, 0:ncols], src[c * 128:(c + 1) * 128, :], [], [stb])
            if post is None:
                op("dve", lambda e: e.tensor_scalar(out=dst[:, c, :], in0=st[:, 0:ncols], scalar1=gcol[:, c:c + 1],
                                                    scalar2=None, op0=ALU.mult), [stb, gcol_b], [dst_b])
            else:
                for (d2, d2b, pt, ptb) in post:
                    op("dve", lambda e: e.scalar_tensor_tensor(out=d2[:, c, :], in0=st[:, 0:ncols],
                                                               scalar=gcol[:, c:c + 1], in1=pt, op0=ALU.mult,
                                                               op1=ALU.mult), [stb, gcol_b, ptb], [d2b])

    def rmsnorm_to_hT(xt, xt_b, h, h_b, junk, junk_b, ss, ss_b, tpb, tpb_b, dst_ap_fn, dst_b):
        op("act", lambda e: e.activation(out=junk[:], in_=xt[:], func=AF.Square, accum_out=ss[:, 0:1]),
           [xt_b], [junk_b, ss_b])
        op("act", lambda e: e.activation(out=ss[:, 1:2], in_=ss[:, 0:1], func=AF.Ln, scale=1.0 / D_MODEL,
                                         bias=NORM_EPS), [ss_b], [ss_b])
        op("act", lambda e: e.activation(out=ss[:, 2:3], in_=ss[:, 1:2], func=AF.Exp, scale=-0.5), [ss_b], [ss_b])
        op("act", lambda e: e.activation(out=h[:], in_=xt[:], func=AF.Copy, scale=ss[:, 2:3]), [xt_b, ss_b], [h_b])
        tpv = tpb[:].rearrange("p (c t) -> p c t", c=KC)
        for c in range(KC):
            op("pe", lambda e: e.transpose(out=tpv[:, c, :], in_=h[:, c * 128:(c + 1) * 128], identity=ident_bf[:]),
               [h_b, ident_bf_b], [tpb_b])
        op("dve", lambda e: e.tensor_copy(out=dst_ap_fn(), in_=tpv), [tpb_b], [dst_b])

    P1 = Scope()
    Wa, Wa_b = P1.sb("Wa", [128, KC, RWC], BF16)
    Wb, Wb_b = P1.sb("Wb", [128, KC, RWC], BF16)
    Wg, Wg_b = P1.sb("Wg1", [128, KC, 256], BF16)
    pb1, pb1_b = P1.sb("pb1", [128, 2688], F32)
    omm, omm_b = P1.sb("omm", [128, RWC], F32)
    wup, wup_b = P1.sb("wup", [64, 256], F32)
    aup, aup_b = P1.sb("aup", [64, 256], F32)
    sch.dma(pb1[:], pb1_d[:, :], [], [pb1_b])
    sch.dma(wup[:], w_up_d[:, :], [], [wup_b])
    sch.dma(aup[:], a_up_d[:, :], [], [aup_b])
    MU = pb1[:, 0:896]
    W0 = pb1[:, 896:1152]
    A0 = pb1[:, 1152:1408]
    KK_ = pb1[:, 1408:1664]
    KA_ = pb1[:, 1664:1920]
    RK_ = pb1[:, 1920:2176]
    GNG = pb1[:, 2176:2432]
    GNB = pb1[:, 2432:2688]
    op("dve", lambda e: e.tensor_scalar(out=omm[:], in0=MU, scalar1=-1.0, scalar2=1.0, op0=ALU.mult, op1=ALU.add),
       [pb1_b], [omm_b])
    load_weight(None, None, w_rw, RWC, post=[(Wa, Wa_b, omm[:], omm_b), (Wb, Wb_b, MU, pb1_b)])
    load_weight(Wg, Wg_b, w_grw, 256)

    tri_i, tri_i_b = P1.sb("tri_i", [128, 128], F32)
    tri_s, tri_s_b = P1.sb("tri_s", [128, 128], F32)
    tri_r, tri_r_b = P1.sb("tri_r", [128, 128], F32)
    mask4, mask4_b = P1.sb("mask4", [128, 512], F32)
    maskr4, maskr4_b = P1.sb("maskr4", [128, 512], F32)
    ones_c, ones_c_b = P1.sb("ones_c", [128, 1], F32)

    def mk_tri(t, b, cm, pat, cmp_op):
        op("pool", lambda e: e.memset(t[:], 1.0), [], [b])
        op("pool", lambda e: e.affine_select(out=t[:], in_=t[:], pattern=[[pat, 128]], compare_op=cmp_op, fill=0.0,
                                             base=0, channel_multiplier=cm), [b], [b])
        op("pool", lambda e: e.memset(t[0:64, 64:128], 0.0), [b], [b])
        op("pool", lambda e: e.memset(t[64:128, 0:64], 0.0), [b], [b])

    mk_tri(tri_i, tri_i_b, -1, 1, ALU.is_ge)
    mk_tri(tri_s, tri_s_b, -1, 1, ALU.is_gt)
    mk_tri(tri_r, tri_r_b, 1, -1, ALU.is_gt)
    op("pool", lambda e: e.memset(ones_c[:], 1.0), [], [ones_c_b])
    for q_, (src, srcb) in enumerate([(tri_s, tri_s_b), (tri_i, tri_i_b), (tri_s, tri_s_b), (tri_i, tri_i_b)]):
        op("pool", lambda e: e.tensor_copy(out=mask4[:, q_ * 128:(q_ + 1) * 128], in_=src[:]), [srcb], [mask4_b])
        op("pool", lambda e: e.tensor_copy(out=maskr4[:, q_ * 128:(q_ + 1) * 128], in_=tri_r[:]), [tri_r_b], [maskr4_b])

    bank = [P1.ps("bank%d" % i, [128, 512], F32) for i in range(7)]
    bankT, bankT_b = P1.ps("bankT", [128, 1024], BF16)

    def sb(name, shape, dt=F32):
        return P1.sb(name, shape, dt)

    xt2 = [sb("xt%d" % i, [128, 1024]) for i in range(2)]
    h, h_b = sb("h", [128, 1024], BF16)
    junk, junk_b = sb("junk", [128, 1024], BF16)
    ss, ss_b = sb("ss", [128, 4])
    hT2 = [sb("hT%d" % i, [128, KC, 129], BF16) for i in range(2)]
    r32, r32_b = sb("r32", [128, 256])
    k32, k32_b = sb("k32", [128, 256])
    v32, v32_b = sb("v32", [128, 256])
    lt32, lt32_b = sb("lt32", [128, 128])
    lT, lT_b = sb("lT", [64, 2, 128])
    lwn, lwn_b = sb("lwn", [128, 256])
    t1, t1_b = sb("t1", [128, 256])
    t2, t2_b = sb("t2", [128, 256])
    a_, a_b = sb("a_", [128, 256])
    kk, kk_b = sb("kk", [128, 256])
    km, km_b = sb("km", [128, 256])
    ba, ba_b = sb("ba", [128, 256])
    s4, s4_b = sb("s4", [128, 16])
    ecum, ecum_b = sb("ecum", [128, 256])
    encum, encum_b = sb("encum", [128, 256])
    eprev, eprev_b = sb("eprev", [128, 256])
    erc, erc_b = sb("erc", [128, 256])
    wc, wc_b = sb("wc", [64, 8])
    tl4 = sb("tl4", [128, 4, 256], BF16)
    bh, bh_b = sb("bh", [128, 256], BF16)
    kh, kh_b = sb("kh", [128, 256], BF16)
    vb, vb_b = sb("vb", [128, 256], BF16)
    bhm, bhm_b = sb("bhm", [128, 2, 256], BF16)
    khm, khm_b = sb("khm", [128, 2, 256], BF16)
    cmask, cmask_b = sb("cmask", [128, 2])
    op("pool", lambda e: e.memset(bhm[:], 0.0), [], [bhm_b])
    op("pool", lambda e: e.memset(khm[:], 0.0), [], [khm_b])
    op("pool", lambda e: e.memset(cmask[:], 0.0), [], [cmask_b])
    op("pool", lambda e: e.memset(cmask[0:64, 0:1], 1.0), [cmask_b], [cmask_b])
    op("pool", lambda e: e.memset(cmask[64:128, 1:2], 1.0), [cmask_b], [cmask_b])
    TT, TT_b = sb("TT", [64, 16, 128], BF16)
    A_sb = [sb("A%d" % hh, [128, 512], BF16) for hh in range(NH)]
    Ni2 = [sb("Ni%d" % i, [128, 512], BF16) for i in range(2)]
    NiT2 = [sb("NiT%d" % i, [128, 512], BF16) for i in range(2)]
    X32, X32_b = sb("X32", [128, 512])
    Xb, Xb_b = sb("Xb", [128, 512], BF16)
    GTs, GTs_b = sb("GTs", [64, 1024])
    M0T, M0T_b = sb("M0T", [64, 512])
    Nst, Nst_b = sb("Nst", [64, 512])
    ST2 = [sb("ST%d" % i, [64, 256]) for i in range(2)]
    stt, stt_b = sb("stt", [64, 256])
    y32, y32_b = sb("y32", [128, 256])
    sg, sg_b = sb("sg", [128, 256])
    bon, bon_b = sb("bon", [128, 256])
    yg, yg_b = sb("yg", [128, 256], BF16)
    ygT, ygT_b = sb("ygT", [128, 2, 512], BF16)

    op("dve", lambda e: e.memset(ST2[0][0][:], 0.0), [], [ST2[0][1]])
    op("dve", lambda e: e.memset(GTs[:], 0.0), [], [GTs_b])
    op("dve", lambda e: e.memset(hT2[0][0][:, :, 0:1], 0.0), [], [hT2[0][1]])

    def v3(ap, hh=NH):
        return ap.rearrange("p (h d) -> p h d", h=hh)

    def bc(ap_small, hh=NH, d=HD):
        return ap_small.unsqueeze(2).to_broadcast([128, hh, d])

    st_cur = 0
    for i in range(NT):
        xt, xt_b = xt2[i % 2]
        hT, hT_b = hT2[i % 2]
        hTp, hTp_b = hT2[(i + 1) % 2]
        if i == 0:
            sch.dma(xt[:], x[0:128, :], [], [xt_b])
        if i + 1 < NT:
            sch.dma(xt2[(i + 1) % 2][0][:], x[(i + 1) * 128:(i + 2) * 128, :], [], [xt2[(i + 1) % 2][1]])
        rmsnorm_to_hT(xt, xt_b, h, h_b, junk, junk_b, ss, ss_b, bankT, bankT_b, lambda: hT[:, :, 1:129], hT_b)
        if i > 0:
            op("pool", lambda e: e.tensor_copy(out=hT[:, :, 0:1], in_=hTp[:, :, 128:129]), [hTp_b], [hT_b])
        pr0, pr0_b = bank[0]
        pr1, pr1_b = bank[1]
        for (pt, ptb, n0, n1) in ((pr0, pr0_b, 0, 512), (pr1, pr1_b, 512, 896)):
            for c in range(KC):
                op("pe", lambda e: e.matmul(pt[:, 0:n1 - n0], hT[:, c, 1:129], Wa[:, c, n0:n1], start=(c == 0), stop=False),
                   [hT_b, Wa_b], [ptb])
            for c in range(KC):
                op("pe", lambda e: e.matmul(pt[:, 0:n1 - n0], hT[:, c, 0:128], Wb[:, c, n0:n1], start=False, stop=(c == KC - 1)),
                   [hT_b, Wb_b], [ptb])
        pg, pg_b = bank[2]
        for c in range(KC):
            op("pe", lambda e: e.matmul(pg[:, 0:256], hT[:, c, 1:129], Wg[:, c, :], start=(c == 0), stop=(c == KC - 1)),
               [hT_b, Wg_b], [pg_b])
        op("act", lambda e: e.activation(out=r32[:], in_=pr0[:, 0:256], func=AF.Copy), [pr0_b], [r32_b])
        op("dve", lambda e: e.tensor_copy(out=k32[:], in_=pr0[:, 256:512]), [pr0_b], [k32_b])
        op("act", lambda e: e.activation(out=v32[:], in_=pr1[:, 0:256], func=AF.Copy), [pr1_b], [v32_b])
        op("act", lambda e: e.activation(out=lt32[:, 0:64], in_=pr1[:, 256:320], func=AF.Exp, scale=-2.0), [pr1_b], [lt32_b])
        op("act", lambda e: e.activation(out=lt32[:, 0:64], in_=lt32[:, 0:64], func=AF.Ln, bias=1.0), [lt32_b], [lt32_b])
        op("act", lambda e: e.activation(out=lt32[:, 0:64], in_=lt32[:, 0:64], func=AF.Exp, scale=-1.0), [lt32_b], [lt32_b])
        op("dve", lambda e: e.tensor_scalar(out=lt32[:, 0:64], in0=lt32[:, 0:64], scalar1=2.0, scalar2=-1.0, op0=ALU.mult,
                                            op1=ALU.add), [lt32_b], [lt32_b])
        op("dve", lambda e: e.tensor_copy(out=lt32[:, 64:128], in_=pr1[:, 320:384]), [pr1_b, lt32_b], [lt32_b])
        ptl, ptl_b = bank[3]
        for q_ in range(2):
            op("pe", lambda e: e.transpose(out=ptl[0:64, q_ * 128:(q_ + 1) * 128], in_=lt32[:, q_ * 64:(q_ + 1) * 64],
                                           identity=ident_f[:]), [lt32_b, ident_f_b], [ptl_b])
        op("dve", lambda e: e.tensor_copy(out=lT[:].rearrange("p a t -> p (a t)"), in_=ptl[0:64, 0:256]), [ptl_b], [lT_b])
        pl, pl_b = bank[4]
        op("pe", lambda e: e.matmul(pl[:, 0:256], lT[:, 0, :], wup[:], start=True, stop=True), [lT_b, wup_b], [pl_b])
        op("pe", lambda e: e.matmul(pl[:, 256:512], lT[:, 1, :], aup[:], start=True, stop=True), [lT_b, aup_b], [pl_b])
        op("dve", lambda e: e.tensor_tensor(out=t1[:], in0=pl[:, 0:256], in1=W0, op=ALU.add), [pl_b, pb1_b], [t1_b])
        op("act", lambda e: e.activation(out=t1[:], in_=t1[:], func=AF.Exp, scale=-1.0), [t1_b], [t1_b])
        op("act", lambda e: e.activation(out=t1[:], in_=t1[:], func=AF.Ln, bias=1.0), [t1_b], [t1_b])
        op("act", lambda e: e.activation(out=lwn[:], in_=t1[:], func=AF.Exp, scale=-1.0, bias=-0.5), [t1_b], [lwn_b])
        op("dve", lambda e: e.tensor_tensor(out=t2[:], in0=pl[:, 256:512], in1=A0, op=ALU.add), [pl_b, pb1_b], [t2_b])
        op("act", lambda e: e.activation(out=t2[:], in_=t2[:], func=AF.Exp, scale=-1.0), [t2_b], [t2_b])
        op("act", lambda e: e.activation(out=t2[:], in_=t2[:], func=AF.Ln, bias=1.0), [t2_b], [t2_b])
        op("act", lambda e: e.activation(out=a_[:], in_=t2[:], func=AF.Exp, scale=-1.0), [t2_b], [a_b])
        op("dve", lambda e: e.tensor_tensor(out=kk[:], in0=k32[:], in1=KK_, op=ALU.mult), [k32_b, pb1_b], [kk_b])
        op("dve", lambda e: e.tensor_tensor(out=t1[:], in0=kk[:], in1=kk[:], op=ALU.mult), [kk_b], [t1_b])
        op("dve", lambda e: e.tensor_reduce(out=s4[:, 0:4], in_=v3(t1[:]), axis=AX.X, op=ALU.add), [t1_b], [s4_b])
        op("dve", lambda e: e.tensor_scalar(out=s4[:, 0:4], in0=s4[:, 0:4], scalar1=1e-24, scalar2=None, op0=ALU.max),
           [s4_b], [s4_b])
        op("act", lambda e: e.activation(out=s4[:, 0:4], in_=s4[:, 0:4], func=AF.Ln), [s4_b], [s4_b])
        op("act", lambda e: e.activation(out=s4[:, 0:4], in_=s4[:, 0:4], func=AF.Exp, scale=-0.5), [s4_b], [s4_b])
        op("dve", lambda e: e.tensor_tensor(out=v3(kk[:]), in0=v3(kk[:]), in1=bc(s4[:, 0:4]), op=ALU.mult),
           [kk_b, s4_b], [kk_b])
        op("dve", lambda e: e.scalar_tensor_tensor(out=t1[:], in0=a_[:], scalar=-1.0, in1=KA_, op0=ALU.add, op1=ALU.mult),
           [a_b, pb1_b], [t1_b])
        op("dve", lambda e: e.scalar_tensor_tensor(out=km[:], in0=t1[:], scalar=1.0, in1=k32[:], op0=ALU.add, op1=ALU.mult),
           [t1_b, k32_b], [km_b])
        op("dve", lambda e: e.tensor_tensor(out=ba[:], in0=kk[:], in1=a_[:], op=ALU.mult), [kk_b, a_b], [ba_b])
        op("dve", lambda e: e.tensor_tensor(out=t1[:], in0=r32[:], in1=km[:], op=ALU.mult), [r32_b, km_b], [t1_b])
        op("dve", lambda e: e.tensor_tensor(out=t1[:], in0=t1[:], in1=RK_, op=ALU.mult), [t1_b, pb1_b], [t1_b])
        op("dve", lambda e: e.tensor_reduce(out=s4[:, 4:8], in_=v3(t1[:]), axis=AX.X, op=ALU.add), [t1_b], [s4_b])
        op("dve", lambda e: e.tensor_tensor(out=v3(bon[:]), in0=v3(v32[:]), in1=bc(s4[:, 4:8]), op=ALU.mult),
           [v32_b, s4_b], [bon_b])
        op("act", lambda e: e.activation(out=t2[:], in_=pg[:, 0:256], func=AF.Exp, scale=-1.0), [pg_b], [t2_b])
        op("act", lambda e: e.activation(out=t2[:], in_=t2[:], func=AF.Ln, bias=1.0), [t2_b], [t2_b])
        op("act", lambda e: e.activation(out=t2[:], in_=t2[:], func=AF.Exp, scale=-1.0), [t2_b], [t2_b])
        op("dve", lambda e: e.tensor_tensor(out=sg[:], in0=t2[:], in1=pg[:, 0:256], op=ALU.mult), [t2_b, pg_b], [sg_b])
        pc0, pc0_b = bank[5]
        pc1, pc1_b = bank[6]
        op("pe", lambda e: e.matmul(pc0[:, 0:256], tri_i[:], lwn[:], start=True, stop=True), [tri_i_b, lwn_b], [pc0_b])
        op("pe", lambda e: e.matmul(pc0[:, 256:512], tri_s[:], lwn[:], start=True, stop=True), [tri_s_b, lwn_b], [pc0_b])
        op("pe", lambda e: e.matmul(pc1[:, 0:256], tri_r[:], lwn[:], start=True, stop=True), [tri_r_b, lwn_b], [pc1_b])
        for cch in range(2):
            for hh in range(NH):
                col = 256 + cch * 4 + hh
                op("pe", lambda e: e.matmul(pc1[0:64, col:col + 1], lwn[cch * 64:(cch + 1) * 64, hh * 64:(hh + 1) * 64],
                                            ones_c[cch * 64:(cch + 1) * 64, :], start=True, stop=True),
                   [lwn_b, ones_c_b], [pc1_b])
        op("act", lambda e: e.activation(out=ecum[:], in_=pc0[:, 0:256], func=AF.Exp, scale=-1.0), [pc0_b], [ecum_b])
        op("act", lambda e: e.activation(out=encum[:], in_=pc0[:, 0:256], func=AF.Exp), [pc0_b], [encum_b])
        op("act", lambda e: e.activation(out=eprev[:], in_=pc0[:, 256:512], func=AF.Exp, scale=-1.0), [pc0_b], [eprev_b])
        op("act", lambda e: e.activation(out=erc[:], in_=pc1[:, 0:256], func=AF.Exp, scale=-1.0), [pc1_b], [erc_b])
        op("act", lambda e: e.activation(out=wc[:], in_=pc1[0:64, 256:264], func=AF.Exp, scale=-1.0), [pc1_b], [wc_b])
        tl, tl_b = tl4
        op("dve", lambda e: e.scalar_tensor_tensor(out=tl[:, 0, :], in0=kk[:], scalar=-1.0, in1=eprev[:],
                                                   op0=ALU.mult, op1=ALU.mult), [kk_b, eprev_b], [tl_b])
        op("dve", lambda e: e.tensor_tensor(out=tl[:, 1, :], in0=r32[:], in1=ecum[:], op=ALU.mult), [r32_b, ecum_b], [tl_b])
        op("dve", lambda e: e.tensor_tensor(out=tl[:, 2, :], in0=ba[:], in1=encum[:], op=ALU.mult), [ba_b, encum_b], [tl_b])
        op("dve", lambda e: e.tensor_tensor(out=tl[:, 3, :], in0=km[:], in1=encum[:], op=ALU.mult), [km_b, encum_b], [tl_b])
        for cch in range(2):
            rs = slice(cch * 64, (cch + 1) * 64)
            op("pool", lambda e: e.tensor_tensor(out=bhm[rs, cch, :], in0=ba[rs, :], in1=erc[rs, :], op=ALU.mult),
               [ba_b, erc_b], [bhm_b])
            op("pool", lambda e: e.tensor_tensor(out=khm[rs, cch, :], in0=km[rs, :], in1=erc[rs, :], op=ALU.mult),
               [km_b, erc_b], [khm_b])
        op("pool", lambda e: e.tensor_copy(out=vb[:], in_=v32[:]), [v32_b], [vb_b])
        TTp = bankT[0:64, :].rearrange("p (a t) -> p a t", a=8)
        for rnd in range(2):
            for tq in range(2):
                ty = rnd * 2 + tq
                for hh in range(NH):
                    op("pe", lambda e: e.transpose(out=TTp[:, tq * 4 + hh, :], in_=tl[:, ty, hh * 64:(hh + 1) * 64],
                                                   identity=ident_bf[:]), [tl_b, ident_bf_b], [bankT_b])
            op("act", lambda e: e.activation(out=TT[:, rnd * 8:(rnd + 1) * 8, :], in_=TTp, func=AF.Copy), [bankT_b], [TT_b])

        def kT(ty, hh):
            return TT[:, ty * 4 + hh, :]

        pn, pn_b = bank[2]
        for hh in range(NH):
            pa, pa_b = bank[hh % 2]
            rhs2 = TT[:, hh:hh + 5:4, :]
            op("pe", lambda e: e.matmul(pa[:, 0:256], kT(2, hh), rhs2, start=True, stop=True), [TT_b], [pa_b])
            op("pe", lambda e: e.matmul(pa[:, 256:512], kT(3, hh), rhs2, start=True, stop=True), [TT_b], [pa_b])
            op("pe", lambda e: e.matmul(pn[:, hh * 128:(hh + 1) * 128], kT(0, hh), kT(2, hh), start=True, stop=True),
               [TT_b], [pn_b])
            A, A_b = A_sb[hh]
            op("dve", lambda e: e.tensor_tensor(out=A[:], in0=pa[:], in1=mask4[:], op=ALU.mult), [pa_b, mask4_b], [A_b])
        Ni, Ni_b = Ni2[0]
        op("dve", lambda e: e.tensor_tensor(out=Ni[:], in0=pn[:], in1=maskr4[:], op=ALU.mult), [pn_b, maskr4_b], [Ni_b])
        px, px_b = bank[4]
        for hh in range(NH):
            A, A_b = A_sb[hh]
            op("pe", lambda e: e.matmul(px[:, hh * 128:hh * 128 + 64], ident_bf[:], tl[:, 0, hh * 64:(hh + 1) * 64],
                                        start=(hh == 0), stop=False, skip_group_check=True), [tl_b, ident_bf_b], [px_b])
            op("pe", lambda e: e.matmul(px[:, hh * 128 + 64:(hh + 1) * 128], A[:, 256:384], vb[:, hh * 64:(hh + 1) * 64],
                                        start=False, stop=False, skip_group_check=True), [A_b, vb_b], [px_b])
        op("dve", lambda e: e.tensor_copy(out=Xb[:], in_=px[:]), [px_b], [Xb_b])
        for s_ in range(6):
            cur = s_ % 2
            Ni, Ni_b = Ni2[cur]
            NiT, NiT_b = NiT2[cur]
            for hh in range(NH):
                if s_ == 0:
                    lt_, ltb = A_sb[hh][0][:, 0:128], A_sb[hh][1]
                else:
                    lt_, ltb = NiT[:, hh * 128:(hh + 1) * 128], NiT_b
                op("pe", lambda e: e.matmul(px[:, hh * 128:(hh + 1) * 128], lt_, Xb[:, hh * 128:(hh + 1) * 128],
                                            start=False, stop=(s_ == 5), skip_group_check=True), [ltb, Xb_b], [px_b])
            if s_ < 5:
                ps1, ps1_b = bank[5]
                ps2, ps2_b = bank[6]
                Nn_, Nn_b = Ni2[1 - cur]
                NnT, NnT_b = NiT2[1 - cur]
                for hh in range(NH):
                    if s_ == 0:
                        lt_, ltb = A_sb[hh][0][:, 0:128], A_sb[hh][1]
                    else:
                        lt_, ltb = NiT[:, hh * 128:(hh + 1) * 128], NiT_b
                    n_ = Ni[:, hh * 128:(hh + 1) * 128]
                    op("pe", lambda e: e.matmul(ps2[:, hh * 128:(hh + 1) * 128], n_, lt_, start=True, stop=True),
                       [ltb, Ni_b], [ps2_b])
                    op("pe", lambda e: e.matmul(ps1[:, hh * 128:(hh + 1) * 128], lt_, n_, start=True, stop=True),
                       [ltb, Ni_b], [ps1_b])
            op("dve", lambda e: e.tensor_copy(out=Xb[:], in_=px[:]), [px_b], [Xb_b])
            if s_ < 5:
                op("dve", lambda e: e.tensor_copy(out=NnT[:], in_=ps2[:]), [ps2_b], [NnT_b])
                op("act", lambda e: e.activation(out=Nn_[:], in_=ps1[:], func=AF.Copy), [ps1_b], [Nn_b])
        Xb3 = Xb[:].rearrange("p (h c) -> p h c", h=NH)
        pgt, pgt_b = bank[0]
        tlr = tl[:, 1, :]
        for hh in range(NH):
            A, A_b = A_sb[hh]
            op("pe", lambda e: e.matmul(pgt[0:64, hh * 128:(hh + 1) * 128], Xb3[:, hh, 0:64], A[:, 128:256],
                                        start=True, stop=False), [Xb_b, A_b], [pgt_b])
            op("pe", lambda e: e.matmul(pgt[0:64, hh * 128:(hh + 1) * 128], tlr[:, hh * 64:(hh + 1) * 64], ident_bf[:],
                                        start=False, stop=True), [tl_b, ident_bf_b], [pgt_b])
        pgt5 = pgt[0:64, :].rearrange("p (h c t) -> p h c t", h=NH, c=2)
        GT5 = GTs[:].rearrange("p (a h c t) -> p a h c t", a=2, h=NH, c=2)
        for cch in range(2):
            op("act", lambda e: e.activation(out=GT5[:, cch, :, cch, :], in_=pgt5[:, :, cch, :], func=AF.Copy), [pgt_b], [GTs_b])
        pm, pm_b = bank[1]
        pn2, pn2_b = bank[5]
        for cch in range(2):
            for hh in range(NH):
                cs = slice((cch * 4 + hh) * 64, (cch * 4 + hh + 1) * 64)
                hs = slice(hh * 64, (hh + 1) * 64)
                op("pe", lambda e: e.matmul(pm[0:64, cs], Xb3[:, hh, 0:64], bhm[:, cch, hs], start=True, stop=True),
                   [Xb_b, bhm_b], [pm_b])
                op("pe", lambda e: e.matmul(pn2[0:64, cs], bhm[:, cch, hs], Xb3[:, hh, 64:128], start=True, stop=False),
                   [Xb_b, bhm_b], [pn2_b])
                op("pe", lambda e: e.matmul(pn2[0:64, cs], khm[:, cch, hs], vb[:, hs], start=False, stop=True),
                   [khm_b, vb_b], [pn2_b])
        op("act", lambda e: e.activation(out=M0T[:], in_=pm[0:64, :], func=AF.Copy), [pm_b], [M0T_b])
        op("dve", lambda e: e.tensor_copy(out=Nst[:], in_=pn2[0:64, :]), [pn2_b], [Nst_b])
        py, py_b = bank[3]
        for hh in range(NH):
            A, A_b = A_sb[hh]
            hs = slice(hh * 64, (hh + 1) * 64)
            op("pe", lambda e: e.matmul(py[:, hs], A[:, 128:256], Xb3[:, hh, 64:128], start=(hh == 0), stop=False,
                                        skip_group_check=True), [A_b, Xb_b], [py_b])
            op("pe", lambda e: e.matmul(py[:, hs], A[:, 384:512], vb[:, hs], start=False, stop=False, skip_group_check=True),
               [A_b, vb_b], [py_b])
        pst, pst_b = bank[6]
        for cch in range(2):
            ST, ST_b = ST2[st_cur]
            STn, STn_b = ST2[1 - st_cur]
            for hh in range(NH):
                hs = slice(hh * 64, (hh + 1) * 64)
                op("pe", lambda e: e.matmul(py[:, hs], GTs[:, cch * 512 + hh * 128: cch * 512 + (hh + 1) * 128],
                                            ST[:, hs], start=False, stop=True, skip_group_check=True), [GTs_b, ST_b], [py_b])
            for hh in range(NH):
                hs = slice(hh * 64, (hh + 1) * 64)
                cs = slice((cch * 4 + hh) * 64, (cch * 4 + hh + 1) * 64)
                op("pe", lambda e: e.matmul(pst[0:64, hs], M0T[:, cs], ST[:, hs], start=True, stop=True), [M0T_b, ST_b], [pst_b])
            op("dve", lambda e: e.tensor_tensor(out=v3(stt[:]), in0=v3(ST[:]),
                                                in1=wc[:, cch * 4:(cch + 1) * 4].unsqueeze(2).to_broadcast([64, NH, HD]),
                                                op=ALU.mult), [ST_b, wc_b], [stt_b])
            op("dve", lambda e: e.tensor_tensor(out=stt[:], in0=stt[:], in1=Nst[:, cch * 256:(cch + 1) * 256], op=ALU.add),
               [stt_b, Nst_b], [stt_b])
            op("dve", lambda e: e.tensor_tensor(out=STn[:], in0=stt[:], in1=pst[0:64, 0:256], op=ALU.add),
               [stt_b, pst_b], [STn_b])
            st_cur = 1 - st_cur
        op("act", lambda e: e.activation(out=y32[:], in_=py[:, 0:256], func=AF.Copy), [py_b], [y32_b])
        op("dve", lambda e: e.tensor_reduce(out=s4[:, 8:12], in_=v3(y32[:]), axis=AX.X, op=ALU.add), [y32_b], [s4_b])
        op("dve", lambda e: e.tensor_scalar(out=s4[:, 8:12], in0=s4[:, 8:12], scalar1=1.0 / HD, scalar2=None, op0=ALU.mult),
           [s4_b], [s4_b])
        op("dve", lambda e: e.tensor_tensor(out=v3(y32[:]), in0=v3(y32[:]), in1=bc(s4[:, 8:12]), op=ALU.subtract),
           [y32_b, s4_b], [y32_b])
        op("dve", lambda e: e.tensor_tensor(out=t1[:], in0=y32[:], in1=y32[:], op=ALU.mult), [y32_b], [t1_b])
        op("dve", lambda e: e.tensor_reduce(out=s4[:, 12:16], in_=v3(t1[:]), axis=AX.X, op=ALU.add), [t1_b], [s4_b])
        op("act", lambda e: e.activation(out=s4[:, 12:16], in_=s4[:, 12:16], func=AF.Ln, scale=1.0 / HD, bias=GN_EPS),
           [s4_b], [s4_b])
        op("act", lambda e: e.activation(out=s4[:, 12:16], in_=s4[:, 12:16], func=AF.Exp, scale=-0.5), [s4_b], [s4_b])
        op("dve", lambda e: e.tensor_tensor(out=v3(y32[:]), in0=v3(y32[:]), in1=bc(s4[:, 12:16]), op=ALU.mult),
           [y32_b, s4_b], [y32_b])
        op("dve", lambda e: e.tensor_tensor(out=y32[:], in0=y32[:], in1=GNG, op=ALU.mult), [y32_b, pb1_b], [y32_b])
        op("dve", lambda e: e.tensor_tensor(out=y32[:], in0=y32[:], in1=GNB, op=ALU.add), [y32_b, pb1_b], [y32_b])
        op("dve", lambda e: e.tensor_tensor(out=y32[:], in0=y32[:], in1=bon[:], op=ALU.add), [y32_b, bon_b], [y32_b])
        op("dve", lambda e: e.tensor_tensor(out=yg[:], in0=y32[:], in1=sg[:], op=ALU.mult), [y32_b, sg_b], [yg_b])
        ygp = bankT[:, 0:256].rearrange("p (a t) -> p a t", a=2)
        for q_ in range(2):
            op("pe", lambda e: e.transpose(out=ygp[:, q_, :], in_=yg[:, q_ * 128:(q_ + 1) * 128], identity=ident_bf[:]),
               [yg_b, ident_bf_b], [bankT_b])
        ti = i % 4
        op("act", lambda e: e.activation(out=ygT[:, :, ti * 128:(ti + 1) * 128], in_=ygp, func=AF.Copy), [bankT_b], [ygT_b])
        if ti == 3:
            t0 = (i // 4) * 512
            for hd in range(NH):
                sch.dma(ybuf[hd][:, t0:t0 + 512], ygT[(hd % 2) * 64:(hd % 2 + 1) * 64, hd // 2, :], [ygT_b], [ybuf_b[hd]])

    sch.barrier()
    P1.close()

    def gather(c):
        sch.collective(lambda e: e.collective_compute("AllGather", ALU.bypass, replica_groups=[[0, 1], [2, 3], [4, 5], [6, 7]],
                                                      ins=[ybuf[c][:, :]], outs=[yall[c][:, :]]), [ybuf_b[c]], [yall_b[c]])

    if stop_after > 2:
        for c in range(4):
            gather(c)
    if stop_after <= 1:
        sch.finish()
        return nc

    P2 = Scope()

    def sb2(name, shape, dt=F32):
        return P2.sb(name, shape, dt)

    Wf, Wf_b = sb2("Wf", [128, KC, FXC], BF16)
    Wg2, Wg2_b = sb2("Wg2", [128, KC, 256], BF16)
    pb2, pb2_b = sb2("pb2", [128, 132])
    sch.dma(pb2[:], pb2_d[:, :], [], [pb2_b])
    load_weight(Wf, Wf_b, w_fox, FXC)
    load_weight(Wg2, Wg2_b, w_gfx, 256)
    QG = pb2[:, 0:64]
    KG = pb2[:, 64:128]
    BFB = pb2[:, 128:132]
    qgs, qgs_b = sb2("qgs", [128, 64])
    op("dve", lambda e: e.tensor_scalar(out=qgs[:], in0=QG, scalar1=HD ** -0.5, scalar2=None, op0=ALU.mult), [pb2_b], [qgs_b])
    tri_f, tri_f_b = sb2("tri_f", [128, 128])
    sel127, sel127_b = sb2("sel127", [128, 128])
    ones_f, ones_f_b = sb2("ones_f", [128, 64])
    op("pool", lambda e: e.memset(tri_f[:], 1.0), [], [tri_f_b])
    op("pool", lambda e: e.affine_select(out=tri_f[:], in_=tri_f[:], pattern=[[1, 128]], compare_op=ALU.is_ge, fill=0.0,
                                         base=0, channel_multiplier=-1), [tri_f_b], [tri_f_b])
    op("pool", lambda e: e.memset(sel127[:], 1.0), [], [sel127_b])
    op("pool", lambda e: e.affine_select(out=sel127[:], in_=sel127[:], pattern=[[0, 128]], compare_op=ALU.is_ge, fill=0.0,
                                         base=-127, channel_multiplier=1), [sel127_b], [sel127_b])
    op("pool", lambda e: e.memset(ones_f[:], 1.0), [], [ones_f_b])

    KT, KT_b = sb2("KT", [70, NH, S], BF16)
    vaug, vaug_b = sb2("vaug", [128, NT, NH, 65], BF16)
    op("pool", lambda e: e.memset(vaug[:], 1.0), [], [vaug_b])
    bank = [P2.ps("bank2_%d" % i, [128, 512], F32) for i in range(7)]
    bankT, bankT_b = P2.ps("bankT2", [128, 1024], BF16)
    xt2 = [sb2("xtb%d" % i, [128, 1024]) for i in range(2)]
    h, h_b = sb2("hb", [128, 1024], BF16)
    junk, junk_b = sb2("junkb", [128, 1024], BF16)
    ss, ss_b = sb2("ssb", [128, 4])
    hTb, hTb_b = sb2("hTb", [128, KC, 512], BF16)
    qk32, qk32_b = sb2("qk32", [128, 512])
    sq, sq_b = sb2("sq", [128, 512])
    s8, s8_b = sb2("s8", [128, 8])
    cn2 = [sb2("cn%d" % i, [128, 4]) for i in range(2)]
    z4, z4_b = sb2("z4", [128, 4])
    spl, spl_b = sb2("spl", [128, 3, 4], BF16)
    rr, rr_b = sb2("rr", [128, 4])
    qaug2 = [sb2("qaug%d" % i, [128, NH, 70], BF16) for i in range(2)]
    kaug2 = [sb2("kaug%d" % i, [128, NH, 70], BF16) for i in range(2)]
    QT, QT_b = sb2("QT", [70, NH, 512], BF16)
    sgT, sgT_b = sb2("sgT", [64, NH, 512], BF16)
    eg, eg_b = sb2("eg", [64, 512])
    pT3 = [sb2("pT%d" % i, [128, 512], BF16) for i in range(3)]
    osb, osb_b = sb2("osb", [65, 512])
    rec, rec_b = sb2("rec", [65, 512])
    yt32, yt32_b = sb2("yt32", [64, 512])
    yfT2 = [sb2("yfT%d" % i, [64, 512], BF16) for i in range(2)]
    for i_ in range(2):
        op("pool", lambda e: e.memset(qaug2[i_][0][:], 1.0), [], [qaug2[i_][1]])
        op("pool", lambda e: e.memset(kaug2[i_][0][:], 1.0), [], [kaug2[i_][1]])
    op("dve", lambda e: e.memset(cn2[1][0][:], 0.0), [], [cn2[1][1]])

    pt_rot = 0
    sc_rot = 0
    yf_rot = 0
    for qb in range(NB):
        for ti in range(4):
            i = qb * 4 + ti
            xt, xt_b = xt2[i % 2]
            if i == 0:
                sch.dma(xt[:], x[0:128, :], [], [xt_b])
            if i + 1 < NT:
                sch.dma(xt2[(i + 1) % 2][0][:], x[(i + 1) * 128:(i + 2) * 128, :], [], [xt2[(i + 1) % 2][1]])
            rmsnorm_to_hT(xt, xt_b, h, h_b, junk, junk_b, ss, ss_b, bankT, bankT_b,
                          lambda: hTb[:, :, ti * 128:(ti + 1) * 128], hTb_b)
            pf0, pf0_b = bank[0]
            pf1, pf1_b = bank[1]
            for (pt, ptb, n0, n1) in ((pf0, pf0_b, 0, 512), (pf1, pf1_b, 512, FXC)):
                for c in range(KC):
                    op("pe", lambda e: e.matmul(pt[:, 0:n1 - n0], hTb[:, c, ti * 128:(ti + 1) * 128], Wf[:, c, n0:n1],
                                                start=(c == 0), stop=(c == KC - 1)), [hTb_b, Wf_b], [ptb])
            qa, qa_b = qaug2[i % 2]
            ka, ka_b = kaug2[i % 2]
            op("act", lambda e: e.activation(out=qk32[:], in_=pf0[:], func=AF.Copy), [pf0_b], [qk32_b])
            op("dve", lambda e: e.tensor_tensor(out=sq[:], in0=qk32[:], in1=qk32[:], op=ALU.mult), [qk32_b], [sq_b])
            op("dve", lambda e: e.tensor_reduce(out=s8[:], in_=v3(sq[:], 8), axis=AX.X, op=ALU.add), [sq_b], [s8_b])
            op("act", lambda e: e.activation(out=s8[:], in_=s8[:], func=AF.Ln, scale=1.0 / HD, bias=NORM_EPS), [s8_b], [s8_b])
            op("act", lambda e: e.activation(out=s8[:], in_=s8[:], func=AF.Exp, scale=-0.5), [s8_b], [s8_b])
            op("dve", lambda e: e.tensor_tensor(out=v3(qk32[:], 8), in0=v3(qk32[:], 8), in1=bc(s8[:], 8), op=ALU.mult),
               [qk32_b, s8_b], [qk32_b])
            op("dve", lambda e: e.tensor_tensor(out=qa[:, :, 0:64], in0=v3(qk32[:, 0:256]),
                                                in1=qgs[:].unsqueeze(1).to_broadcast([128, NH, HD]), op=ALU.mult),
               [qk32_b, qgs_b], [qa_b])
            op("dve", lambda e: e.tensor_tensor(out=ka[:, :, 0:64], in0=v3(qk32[:, 256:512]),
                                                in1=KG.unsqueeze(1).to_broadcast([128, NH, HD]), op=ALU.mult),
               [qk32_b, pb2_b], [ka_b])
            op("act", lambda e: e.activation(out=vaug[:, i, :, 0:64], in_=v3(pf1[:, 0:256]), func=AF.Copy), [pf1_b], [vaug_b])
            op("dve", lambda e: e.tensor_tensor(out=z4[:], in0=pf1[:, 256:260], in1=BFB, op=ALU.add), [pf1_b, pb2_b], [z4_b])
            op("act", lambda e: e.activation(out=z4[:], in_=z4[:], func=AF.Exp, scale=-1.0), [z4_b], [z4_b])
            op("act", lambda e: e.activation(out=z4[:], in_=z4[:], func=AF.Ln, bias=1.0), [z4_b], [z4_b])
            cn, cn_b = cn2[i % 2]
            cnp, cnp_b = cn2[(i + 1) % 2]
            pcs, pcs_b = bank[2]
            op("pe", lambda e: e.matmul(pcs[:, 0:4], tri_f[:], z4[:], start=True, stop=False), [tri_f_b, z4_b], [pcs_b])
            op("pe", lambda e: e.matmul(pcs[:, 0:4], sel127[:], cnp[:], start=False, stop=True), [sel127_b, cnp_b], [pcs_b])
            op("dve", lambda e: e.tensor_copy(out=cn[:], in_=pcs[:, 0:4]), [pcs_b], [cn_b])
            op("dve", lambda e: e.tensor_copy(out=spl[:, 0, :], in_=cn[:]), [cn_b], [spl_b])
            op("dve", lambda e: e.tensor_tensor(out=rr[:], in0=cn[:], in1=spl[:, 0, :], op=ALU.subtract), [cn_b, spl_b], [rr_b])
            op("dve", lambda e: e.tensor_copy(out=spl[:, 1, :], in_=rr[:]), [rr_b], [spl_b])
            op("dve", lambda e: e.tensor_tensor(out=rr[:], in0=rr[:], in1=spl[:, 1, :], op=ALU.subtract), [rr_b, spl_b], [rr_b])
            op("dve", lambda e: e.tensor_copy(out=spl[:, 2, :], in_=rr[:]), [rr_b], [spl_b])
            op("dve", lambda e: e.tensor_copy(out=ka[:, :, 67:70], in_=spl[:].rearrange("p s h -> p h s")), [spl_b], [ka_b])
            op("dve", lambda e: e.tensor_scalar(out=qa[:, :, 64:67], in0=spl[:].rearrange("p s h -> p h s"), scalar1=-1.0,
                                                scalar2=None, op0=ALU.mult), [spl_b], [qa_b])
            tqp = bankT[:].rearrange("p (a t) -> p a t", a=8)
            for hh in range(NH):
                op("pe", lambda e: e.transpose(out=tqp[0:70, hh, :], in_=qa[:, hh, :], identity=ident_bf[:]),
                   [qa_b, ident_bf_b], [bankT_b])
                op("pe", lambda e: e.transpose(out=tqp[0:70, 4 + hh, :], in_=ka[:, hh, :], identity=ident_bf[:]),
                   [ka_b, ident_bf_b], [bankT_b])
            op("dve", lambda e: e.tensor_copy(out=QT[:, :, ti * 128:(ti + 1) * 128], in_=tqp[0:70, 0:4, :]), [bankT_b], [QT_b])
            op("act", lambda e: e.activation(out=KT[:, :, i * 128:(i + 1) * 128], in_=tqp[0:70, 4:8, :], func=AF.Copy),
               [bankT_b], [KT_b])
        for hh in range(NH):
            pgt, pgt_b = bank[2]
            for c in range(KC):
                op("pe", lambda e: e.matmul(pgt[0:64, :], Wg2[:, c, hh * 64:(hh + 1) * 64], hTb[:, c, :], start=(c == 0),
                                            stop=(c == KC - 1)), [Wg2_b, hTb_b], [pgt_b])
            op("act", lambda e: e.activation(out=eg[:], in_=pgt[0:64, :], func=AF.Exp, scale=-1.0), [pgt_b], [eg_b])
            op("act", lambda e: e.activation(out=eg[:], in_=eg[:], func=AF.Ln, bias=1.0), [eg_b], [eg_b])
            op("act", lambda e: e.activation(out=eg[:], in_=eg[:], func=AF.Exp, scale=-1.0), [eg_b], [eg_b])
            op("dve", lambda e: e.tensor_tensor(out=sgT[:, hh, :], in0=eg[:], in1=pgt[0:64, :], op=ALU.mult), [eg_b, pgt_b], [sgT_b])
        nkt = 4 * (qb + 1)
        pending = None
        for hh in range(NH):
            po, po_b = bank[3]
            items = []
            for jt in range(nkt):
                jl = jt - 4 * qb
                col0 = 128 * jl if jl > 0 else 0
                items.append((jt, jl, col0, bank[4 + sc_rot], pT3[pt_rot]))
                sc_rot = (sc_rot + 1) % 3
                pt_rot = (pt_rot + 1) % 3
            LA = 2
            for k in range(nkt + LA):
                if k < nkt:
                    jt, jl, col0, (sc, sc_b), (pT, pT_b) = items[k]
                    op("pe", lambda e: e.matmul(sc[:, col0:512], KT[:, hh, jt * 128:(jt + 1) * 128], QT[:, hh, col0:512],
                                                start=True, stop=True), [KT_b, QT_b], [sc_b])
                if k >= LA:
                    jt, jl, col0, (sc, sc_b), (pT, pT_b) = items[k - LA]
                    op("act", lambda e: e.activation(out=pT[:, col0:512], in_=sc[:, col0:512], func=AF.Exp), [sc_b], [pT_b])
                    if jl >= 0:
                        op("pool", lambda e: e.affine_select(out=pT[:, col0:col0 + 128], in_=pT[:, col0:col0 + 128],
                                                             pattern=[[1, 128]], compare_op=ALU.is_ge, fill=0.0, base=0,
                                                             channel_multiplier=-1), [pT_b], [pT_b])
                    op("pe", lambda e: e.matmul(po[0:65, col0:512], vaug[:, jt, hh, :], pT[:, col0:512], start=(jt == 0),
                                                stop=(jt == nkt - 1)), [vaug_b, pT_b], [po_b])
                if pending is not None and k == min(LA + 3, nkt + LA - 1):
                    pending()
                    pending = None
            op("act", lambda e: e.activation(out=osb[:], in_=po[0:65, :], func=AF.Copy), [po_b], [osb_b])
            op("act", lambda e: e.activation(out=rec[64:65, :], in_=osb[64:65, :], func=AF.Ln), [osb_b], [rec_b])
            op("act", lambda e: e.activation(out=rec[64:65, :], in_=rec[64:65, :], func=AF.Exp, scale=-1.0), [rec_b], [rec_b])

            def part_b(hh=hh, qb=qb, yf=yfT2[yf_rot]):
                pbc, pbc_b = bank[2]
                yfT, yfT_b = yf
                op("pe", lambda e: e.matmul(pbc[0:64, :], ones_f[64:65, :], rec[64:65, :], start=True, stop=True),
                   [ones_f_b, rec_b], [pbc_b])
                op("dve", lambda e: e.tensor_tensor(out=yt32[:], in0=osb[0:64, :], in1=pbc[0:64, :], op=ALU.mult),
                   [osb_b, pbc_b], [yt32_b])
                op("dve", lambda e: e.tensor_tensor(out=yfT[:], in0=yt32[:], in1=sgT[:, hh, :], op=ALU.mult),
                   [yt32_b, sgT_b], [yfT_b])
                t0 = qb * 512
                sch.dma(ybuf[4 + hh][:, t0:t0 + 512], yfT[:], [yfT_b], [ybuf_b[4 + hh]])

            yf_rot = 1 - yf_rot
            pending = part_b
        if pending is not None:
            pending()
            pending = None

    sch.barrier()
    P2.close()
    if stop_after <= 2:
        sch.finish()
        return nc

    for c in range(4, 8):
        gather(c)
    P3 = Scope()

    def sb3(name, shape, dt=F32):
        return P3.sb(name, shape, dt)

    Wo, Wo_b = sb3("Wo", [128, KC, D_MODEL], BF16)
    onesg, onesg_b = sb3("onesg", [128, KC])
    op("dve", lambda e: e.memset(onesg[:], 1.0), [], [onesg_b])
    gcol, gcol_b = onesg, onesg_b
    load_weight(Wo, Wo_b, w_outp, D_MODEL)
    fg, fg_b = sb3("fg", [128, D_MODEL])
    sch.dma(fg[:], pb3_d[:, :], [], [fg_b])
    bank = [P3.ps("bank3_%d" % i, [128, 512], F32) for i in range(4)]
    yT2 = [sb3("yT%d" % i, [128, KC, 512], BF16) for i in range(2)]
    yA, yA_b = sb3("yA", [128, KC, 512], BF16)
    yB, yB_b = sb3("yB", [128, KC, 512], BF16)
    sel, sel_b = sb3("sel", [128, 2])
    sch.dma(sel[:], sel_d[:, :], [], [sel_b])
    xr2 = [sb3("xr%d" % i, [128, D_MODEL]) for i in range(2)]
    z2 = [sb3("z%d" % i, [128, D_MODEL]) for i in range(2)]
    junk, junk_b = sb3("junk3", [128, D_MODEL], BF16)
    ss3, ss3_b = sb3("ss3", [128, 4])
    def load_y(blk):
        for c in range(KC):
            sch.dma(yA[:, c, :], yall[c][:, blk * 512:(blk + 1) * 512], [yall_b[c]], [yA_b])
            sch.dma(yB[:, c, :], yall[c][:, SH + blk * 512:SH + (blk + 1) * 512], [yall_b[c]], [yB_b])

    NT3 = SH // 128
    sch.dma(xr2[0][0][:], xres[0:128, :], [], [xr2[0][1]])
    load_y(0)
    for blk in range(SH // 512):
        yT, yT_b = yT2[blk % 2]
        op("pool", lambda e: e.tensor_scalar(out=yT[:], in0=yA[:], scalar1=sel[:, 0:1], scalar2=None, op0=ALU.mult),
           [yA_b, sel_b], [yT_b])
        op("dve", lambda e: e.scalar_tensor_tensor(out=yT[:], in0=yB[:], scalar=sel[:, 1:2], in1=yT[:], op0=ALU.mult,
                                                   op1=ALU.add), [yB_b, sel_b, yT_b], [yT_b])
        if blk + 1 < SH // 512:
            load_y(blk + 1)
        for ti in range(4):
            i = blk * 4 + ti
            xr, xr_b = xr2[i % 2]
            z, z_b = z2[i % 2]
            if i + 1 < NT3:
                sch.dma(xr2[(i + 1) % 2][0][:], xres[(i + 1) * 128:(i + 2) * 128, :], [], [xr2[(i + 1) % 2][1]])
            for nn in range(2):
                po, po_b = bank[(i % 2) * 2 + nn]
                for c in range(KC):
                    op("pe", lambda e: e.matmul(po[:], yT[:, c, ti * 128:(ti + 1) * 128], Wo[:, c, nn * 512:(nn + 1) * 512],
                                                start=(c == 0), stop=(c == KC - 1)), [yT_b, Wo_b], [po_b])
                op("dve", lambda e: e.tensor_tensor(out=z[:, nn * 512:(nn + 1) * 512], in0=po[:], in1=xr[:, nn * 512:(nn + 1) * 512],
                                                    op=ALU.add), [po_b, xr_b], [z_b])
            op("act", lambda e: e.activation(out=junk[:], in_=z[:], func=AF.Square, accum_out=ss3[:, 0:1]), [z_b], [junk_b, ss3_b])
            op("act", lambda e: e.activation(out=ss3[:, 1:2], in_=ss3[:, 0:1], func=AF.Ln, scale=1.0 / D_MODEL, bias=NORM_EPS),
               [ss3_b], [ss3_b])
            op("act", lambda e: e.activation(out=ss3[:, 2:3], in_=ss3[:, 1:2], func=AF.Exp, scale=-0.5), [ss3_b], [ss3_b])
            op("dve", lambda e: e.scalar_tensor_tensor(out=z[:], in0=z[:], scalar=ss3[:, 2:3], in1=fg[:], op0=ALU.mult,
                                                       op1=ALU.mult), [z_b, ss3_b, fg_b], [z_b])
            sch.dma(out[i * 128:(i + 1) * 128, :], z[:], [z_b], [out_b])
    sch.finish()
    return nc


def make_core_inputs(inputs, b, g, S):
    f = lambda a: np.ascontiguousarray(np.asarray(a, dtype=np.float32))
    w_in = np.asarray(inputs["w_in"])[0]
    SH = S // 2
    hs = slice(g * 256, (g + 1) * 256)

    def cols(base):
        return np.arange(base + g * 256, base + (g + 1) * 256)

    rw_cols = np.concatenate([cols(0), cols(512), cols(1024), np.arange(1536, 1664)])
    fx0 = 1664
    fox_cols = np.concatenate([cols(fx0), cols(fx0 + 512), cols(fx0 + 1024), np.arange(fx0 + 1536 + 4 * g, fx0 + 1536 + 4 * g + 4)])
    g0 = 1664 + 1544
    mu = np.asarray(inputs["rw_mu"])[0][rw_cols]
    rep = lambda v: np.broadcast_to(np.asarray(v, np.float32).reshape(1, -1), (128, np.asarray(v).size))
    pb1 = np.concatenate([rep(mu), rep(np.asarray(inputs["rw_w0"])[0][hs]), rep(np.asarray(inputs["rw_a0"])[0][hs]),
                          rep(np.asarray(inputs["rw_k_k"])[0][hs]), rep(np.asarray(inputs["rw_k_a"])[0][hs]),
                          rep(np.asarray(inputs["rw_r_k"])[0][4 * g:4 * g + 4].reshape(-1)),
                          rep(np.asarray(inputs["rw_gn_g"])[0][hs]), rep(np.asarray(inputs["rw_gn_b"])[0][hs])], axis=1)
    pb2 = np.concatenate([rep(np.asarray(inputs["fox_q_g"])[0]), rep(np.asarray(inputs["fox_k_g"])[0]),
                          rep(np.asarray(inputs["fox_b_f"])[0][4 * g:4 * g + 4])], axis=1)
    w_out = np.asarray(inputs["w_out"])[0]
    rows = []
    for c in range(8):
        for gg in range(2):
            base = (4 * gg + c) * 64 if c < 4 else 512 + (4 * gg + c - 4) * 64
            rows.append(np.arange(base, base + 64))
    w_outp = w_out[np.concatenate(rows)]
    xb = np.asarray(inputs["x"])[b]
    return {
        "x": f(xb),
        "xres": f(xb[g * SH:(g + 1) * SH]),
        "w_rw": f(w_in[:, rw_cols]),
        "w_fox": f(w_in[:, fox_cols]),
        "w_grw": f(w_in[:, cols(g0)]),
        "w_gfx": f(w_in[:, cols(g0 + 512)]),
        "w_outp": f(w_outp),
        "g_col": f(np.asarray(inputs["norm_g"])[0].reshape(KC, 128).T),
        "pb1": f(pb1),
        "pb2": f(pb2),
        "pb3": f(rep(np.asarray(inputs["final_g"]))),
        "w_up": f(np.asarray(inputs["rw_w_up"])[0][:, hs]),
        "a_up": f(np.asarray(inputs["rw_a_up"])[0][:, hs]),
        "sel": f(np.tile(np.array([[1.0 - g, float(g)]], np.float32), (128, 1))),
    }


def kernel(**inputs):
    x = np.asarray(inputs["x"])
    B, S, D = x.shape
    nc = build_nc(S)
    in_maps = [make_core_inputs(inputs, c // 2, c % 2, S) for c in range(2 * B)]
    res = run_bass_kernel_spmd(nc, in_maps, core_ids=list(range(2 * B)))
    out = np.empty((B, S, D), np.float32)
    SH = S // 2
    for c in range(2 * B):
        out[c // 2, (c % 2) * SH:(c % 2 + 1) * SH] = res.results[c]["out"]
    return out
```

```python
import numpy as np
import concourse.bass as bass
import concourse.mybir as mybir
from concourse.bass_utils import run_bass_kernel_spmd

F32 = mybir.dt.float32
BF16 = mybir.dt.bfloat16
AF = mybir.ActivationFunctionType
ALU = mybir.AluOpType
AX = mybir.AxisListType

D_MODEL = 1024
KC = 8
HD = 64
NH = 4
RWC = 896
FXC = 772
NORM_EPS = 1e-6
GN_EPS = 64e-5
N_DCH = 6


class Buf:
    __slots__ = ("name", "lw", "rd", "psum")

    def __init__(self, name, psum=False):
        self.name = name
        self.lw = None
        self.rd = {}
        self.psum = psum


class Sched:
    def __init__(self, nc):
        self.nc = nc
        self.eng = {"pe": nc.tensor, "act": nc.scalar, "dve": nc.vector, "pool": nc.gpsimd, "sp": nc.sync}
        self.sem = {}
        self.cnt = {}
        for k in self.eng:
            self.sem[k] = nc.semaphore("sem_" + k).__enter__()
            self.cnt[k] = 0
        for c in range(N_DCH):
            k = "d%d" % c
            self.sem[k] = nc.semaphore("sem_" + k).__enter__()
            self.cnt[k] = 0
        self.sem["cc"] = nc.semaphore("sem_cc").__enter__()
        self.cnt["cc"] = 0
        self.waited = {k: {} for k in self.eng}
        self.next_dch = 0
        self.limit = None
        self.total = 0
        self.log = []

    def _lim(self):
        self.total += 1
        if self.log is not None:
            import sys as _s
            fr = _s._getframe(2)
            self.log.append((self.total, fr.f_lineno))
        return self.limit is not None and self.total > self.limit

    def _val(self, k, i):
        return i * 16 if (k[0] == "d" and k != "dve") else i

    def _deps(self, reads, writes):
        deps = {}

        def need(k, i):
            if deps.get(k, 0) < i:
                deps[k] = i

        for b in reads:
            if b.lw is not None:
                need(*b.lw)
            if b.psum:
                for k, i in b.rd.items():
                    need(k, i)
        for b in writes:
            if b.lw is not None:
                need(*b.lw)
            for k, i in b.rd.items():
                need(k, i)
        return deps

    def _wait(self, e, deps):
        w = self.waited[e]
        for k, i in deps.items():
            if w.get(k, 0) < i:
                self.eng[e].wait_ge(self.sem[k], self._val(k, i))
                w[k] = i

    def _commit(self, me, reads, writes):
        k, i = me
        for b in reads:
            b.rd[k] = i
        for b in writes:
            b.lw = me
            b.rd = {}

    def op(self, e, fn, reads=(), writes=()):
        if self._lim():
            return
        deps = self._deps(reads, writes)
        if e == "pe":
            deps.pop("pe", None)
        self._wait(e, deps)
        ins = fn(self.eng[e])
        self.cnt[e] += 1
        ins.then_inc(self.sem[e], 1)
        self._commit((e, self.cnt[e]), reads, writes)

    def dma(self, out, in_, reads=(), writes=(), q="sp"):
        if self._lim():
            return
        c = "d%d" % self.next_dch
        self.next_dch = (self.next_dch + 1) % N_DCH
        deps = self._deps(reads, writes)
        if self.cnt[c] > 0:
            deps[c] = max(deps.get(c, 0), self.cnt[c])
        self._wait(q, deps)
        ins = self.eng[q].dma_start(out=out, in_=in_)
        self.cnt[c] += 1
        ins.then_inc(self.sem[c], 16)
        self._commit((c, self.cnt[c]), reads, writes)

    def collective(self, fn, reads=(), writes=()):
        if self._lim():
            return
        deps = self._deps(reads, writes)
        self._wait("pool", deps)
        ins = fn(self.eng["pool"])
        self.cnt["cc"] += 1
        ins.then_inc(self.sem["cc"], 1)
        self._commit(("cc", self.cnt["cc"]), reads, writes)

    def barrier(self):
        for e in self.eng:
            deps = {k: c for k, c in self.cnt.items() if c > 0 and k != e}
            self._wait(e, deps)

    def finish(self):
        deps = {k: c for k, c in self.cnt.items() if c > 0 and k != "sp"}
        self._wait("sp", deps)


def build_nc(S, debug=False, stop_after=9, limit=None):
    NT = S // 128
    NB = S // 512
    SH = S // 2
    nc = bass.Bass("TRN2", target_bir_lowering=False)

    def din(name, shape, dt=F32):
        return nc.dram_tensor(name, list(shape), dt, kind="ExternalInput").ap()

    x = din("x", [S, D_MODEL])
    xres = din("xres", [SH, D_MODEL])
    w_rw = din("w_rw", [D_MODEL, RWC])
    w_fox = din("w_fox", [D_MODEL, FXC])
    w_grw = din("w_grw", [D_MODEL, 256])
    w_gfx = din("w_gfx", [D_MODEL, 256])
    w_outp = din("w_outp", [D_MODEL, D_MODEL])
    g_col = din("g_col", [128, KC])
    pb1_d = din("pb1", [128, 2688])
    pb2_d = din("pb2", [128, 132])
    pb3_d = din("pb3", [128, D_MODEL])
    w_up_d = din("w_up", [64, 256])
    a_up_d = din("a_up", [64, 256])
    out = nc.dram_tensor("out", [SH, D_MODEL], F32, kind="ExternalOutput").ap()
    dbg_kind = "ExternalOutput" if debug else "Internal"
    ybuf = [nc.dram_tensor("ybuf%d" % c, [64, S], BF16, kind=dbg_kind).ap() for c in range(8)]
    yall = [nc.dram_tensor("yall%d" % c, [128, S], BF16, kind=dbg_kind).ap() for c in range(8)]
    sel_d = din("sel", [128, 2])
    ybuf_b = [Buf("ybuf%d" % c) for c in range(8)]
    yall_b = [Buf("yall%d" % c) for c in range(8)]
    out_b = Buf("out")

    sch = Sched(nc)
    sch.limit = limit
    op = sch.op

    class Scope:
        def __init__(self):
            self.stack = []

        def sb(self, name, shape, dt=F32):
            cm = nc.sbuf_tensor("s_" + name, list(shape), dt)
            t = cm.__enter__()
            self.stack.append(cm)
            return t, Buf(name)

        def ps(self, name, shape, dt=F32):
            cm = nc.psum_tensor("p_" + name, list(shape), dt)
            t = cm.__enter__()
            self.stack.append(cm)
            return t, Buf(name, psum=True)

        def close(self):
            while self.stack:
                self.stack.pop().__exit__(None, None, None)

    G = Scope()
    ident_bf, ident_bf_b = G.sb("ident_bf", [128, 128], BF16)
    ident_f, ident_f_b = G.sb("ident_f", [128, 128], F32)
    gcol, gcol_b = G.sb("gcol", [128, KC], F32)
    stage, stage_b = G.sb("stage", [128, 1024], F32)
    stage2, stage2_b = G.sb("stage2", [128, 1024], F32)

    def mk_ident(t, b):
        op("pool", lambda e: e.memset(t[:], 0.0), [], [b])
        op("pool", lambda e: e.affine_select(out=t[:], in_=t[:], pattern=[[-1, 128]], compare_op=ALU.not_equal,
                                             fill=1.0, base=0, channel_multiplier=1), [b], [b])

    mk_ident(ident_bf, ident_bf_b)
    mk_ident(ident_f, ident_f_b)
    sch.dma(gcol[:], g_col[:, :], [], [gcol_b])

    def load_weight(dst, dst_b, src, ncols, post=None):
        for c in range(KC):
            st, stb = (stage, stage_b) if c % 2 == 0 else (stage2, stage2_b)
            sch.dma(st[:, 0:ncols], src[c * 128:(c + 1) * 128, :], [], [stb])
            if post is None:
                op("dve", lambda e: e.tensor_scalar(out=dst[:, c, :], in0=st[:, 0:ncols], scalar1=gcol[:, c:c + 1],
                                                    scalar2=None, op0=ALU.mult), [stb, gcol_b], [dst_b])
            else:
                for (d2, d2b, pt, ptb) in post:
                    op("dve", lambda e: e.scalar_tensor_tensor(out=d2[:, c, :], in0=st[:, 0:ncols],
                                                               scalar=gcol[:, c:c + 1], in1=pt, op0=ALU.mult,
                                                               op1=ALU.mult), [stb, gcol_b, ptb], [d2b])

    def rmsnorm_to_hT(xt, xt_b, h, h_b, junk, junk_b, ss, ss_b, tpb, tpb_b, dst_ap_fn, dst_b):
        op("act", lambda e: e.activation(out=junk[:], in_=xt[:], func=AF.Square, accum_out=ss[:, 0:1]),
           [xt_b], [junk_b, ss_b])
        op("act", lambda e: e.activation(out=ss[:, 1:2], in_=ss[:, 0:1], func=AF.Ln, scale=1.0 / D_MODEL,
                                         bias=NORM_EPS), [ss_b], [ss_b])
        op("act", lambda e: e.activation(out=ss[:, 2:3], in_=ss[:, 1:2], func=AF.Exp, scale=-0.5), [ss_b], [ss_b])
        op("act", lambda e: e.activation(out=h[:], in_=xt[:], func=AF.Copy, scale=ss[:, 2:3]), [xt_b, ss_b], [h_b])
        tpv = tpb[:].rearrange("p (c t) -> p c t", c=KC)
        for c in range(KC):
            op("pe", lambda e: e.transpose(out=tpv[:, c, :], in_=h[:, c * 128:(c + 1) * 128], identity=ident_bf[:]),
               [h_b, ident_bf_b], [tpb_b])
        op("dve", lambda e: e.tensor_copy(out=dst_ap_fn(), in_=tpv), [tpb_b], [dst_b])

    P1 = Scope()
    Wa, Wa_b = P1.sb("Wa", [128, KC, RWC], BF16)
    Wb, Wb_b = P1.sb("Wb", [128, KC, RWC], BF16)
    Wg, Wg_b = P1.sb("Wg1", [128, KC, 256], BF16)
    pb1, pb1_b = P1.sb("pb1", [128, 2688], F32)
    omm, omm_b = P1.sb("omm", [128, RWC], F32)
    wup, wup_b = P1.sb("wup", [64, 256], F32)
    aup, aup_b = P1.sb("aup", [64, 256], F32)
    sch.dma(pb1[:], pb1_d[:, :], [], [pb1_b])
    sch.dma(wup[:], w_up_d[:, :], [], [wup_b])
    sch.dma(aup[:], a_up_d[:, :], [], [aup_b])
    MU = pb1[:, 0:896]
    W0 = pb1[:, 896:1152]
    A0 = pb1[:, 1152:1408]
    KK_ = pb1[:, 1408:1664]
    KA_ = pb1[:, 1664:1920]
    RK_ = pb1[:, 1920:2176]
    GNG = pb1[:, 2176:2432]
    GNB = pb1[:, 2432:2688]
    op("dve", lambda e: e.tensor_scalar(out=omm[:], in0=MU, scalar1=-1.0, scalar2=1.0, op0=ALU.mult, op1=ALU.add),
       [pb1_b], [omm_b])
    load_weight(None, None, w_rw, RWC, post=[(Wa, Wa_b, omm[:], omm_b), (Wb, Wb_b, MU, pb1_b)])
    load_weight(Wg, Wg_b, w_grw, 256)

    tri_i, tri_i_b = P1.sb("tri_i", [128, 128], F32)
    tri_s, tri_s_b = P1.sb("tri_s", [128, 128], F32)
    tri_r, tri_r_b = P1.sb("tri_r", [128, 128], F32)
    mask4, mask4_b = P1.sb("mask4", [128, 512], F32)
    maskr4, maskr4_b = P1.sb("maskr4", [128, 512], F32)
    ones_c, ones_c_b = P1.sb("ones_c", [128, 1], F32)

    def mk_tri(t, b, cm, pat, cmp_op):
        op("pool", lambda e: e.memset(t[:], 1.0), [], [b])
        op("pool", lambda e: e.affine_select(out=t[:], in_=t[:], pattern=[[pat, 128]], compare_op=cmp_op, fill=0.0,
                                             base=0, channel_multiplier=cm), [b], [b])
        op("pool", lambda e: e.memset(t[0:64, 64:128], 0.0), [b], [b])
        op("pool", lambda e: e.memset(t[64:128, 0:64], 0.0), [b], [b])

    mk_tri(tri_i, tri_i_b, -1, 1, ALU.is_ge)
    mk_tri(tri_s, tri_s_b, -1, 1, ALU.is_gt)
    mk_tri(tri_r, tri_r_b, 1, -1, ALU.is_gt)
    op("pool", lambda e: e.memset(ones_c[:], 1.0), [], [ones_c_b])
    for q_, (src, srcb) in enumerate([(tri_s, tri_s_b), (tri_i, tri_i_b), (tri_s, tri_s_b), (tri_i, tri_i_b)]):
        op("pool", lambda e: e.tensor_copy(out=mask4[:, q_ * 128:(q_ + 1) * 128], in_=src[:]), [srcb], [mask4_b])
        op("pool", lambda e: e.tensor_copy(out=maskr4[:, q_ * 128:(q_ + 1) * 128], in_=tri_r[:]), [tri_r_b], [maskr4_b])

    bank = [P1.ps("bank%d" % i, [128, 512], F32) for i in range(7)]
    bankT, bankT_b = P1.ps("bankT", [128, 1024], BF16)

    def sb(name, shape, dt=F32):
        return P1.sb(name, shape, dt)

    xt2 = [sb("xt%d" % i, [128, 1024]) for i in range(2)]
    h, h_b = sb("h", [128, 1024], BF16)
    junk, junk_b = sb("junk", [128, 1024], BF16)
    ss, ss_b = sb("ss", [128, 4])
    hT2 = [sb("hT%d" % i, [128, KC, 129], BF16) for i in range(2)]
    r32, r32_b = sb("r32", [128, 256])
    k32, k32_b = sb("k32", [128, 256])
    v32, v32_b = sb("v32", [128, 256])
    lt32, lt32_b = sb("lt32", [128, 128])
    lT, lT_b = sb("lT", [64, 2, 128])
    lwn, lwn_b = sb("lwn", [128, 256])
    t1, t1_b = sb("t1", [128, 256])
    t2, t2_b = sb("t2", [128, 256])
    a_, a_b = sb("a_", [128, 256])
    kk, kk_b = sb("kk", [128, 256])
    km, km_b = sb("km", [128, 256])
    ba, ba_b = sb("ba", [128, 256])
    s4, s4_b = sb("s4", [128, 16])
    ecum, ecum_b = sb("ecum", [128, 256])
    encum, encum_b = sb("encum", [128, 256])
    eprev, eprev_b = sb("eprev", [128, 256])
    erc, erc_b = sb("erc", [128, 256])
    wc, wc_b = sb("wc", [64, 8])
    tl4 = sb("tl4", [128, 4, 256], BF16)
    bh, bh_b = sb("bh", [128, 256], BF16)
    kh, kh_b = sb("kh", [128, 256], BF16)
    vb, vb_b = sb("vb", [128, 256], BF16)
    bhm, bhm_b = sb("bhm", [128, 2, 256], BF16)
    khm, khm_b = sb("khm", [128, 2, 256], BF16)
    cmask, cmask_b = sb("cmask", [128, 2])
    op("pool", lambda e: e.memset(bhm[:], 0.0), [], [bhm_b])
    op("pool", lambda e: e.memset(khm[:], 0.0), [], [khm_b])
    op("pool", lambda e: e.memset(cmask[:], 0.0), [], [cmask_b])
    op("pool", lambda e: e.memset(cmask[0:64, 0:1], 1.0), [cmask_b], [cmask_b])
    op("pool", lambda e: e.memset(cmask[64:128, 1:2], 1.0), [cmask_b], [cmask_b])
    TT, TT_b = sb("TT", [64, 16, 128], BF16)
    A_sb = [sb("A%d" % hh, [128, 512], BF16) for hh in range(NH)]
    Ni2 = [sb("Ni%d" % i, [128, 512], BF16) for i in range(2)]
    NiT2 = [sb("NiT%d" % i, [128, 512], BF16) for i in range(2)]
    X32, X32_b = sb("X32", [128, 512])
    Xb, Xb_b = sb("Xb", [128, 512], BF16)
    GTs, GTs_b = sb("GTs", [64, 1024])
    M0T, M0T_b = sb("M0T", [64, 512])
    Nst, Nst_b = sb("Nst", [64, 512])
    ST2 = [sb("ST%d" % i, [64, 256]) for i in range(2)]
    stt, stt_b = sb("stt", [64, 256])
    y32, y32_b = sb("y32", [128, 256])
    sg, sg_b = sb("sg", [128, 256])
    bon, bon_b = sb("bon", [128, 256])
    yg, yg_b = sb("yg", [128, 256], BF16)
    ygT, ygT_b = sb("ygT", [128, 2, 512], BF16)

    op("dve", lambda e: e.memset(ST2[0][0][:], 0.0), [], [ST2[0][1]])
    op("dve", lambda e: e.memset(GTs[:], 0.0), [], [GTs_b])
    op("dve", lambda e: e.memset(hT2[0][0][:, :, 0:1], 0.0), [], [hT2[0][1]])

    def v3(ap, hh=NH):
        return ap.rearrange("p (h d) -> p h d", h=hh)

    def bc(ap_small, hh=NH, d=HD):
        return ap_small.unsqueeze(2).to_broadcast([128, hh, d])

    st_cur = 0
    for i in range(NT):
        xt, xt_b = xt2[i % 2]
        hT, hT_b = hT2[i % 2]
        hTp, hTp_b = hT2[(i + 1) % 2]
        if i == 0:
            sch.dma(xt[:], x[0:128, :], [], [xt_b])
        if i + 1 < NT:
            sch.dma(xt2[(i + 1) % 2][0][:], x[(i + 1) * 128:(i + 2) * 128, :], [], [xt2[(i + 1) % 2][1]])
        rmsnorm_to_hT(xt, xt_b, h, h_b, junk, junk_b, ss, ss_b, bankT, bankT_b, lambda: hT[:, :, 1:129], hT_b)
        if i > 0:
            op("pool", lambda e: e.tensor_copy(out=hT[:, :, 0:1], in_=hTp[:, :, 128:129]), [hTp_b], [hT_b])
        pr0, pr0_b = bank[0]
        pr1, pr1_b = bank[1]
        for (pt, ptb, n0, n1) in ((pr0, pr0_b, 0, 512), (pr1, pr1_b, 512, 896)):
            for c in range(KC):
                op("pe", lambda e: e.matmul(pt[:, 0:n1 - n0], hT[:, c, 1:129], Wa[:, c, n0:n1], start=(c == 0), stop=False),
                   [hT_b, Wa_b], [ptb])
            for c in range(KC):
                op("pe", lambda e: e.matmul(pt[:, 0:n1 - n0], hT[:, c, 0:128], Wb[:, c, n0:n1], start=False, stop=(c == KC - 1)),
                   [hT_b, Wb_b], [ptb])
        pg, pg_b = bank[2]
        for c in range(KC):
            op("pe", lambda e: e.matmul(pg[:, 0:256], hT[:, c, 1:129], Wg[:, c, :], start=(c == 0), stop=(c == KC - 1)),
               [hT_b, Wg_b], [pg_b])
        op("act", lambda e: e.activation(out=r32[:], in_=pr0[:, 0:256], func=AF.Copy), [pr0_b], [r32_b])
        op("dve", lambda e: e.tensor_copy(out=k32[:], in_=pr0[:, 256:512]), [pr0_b], [k32_b])
        op("act", lambda e: e.activation(out=v32[:], in_=pr1[:, 0:256], func=AF.Copy), [pr1_b], [v32_b])
        op("act", lambda e: e.activation(out=lt32[:, 0:64], in_=pr1[:, 256:320], func=AF.Exp, scale=-2.0), [pr1_b], [lt32_b])
        op("act", lambda e: e.activation(out=lt32[:, 0:64], in_=lt32[:, 0:64], func=AF.Ln, bias=1.0), [lt32_b], [lt32_b])
        op("act", lambda e: e.activation(out=lt32[:, 0:64], in_=lt32[:, 0:64], func=AF.Exp, scale=-1.0), [lt32_b], [lt32_b])
        op("dve", lambda e: e.tensor_scalar(out=lt32[:, 0:64], in0=lt32[:, 0:64], scalar1=2.0, scalar2=-1.0, op0=ALU.mult,
                                            op1=ALU.add), [lt32_b], [lt32_b])
        op("dve", lambda e: e.tensor_copy(out=lt32[:, 64:128], in_=pr1[:, 320:384]), [pr1_b, lt32_b], [lt32_b])
        ptl, ptl_b = bank[3]
        for q_ in range(2):
            op("pe", lambda e: e.transpose(out=ptl[0:64, q_ * 128:(q_ + 1) * 128], in_=lt32[:, q_ * 64:(q_ + 1) * 64],
                                           identity=ident_f[:]), [lt32_b, ident_f_b], [ptl_b])
        op("dve", lambda e: e.tensor_copy(out=lT[:].rearrange("p a t -> p (a t)"), in_=ptl[0:64, 0:256]), [ptl_b], [lT_b])
        pl, pl_b = bank[4]
        op("pe", lambda e: e.matmul(pl[:, 0:256], lT[:, 0, :], wup[:], start=True, stop=True), [lT_b, wup_b], [pl_b])
        op("pe", lambda e: e.matmul(pl[:, 256:512], lT[:, 1, :], aup[:], start=True, stop=True), [lT_b, aup_b], [pl_b])
        op("dve", lambda e: e.tensor_tensor(out=t1[:], in0=pl[:, 0:256], in1=W0, op=ALU.add), [pl_b, pb1_b], [t1_b])
        op("act", lambda e: e.activation(out=t1[:], in_=t1[:], func=AF.Exp, scale=-1.0), [t1_b], [t1_b])
        op("act", lambda e: e.activation(out=t1[:], in_=t1[:], func=AF.Ln, bias=1.0), [t1_b], [t1_b])
        op("act", lambda e: e.activation(out=lwn[:], in_=t1[:], func=AF.Exp, scale=-1.0, bias=-0.5), [t1_b], [lwn_b])
        op("dve", lambda e: e.tensor_tensor(out=t2[:], in0=pl[:, 256:512], in1=A0, op=ALU.add), [pl_b, pb1_b], [t2_b])
        op("act", lambda e: e.activation(out=t2[:], in_=t2[:], func=AF.Exp, scale=-1.0), [t2_b], [t2_b])
        op("act", lambda e: e.activation(out=t2[:], in_=t2[:], func=AF.Ln, bias=1.0), [t2_b], [t2_b])
        op("act", lambda e: e.activation(out=a_[:], in_=t2[:], func=AF.Exp, scale=-1.0), [t2_b], [a_b])
        op("dve", lambda e: e.tensor_tensor(out=kk[:], in0=k32[:], in1=KK_, op=ALU.mult), [k32_b, pb1_b], [kk_b])
        op("dve", lambda e: e.tensor_tensor(out=t1[:], in0=kk[:], in1=kk[:], op=ALU.mult), [kk_b], [t1_b])
        op("dve", lambda e: e.tensor_reduce(out=s4[:, 0:4], in_=v3(t1[:]), axis=AX.X, op=ALU.add), [t1_b], [s4_b])
        op("dve", lambda e: e.tensor_scalar(out=s4[:, 0:4], in0=s4[:, 0:4], scalar1=1e-24, scalar2=None, op0=ALU.max),
           [s4_b], [s4_b])
        op("act", lambda e: e.activation(out=s4[:, 0:4], in_=s4[:, 0:4], func=AF.Ln), [s4_b], [s4_b])
        op("act", lambda e: e.activation(out=s4[:, 0:4], in_=s4[:, 0:4], func=AF.Exp, scale=-0.5), [s4_b], [s4_b])
        op("dve", lambda e: e.tensor_tensor(out=v3(kk[:]), in0=v3(kk[:]), in1=bc(s4[:, 0:4]), op=ALU.mult),
           [kk_b, s4_b], [kk_b])
        op("dve", lambda e: e.scalar_tensor_tensor(out=t1[:], in0=a_[:], scalar=-1.0, in1=KA_, op0=ALU.add, op1=ALU.mult),
           [a_b, pb1_b], [t1_b])
        op("dve", lambda e: e.scalar_tensor_tensor(out=km[:], in0=t1[:], scalar=1.0, in1=k32[:], op0=ALU.add, op1=ALU.mult),
           [t1_b, k32_b], [km_b])
        op("dve", lambda e: e.tensor_tensor(out=ba[:], in0=kk[:], in1=a_[:], op=ALU.mult), [kk_b, a_b], [ba_b])
        op("dve", lambda e: e.tensor_tensor(out=t1[:], in0=r32[:], in1=km[:], op=ALU.mult), [r32_b, km_b], [t1_b])
        op("dve", lambda e: e.tensor_tensor(out=t1[:], in0=t1[:], in1=RK_, op=ALU.mult), [t1_b, pb1_b], [t1_b])
        op("dve", lambda e: e.tensor_reduce(out=s4[:, 4:8], in_=v3(t1[:]), axis=AX.X, op=ALU.add), [t1_b], [s4_b])
        op("dve", lambda e: e.tensor_tensor(out=v3(bon[:]), in0=v3(v32[:]), in1=bc(s4[:, 4:8]), op=ALU.mult),
           [v32_b, s4_b], [bon_b])
        op("act", lambda e: e.activation(out=t2[:], in_=pg[:, 0:256], func=AF.Exp, scale=-1.0), [pg_b], [t2_b])
        op("act", lambda e: e.activation(out=t2[:], in_=t2[:], func=AF.Ln, bias=1.0), [t2_b], [t2_b])
        op("act", lambda e: e.activation(out=t2[:], in_=t2[:], func=AF.Exp, scale=-1.0), [t2_b], [t2_b])
        op("dve", lambda e: e.tensor_tensor(out=sg[:], in0=t2[:], in1=pg[:, 0:256], op=ALU.mult), [t2_b, pg_b], [sg_b])
        pc0, pc0_b = bank[5]
        pc1, pc1_b = bank[6]
        op("pe", lambda e: e.matmul(pc0[:, 0:256], tri_i[:], lwn[:], start=True, stop=True), [tri_i_b, lwn_b], [pc0_b])
        op("pe", lambda e: e.matmul(pc0[:, 256:512], tri_s[:], lwn[:], start=True, stop=True), [tri_s_b, lwn_b], [pc0_b])
        op("pe", lambda e: e.matmul(pc1[:, 0:256], tri_r[:], lwn[:], start=True, stop=True), [tri_r_b, lwn_b], [pc1_b])
        for cch in range(2):
            for hh in range(NH):
                col = 256 + cch * 4 + hh
                op("pe", lambda e: e.matmul(pc1[0:64, col:col + 1], lwn[cch * 64:(cch + 1) * 64, hh * 64:(hh + 1) * 64],
                                            ones_c[cch * 64:(cch + 1) * 64, :], start=True, stop=True),
                   [lwn_b, ones_c_b], [pc1_b])
        op("act", lambda e: e.activation(out=ecum[:], in_=pc0[:, 0:256], func=AF.Exp, scale=-1.0), [pc0_b], [ecum_b])
        op("act", lambda e: e.activation(out=encum[:], in_=pc0[:, 0:256], func=AF.Exp), [pc0_b], [encum_b])
        op("act", lambda e: e.activation(out=eprev[:], in_=pc0[:, 256:512], func=AF.Exp, scale=-1.0), [pc0_b], [eprev_b])
        op("act", lambda e: e.activation(out=erc[:], in_=pc1[:, 0:256], func=AF.Exp, scale=-1.0), [pc1_b], [erc_b])
        op("act", lambda e: e.activation(out=wc[:], in_=pc1[0:64, 256:264], func=AF.Exp, scale=-1.0), [pc1_b], [wc_b])
        tl, tl_b = tl4
        op("dve", lambda e: e.scalar_tensor_tensor(out=tl[:, 0, :], in0=kk[:], scalar=-1.0, in1=eprev[:],
                                                   op0=ALU.mult, op1=ALU.mult), [kk_b, eprev_b], [tl_b])
        op("dve", lambda e: e.tensor_tensor(out=tl[:, 1, :], in0=r32[:], in1=ecum[:], op=ALU.mult), [r32_b, ecum_b], [tl_b])
        op("dve", lambda e: e.tensor_tensor(out=tl[:, 2, :], in0=ba[:], in1=encum[:], op=ALU.mult), [ba_b, encum_b], [tl_b])
        op("dve", lambda e: e.tensor_tensor(out=tl[:, 3, :], in0=km[:], in1=encum[:], op=ALU.mult), [km_b, encum_b], [tl_b])
        for cch in range(2):
            rs = slice(cch * 64, (cch + 1) * 64)
            op("pool", lambda e: e.tensor_tensor(out=bhm[rs, cch, :], in0=ba[rs, :], in1=erc[rs, :], op=ALU.mult),
               [ba_b, erc_b], [bhm_b])
            op("pool", lambda e: e.tensor_tensor(out=khm[rs, cch, :], in0=km[rs, :], in1=erc[rs, :], op=ALU.mult),
               [km_b, erc_b], [khm_b])
        op("pool", lambda e: e.tensor_copy(out=vb[:], in_=v32[:]), [v32_b], [vb_b])
        TTp = bankT[0:64, :].rearrange("p (a t) -> p a t", a=8)
        for rnd in range(2):
            for tq in range(2):
                ty = rnd * 2 + tq
                for hh in range(NH):
                    op("pe", lambda e: e.transpose(out=TTp[:, tq * 4 + hh, :], in_=tl[:, ty, hh * 64:(hh + 1) * 64],
                                                   identity=ident_bf[:]), [tl_b, ident_bf_b], [bankT_b])
            op("act", lambda e: e.activation(out=TT[:, rnd * 8:(rnd + 1) * 8, :], in_=TTp, func=AF.Copy), [bankT_b], [TT_b])

        def kT(ty, hh):
            return TT[:, ty * 4 + hh, :]

        pn, pn_b = bank[2]
        for hh in range(NH):
            pa, pa_b = bank[hh % 2]
            rhs2 = TT[:, hh:hh + 5:4, :]
            op("pe", lambda e: e.matmul(pa[:, 0:256], kT(2, hh), rhs2, start=True, stop=True), [TT_b], [pa_b])
            op("pe", lambda e: e.matmul(pa[:, 256:512], kT(3, hh), rhs2, start=True, stop=True), [TT_b], [pa_b])
            op("pe", lambda e: e.matmul(pn[:, hh * 128:(hh + 1) * 128], kT(0, hh), kT(2, hh), start=True, stop=True),
               [TT_b], [pn_b])
            A, A_b = A_sb[hh]
            op("dve", lambda e: e.tensor_tensor(out=A[:], in0=pa[:], in1=mask4[:], op=ALU.mult), [pa_b, mask4_b], [A_b])
        Ni, Ni_b = Ni2[0]
        op("dve", lambda e: e.tensor_tensor(out=Ni[:], in0=pn[:], in1=maskr4[:], op=ALU.mult), [pn_b, maskr4_b], [Ni_b])
        px, px_b = bank[4]
        for hh in range(NH):
            A, A_b = A_sb[hh]
            op("pe", lambda e: e.matmul(px[:, hh * 128:hh * 128 + 64], ident_bf[:], tl[:, 0, hh * 64:(hh + 1) * 64],
                                        start=(hh == 0), stop=False, skip_group_check=True), [tl_b, ident_bf_b], [px_b])
            op("pe", lambda e: e.matmul(px[:, hh * 128 + 64:(hh + 1) * 128], A[:, 256:384], vb[:, hh * 64:(hh + 1) * 64],
                                        start=False, stop=False, skip_group_check=True), [A_b, vb_b], [px_b])
        op("dve", lambda e: e.tensor_copy(out=Xb[:], in_=px[:]), [px_b], [Xb_b])
        for s_ in range(6):
            cur = s_ % 2
            Ni, Ni_b = Ni2[cur]
            NiT, NiT_b = NiT2[cur]
            for hh in range(NH):
                if s_ == 0:
                    lt_, ltb = A_sb[hh][0][:, 0:128], A_sb[hh][1]
                else:
                    lt_, ltb = NiT[:, hh * 128:(hh + 1) * 128], NiT_b
                op("pe", lambda e: e.matmul(px[:, hh * 128:(hh + 1) * 128], lt_, Xb[:, hh * 128:(hh + 1) * 128],
                                            start=False, stop=(s_ == 5), skip_group_check=True), [ltb, Xb_b], [px_b])
            if s_ < 5:
                ps1, ps1_b = bank[5]
                ps2, ps2_b = bank[6]
                Nn_, Nn_b = Ni2[1 - cur]
                NnT, NnT_b = NiT2[1 - cur]
                for hh in range(NH):
                    if s_ == 0:
                        lt_, ltb = A_sb[hh][0][:, 0:128], A_sb[hh][1]
                    else:
                        lt_, ltb = NiT[:, hh * 128:(hh + 1) * 128], NiT_b
                    n_ = Ni[:, hh * 128:(hh + 1) * 128]
                    op("pe", lambda e: e.matmul(ps2[:, hh * 128:(hh + 1) * 128], n_, lt_, start=True, stop=True),
                       [ltb, Ni_b], [ps2_b])
                    op("pe", lambda e: e.matmul(ps1[:, hh * 128:(hh + 1) * 128], lt_, n_, start=True, stop=True),
                       [ltb, Ni_b], [ps1_b])
            op("dve", lambda e: e.tensor_copy(out=Xb[:], in_=px[:]), [px_b], [Xb_b])
            if s_ < 5:
                op("dve", lambda e: e.tensor_copy(out=NnT[:], in_=ps2[:]), [ps2_b], [NnT_b])
                op("act", lambda e: e.activation(out=Nn_[:], in_=ps1[:], func=AF.Copy), [ps1_b], [Nn_b])
        Xb3 = Xb[:].rearrange("p (h c) -> p h c", h=NH)
        pgt, pgt_b = bank[0]
        tlr = tl[:, 1, :]
        for hh in range(NH):
            A, A_b = A_sb[hh]
            op("pe", lambda e: e.matmul(pgt[0:64, hh * 128:(hh + 1) * 128], Xb3[:, hh, 0:64], A[:, 128:256],
                                        start=True, stop=False), [Xb_b, A_b], [pgt_b])
            op("pe", lambda e: e.matmul(pgt[0:64, hh * 128:(hh + 1) * 128], tlr[:, hh * 64:(hh + 1) * 64], ident_bf[:],
                                        start=False, stop=True), [tl_b, ident_bf_b], [pgt_b])
        pgt5 = pgt[0:64, :].rearrange("p (h c t) -> p h c t", h=NH, c=2)
        GT5 = GTs[:].rearrange("p (a h c t) -> p a h c t", a=2, h=NH, c=2)
        for cch in range(2):
            op("act", lambda e: e.activation(out=GT5[:, cch, :, cch, :], in_=pgt5[:, :, cch, :], func=AF.Copy), [pgt_b], [GTs_b])
        pm, pm_b = bank[1]
        pn2, pn2_b = bank[5]
        for cch in range(2):
            for hh in range(NH):
                cs = slice((cch * 4 + hh) * 64, (cch * 4 + hh + 1) * 64)
                hs = slice(hh * 64, (hh + 1) * 64)
                op("pe", lambda e: e.matmul(pm[0:64, cs], Xb3[:, hh, 0:64], bhm[:, cch, hs], start=True, stop=True),
                   [Xb_b, bhm_b], [pm_b])
                op("pe", lambda e: e.matmul(pn2[0:64, cs], bhm[:, cch, hs], Xb3[:, hh, 64:128], start=True, stop=False),
                   [Xb_b, bhm_b], [pn2_b])
                op("pe", lambda e: e.matmul(pn2[0:64, cs], khm[:, cch, hs], vb[:, hs], start=False, stop=True),
                   [khm_b, vb_b], [pn2_b])
        op("act", lambda e: e.activation(out=M0T[:], in_=pm[0:64, :], func=AF.Copy), [pm_b], [M0T_b])
        op("dve", lambda e: e.tensor_copy(out=Nst[:], in_=pn2[0:64, :]), [pn2_b], [Nst_b])
        py, py_b = bank[3]
        for hh in range(NH):
            A, A_b = A_sb[hh]
            hs = slice(hh * 64, (hh + 1) * 64)
            op("pe", lambda e: e.matmul(py[:, hs], A[:, 128:256], Xb3[:, hh, 64:128], start=(hh == 0), stop=False,
                                        skip_group_check=True), [A_b, Xb_b], [py_b])
            op("pe", lambda e: e.matmul(py[:, hs], A[:, 384:512], vb[:, hs], start=False, stop=False, skip_group_check=True),
               [A_b, vb_b], [py_b])
        pst, pst_b = bank[6]
        for cch in range(2):
            ST, ST_b = ST2[st_cur]
            STn, STn_b = ST2[1 - st_cur]
            for hh in range(NH):
                hs = slice(hh * 64, (hh + 1) * 64)
                op("pe", lambda e: e.matmul(py[:, hs], GTs[:, cch * 512 + hh * 128: cch * 512 + (hh + 1) * 128],
                                            ST[:, hs], start=False, stop=True, skip_group_check=True), [GTs_b, ST_b], [py_b])
            for hh in range(NH):
                hs = slice(hh * 64, (hh + 1) * 64)
                cs = slice((cch * 4 + hh) * 64, (cch * 4 + hh + 1) * 64)
                op("pe", lambda e: e.matmul(pst[0:64, hs], M0T[:, cs], ST[:, hs], start=True, stop=True), [M0T_b, ST_b], [pst_b])
            op("dve", lambda e: e.tensor_tensor(out=v3(stt[:]), in0=v3(ST[:]),
                                                in1=wc[:, cch * 4:(cch + 1) * 4].unsqueeze(2).to_broadcast([64, NH, HD]),
                                                op=ALU.mult), [ST_b, wc_b], [stt_b])
            op("dve", lambda e: e.tensor_tensor(out=stt[:], in0=stt[:], in1=Nst[:, cch * 256:(cch + 1) * 256], op=ALU.add),
               [stt_b, Nst_b], [stt_b])
            op("dve", lambda e: e.tensor_tensor(out=STn[:], in0=stt[:], in1=pst[0:64, 0:256], op=ALU.add),
               [stt_b, pst_b], [STn_b])
            st_cur = 1 - st_cur
        op("act", lambda e: e.activation(out=y32[:], in_=py[:, 0:256], func=AF.Copy), [py_b], [y32_b])
        op("dve", lambda e: e.tensor_reduce(out=s4[:, 8:12], in_=v3(y32[:]), axis=AX.X, op=ALU.add), [y32_b], [s4_b])
        op("dve", lambda e: e.tensor_scalar(out=s4[:, 8:12], in0=s4[:, 8:12], scalar1=1.0 / HD, scalar2=None, op0=ALU.mult),
           [s4_b], [s4_b])
        op("dve", lambda e: e.tensor_tensor(out=v3(y32[:]), in0=v3(y32[:]), in1=bc(s4[:, 8:12]), op=ALU.subtract),
           [y32_b, s4_b], [y32_b])
        op("dve", lambda e: e.tensor_tensor(out=t1[:], in0=y32[:], in1=y32[:], op=ALU.mult), [y32_b], [t1_b])
        op("dve", lambda e: e.tensor_reduce(out=s4[:, 12:16], in_=v3(t1[:]), axis=AX.X, op=ALU.add), [t1_b], [s4_b])
        op("act", lambda e: e.activation(out=s4[:, 12:16], in_=s4[:, 12:16], func=AF.Ln, scale=1.0 / HD, bias=GN_EPS),
           [s4_b], [s4_b])
        op("act", lambda e: e.activation(out=s4[:, 12:16], in_=s4[:, 12:16], func=AF.Exp, scale=-0.5), [s4_b], [s4_b])
        op("dve", lambda e: e.tensor_tensor(out=v3(y32[:]), in0=v3(y32[:]), in1=bc(s4[:, 12:16]), op=ALU.mult),
           [y32_b, s4_b], [y32_b])
        op("dve", lambda e: e.tensor_tensor(out=y32[:], in0=y32[:], in1=GNG, op=ALU.mult), [y32_b, pb1_b], [y32_b])
        op("dve", lambda e: e.tensor_tensor(out=y32[:], in0=y32[:], in1=GNB, op=ALU.add), [y32_b, pb1_b], [y32_b])
        op("dve", lambda e: e.tensor_tensor(out=y32[:], in0=y32[:], in1=bon[:], op=ALU.add), [y32_b, bon_b], [y32_b])
        op("dve", lambda e: e.tensor_tensor(out=yg[:], in0=y32[:], in1=sg[:], op=ALU.mult), [y32_b, sg_b], [yg_b])
        ygp = bankT[:, 0:256].rearrange("p (a t) -> p a t", a=2)
        for q_ in range(2):
            op("pe", lambda e: e.transpose(out=ygp[:, q_, :], in_=yg[:, q_ * 128:(q_ + 1) * 128], identity=ident_bf[:]),
               [yg_b, ident_bf_b], [bankT_b])
        ti = i % 4
        op("act", lambda e: e.activation(out=ygT[:, :, ti * 128:(ti + 1) * 128], in_=ygp, func=AF.Copy), [bankT_b], [ygT_b])
        if ti == 3:
            t0 = (i // 4) * 512
            for hd in range(NH):
                sch.dma(ybuf[hd][:, t0:t0 + 512], ygT[(hd % 2) * 64:(hd % 2 + 1) * 64, hd // 2, :], [ygT_b], [ybuf_b[hd]])

    sch.barrier()
    P1.close()

    def gather(c):
        sch.collective(lambda e: e.collective_compute("AllGather", ALU.bypass, replica_groups=[[0, 1], [2, 3], [4, 5], [6, 7]],
                                                      ins=[ybuf[c][:, :]], outs=[yall[c][:, :]]), [ybuf_b[c]], [yall_b[c]])

    if stop_after > 2:
        for c in range(4):
            gather(c)
    if stop_after <= 1:
        sch.finish()
        return nc

    P2 = Scope()

    def sb2(name, shape, dt=F32):
        return P2.sb(name, shape, dt)

    Wf, Wf_b = sb2("Wf", [128, KC, FXC], BF16)
    Wg2, Wg2_b = sb2("Wg2", [128, KC, 256], BF16)
    pb2, pb2_b = sb2("pb2", [128, 132])
    sch.dma(pb2[:], pb2_d[:, :], [], [pb2_b])
    load_weight(Wf, Wf_b, w_fox, FXC)
    load_weight(Wg2, Wg2_b, w_gfx, 256)
    QG = pb2[:, 0:64]
    KG = pb2[:, 64:128]
    BFB = pb2[:, 128:132]
    qgs, qgs_b = sb2("qgs", [128, 64])
    op("dve", lambda e: e.tensor_scalar(out=qgs[:], in0=QG, scalar1=HD ** -0.5, scalar2=None, op0=ALU.mult), [pb2_b], [qgs_b])
    tri_f, tri_f_b = sb2("tri_f", [128, 128])
    sel127, sel127_b = sb2("sel127", [128, 128])
    ones_f, ones_f_b = sb2("ones_f", [128, 64])
    op("pool", lambda e: e.memset(tri_f[:], 1.0), [], [tri_f_b])
    op("pool", lambda e: e.affine_select(out=tri_f[:], in_=tri_f[:], pattern=[[1, 128]], compare_op=ALU.is_ge, fill=0.0,
                                         base=0, channel_multiplier=-1), [tri_f_b], [tri_f_b])
    op("pool", lambda e: e.memset(sel127[:], 1.0), [], [sel127_b])
    op("pool", lambda e: e.affine_select(out=sel127[:], in_=sel127[:], pattern=[[0, 128]], compare_op=ALU.is_ge, fill=0.0,
                                         base=-127, channel_multiplier=1), [sel127_b], [sel127_b])
    op("pool", lambda e: e.memset(ones_f[:], 1.0), [], [ones_f_b])

    KT, KT_b = sb2("KT", [70, NH, S], BF16)
    vaug, vaug_b = sb2("vaug", [128, NT, NH, 65], BF16)
    op("pool", lambda e: e.memset(vaug[:], 1.0), [], [vaug_b])
    bank = [P2.ps("bank2_%d" % i, [128, 512], F32) for i in range(7)]
    bankT, bankT_b = P2.ps("bankT2", [128, 1024], BF16)
    xt2 = [sb2("xtb%d" % i, [128, 1024]) for i in range(2)]
    h, h_b = sb2("hb", [128, 1024], BF16)
    junk, junk_b = sb2("junkb", [128, 1024], BF16)
    ss, ss_b = sb2("ssb", [128, 4])
    hTb, hTb_b = sb2("hTb", [128, KC, 512], BF16)
    qk32, qk32_b = sb2("qk32", [128, 512])
    sq, sq_b = sb2("sq", [128, 512])
    s8, s8_b = sb2("s8", [128, 8])
    cn2 = [sb2("cn%d" % i, [128, 4]) for i in range(2)]
    z4, z4_b = sb2("z4", [128, 4])
    spl, spl_b = sb2("spl", [128, 3, 4], BF16)
    rr, rr_b = sb2("rr", [128, 4])
    qaug2 = [sb2("qaug%d" % i, [128, NH, 70], BF16) for i in range(2)]
    kaug2 = [sb2("kaug%d" % i, [128, NH, 70], BF16) for i in range(2)]
    QT, QT_b = sb2("QT", [70, NH, 512], BF16)
    sgT, sgT_b = sb2("sgT", [64, NH, 512], BF16)
    eg, eg_b = sb2("eg", [64, 512])
    pT3 = [sb2("pT%d" % i, [128, 512], BF16) for i in range(3)]
    osb, osb_b = sb2("osb", [65, 512])
    rec, rec_b = sb2("rec", [65, 512])
    yt32, yt32_b = sb2("yt32", [64, 512])
    yfT2 = [sb2("yfT%d" % i, [64, 512], BF16) for i in range(2)]
    for i_ in range(2):
        op("pool", lambda e: e.memset(qaug2[i_][0][:], 1.0), [], [qaug2[i_][1]])
        op("pool", lambda e: e.memset(kaug2[i_][0][:], 1.0), [], [kaug2[i_][1]])
    op("dve", lambda e: e.memset(cn2[1][0][:], 0.0), [], [cn2[1][1]])

    pt_rot = 0
    sc_rot = 0
    yf_rot = 0
    for qb in range(NB):
        for ti in range(4):
            i = qb * 4 + ti
            xt, xt_b = xt2[i % 2]
            if i == 0:
                sch.dma(xt[:], x[0:128, :], [], [xt_b])
            if i + 1 < NT:
                sch.dma(xt2[(i + 1) % 2][0][:], x[(i + 1) * 128:(i + 2) * 128, :], [], [xt2[(i + 1) % 2][1]])
            rmsnorm_to_hT(xt, xt_b, h, h_b, junk, junk_b, ss, ss_b, bankT, bankT_b,
                          lambda: hTb[:, :, ti * 128:(ti + 1) * 128], hTb_b)
            pf0, pf0_b = bank[0]
            pf1, pf1_b = bank[1]
            for (pt, ptb, n0, n1) in ((pf0, pf0_b, 0, 512), (pf1, pf1_b, 512, FXC)):
                for c in range(KC):
                    op("pe", lambda e: e.matmul(pt[:, 0:n1 - n0], hTb[:, c, ti * 128:(ti + 1) * 128], Wf[:, c, n0:n1],
                                                start=(c == 0), stop=(c == KC - 1)), [hTb_b, Wf_b], [ptb])
            qa, qa_b = qaug2[i % 2]
            ka, ka_b = kaug2[i % 2]
            op("act", lambda e: e.activation(out=qk32[:], in_=pf0[:], func=AF.Copy), [pf0_b], [qk32_b])
            op("dve", lambda e: e.tensor_tensor(out=sq[:], in0=qk32[:], in1=qk32[:], op=ALU.mult), [qk32_b], [sq_b])
            op("dve", lambda e: e.tensor_reduce(out=s8[:], in_=v3(sq[:], 8), axis=AX.X, op=ALU.add), [sq_b], [s8_b])
            op("act", lambda e: e.activation(out=s8[:], in_=s8[:], func=AF.Ln, scale=1.0 / HD, bias=NORM_EPS), [s8_b], [s8_b])
            op("act", lambda e: e.activation(out=s8[:], in_=s8[:], func=AF.Exp, scale=-0.5), [s8_b], [s8_b])
            op("dve", lambda e: e.tensor_tensor(out=v3(qk32[:], 8), in0=v3(qk32[:], 8), in1=bc(s8[:], 8), op=ALU.mult),
               [qk32_b, s8_b], [qk32_b])
            op("dve", lambda e: e.tensor_tensor(out=qa[:, :, 0:64], in0=v3(qk32[:, 0:256]),
                                                in1=qgs[:].unsqueeze(1).to_broadcast([128, NH, HD]), op=ALU.mult),
               [qk32_b, qgs_b], [qa_b])
            op("dve", lambda e: e.tensor_tensor(out=ka[:, :, 0:64], in0=v3(qk32[:, 256:512]),
                                                in1=KG.unsqueeze(1).to_broadcast([128, NH, HD]), op=ALU.mult),
               [qk32_b, pb2_b], [ka_b])
            op("act", lambda e: e.activation(out=vaug[:, i, :, 0:64], in_=v3(pf1[:, 0:256]), func=AF.Copy), [pf1_b], [vaug_b])
            op("dve", lambda e: e.tensor_tensor(out=z4[:], in0=pf1[:, 256:260], in1=BFB, op=ALU.add), [pf1_b, pb2_b], [z4_b])
            op("act", lambda e: e.activation(out=z4[:], in_=z4[:], func=AF.Exp, scale=-1.0), [z4_b], [z4_b])
            op("act", lambda e: e.activation(out=z4[:], in_=z4[:], func=AF.Ln, bias=1.0), [z4_b], [z4_b])
            cn, cn_b = cn2[i % 2]
            cnp, cnp_b = cn2[(i + 1) % 2]
            pcs, pcs_b = bank[2]
            op("pe", lambda e: e.matmul(pcs[:, 0:4], tri_f[:], z4[:], start=True, stop=False), [tri_f_b, z4_b], [pcs_b])
            op("pe", lambda e: e.matmul(pcs[:, 0:4], sel127[:], cnp[:], start=False, stop=True), [sel127_b, cnp_b], [pcs_b])
            op("dve", lambda e: e.tensor_copy(out=cn[:], in_=pcs[:, 0:4]), [pcs_b], [cn_b])
            op("dve", lambda e: e.tensor_copy(out=spl[:, 0, :], in_=cn[:]), [cn_b], [spl_b])
            op("dve", lambda e: e.tensor_tensor(out=rr[:], in0=cn[:], in1=spl[:, 0, :], op=ALU.subtract), [cn_b, spl_b], [rr_b])
            op("dve", lambda e: e.tensor_copy(out=spl[:, 1, :], in_=rr[:]), [rr_b], [spl_b])
            op("dve", lambda e: e.tensor_tensor(out=rr[:], in0=rr[:], in1=spl[:, 1, :], op=ALU.subtract), [rr_b, spl_b], [rr_b])
            op("dve", lambda e: e.tensor_copy(out=spl[:, 2, :], in_=rr[:]), [rr_b], [spl_b])
            op("dve", lambda e: e.tensor_copy(out=ka[:, :, 67:70], in_=spl[:].rearrange("p s h -> p h s")), [spl_b], [ka_b])
            op("dve", lambda e: e.tensor_scalar(out=qa[:, :, 64:67], in0=spl[:].rearrange("p s h -> p h s"), scalar1=-1.0,
                                                scalar2=None, op0=ALU.mult), [spl_b], [qa_b])
            tqp = bankT[:].rearrange("p (a t) -> p a t", a=8)
            for hh in range(NH):
                op("pe", lambda e: e.transpose(out=tqp[0:70, hh, :], in_=qa[:, hh, :], identity=ident_bf[:]),
                   [qa_b, ident_bf_b], [bankT_b])
                op("pe", lambda e: e.transpose(out=tqp[0:70, 4 + hh, :], in_=ka[:, hh, :], identity=ident_bf[:]),
                   [ka_b, ident_bf_b], [bankT_b])
            op("dve", lambda e: e.tensor_copy(out=QT[:, :, ti * 128:(ti + 1) * 128], in_=tqp[0:70, 0:4, :]), [bankT_b], [QT_b])
            op("act", lambda e: e.activation(out=KT[:, :, i * 128:(i + 1) * 128], in_=tqp[0:70, 4:8, :], func=AF.Copy),
               [bankT_b], [KT_b])
        for hh in range(NH):
            pgt, pgt_b = bank[2]
            for c in range(KC):
                op("pe", lambda e: e.matmul(pgt[0:64, :], Wg2[:, c, hh * 64:(hh + 1) * 64], hTb[:, c, :], start=(c == 0),
                                            stop=(c == KC - 1)), [Wg2_b, hTb_b], [pgt_b])
            op("act", lambda e: e.activation(out=eg[:], in_=pgt[0:64, :], func=AF.Exp, scale=-1.0), [pgt_b], [eg_b])
            op("act", lambda e: e.activation(out=eg[:], in_=eg[:], func=AF.Ln, bias=1.0), [eg_b], [eg_b])
            op("act", lambda e: e.activation(out=eg[:], in_=eg[:], func=AF.Exp, scale=-1.0), [eg_b], [eg_b])
            op("dve", lambda e: e.tensor_tensor(out=sgT[:, hh, :], in0=eg[:], in1=pgt[0:64, :], op=ALU.mult), [eg_b, pgt_b], [sgT_b])
        nkt = 4 * (qb + 1)
        pending = None
        for hh in range(NH):
            po, po_b = bank[3]
            items = []
            for jt in range(nkt):
                jl = jt - 4 * qb
                col0 = 128 * jl if jl > 0 else 0
                items.append((jt, jl, col0, bank[4 + sc_rot], pT3[pt_rot]))
                sc_rot = (sc_rot + 1) % 3
                pt_rot = (pt_rot + 1) % 3
            LA = 2
            for k in range(nkt + LA):
                if k < nkt:
                    jt, jl, col0, (sc, sc_b), (pT, pT_b) = items[k]
                    op("pe", lambda e: e.matmul(sc[:, col0:512], KT[:, hh, jt * 128:(jt + 1) * 128], QT[:, hh, col0:512],
                                                start=True, stop=True), [KT_b, QT_b], [sc_b])
                if k >= LA:
                    jt, jl, col0, (sc, sc_b), (pT, pT_b) = items[k - LA]
                    op("act", lambda e: e.activation(out=pT[:, col0:512], in_=sc[:, col0:512], func=AF.Exp), [sc_b], [pT_b])
                    if jl >= 0:
                        op("pool", lambda e: e.affine_select(out=pT[:, col0:col0 + 128], in_=pT[:, col0:col0 + 128],
                                                             pattern=[[1, 128]], compare_op=ALU.is_ge, fill=0.0, base=0,
                                                             channel_multiplier=-1), [pT_b], [pT_b])
                    op("pe", lambda e: e.matmul(po[0:65, col0:512], vaug[:, jt, hh, :], pT[:, col0:512], start=(jt == 0),
                                                stop=(jt == nkt - 1)), [vaug_b, pT_b], [po_b])
                if pending is not None and k == min(LA + 3, nkt + LA - 1):
                    pending()
                    pending = None
            op("act", lambda e: e.activation(out=osb[:], in_=po[0:65, :], func=AF.Copy), [po_b], [osb_b])
            op("act", lambda e: e.activation(out=rec[64:65, :], in_=osb[64:65, :], func=AF.Ln), [osb_b], [rec_b])
            op("act", lambda e: e.activation(out=rec[64:65, :], in_=rec[64:65, :], func=AF.Exp, scale=-1.0), [rec_b], [rec_b])

            def part_b(hh=hh, qb=qb, yf=yfT2[yf_rot]):
                pbc, pbc_b = bank[2]
                yfT, yfT_b = yf
                op("pe", lambda e: e.matmul(pbc[0:64, :], ones_f[64:65, :], rec[64:65, :], start=True, stop=True),
                   [ones_f_b, rec_b], [pbc_b])
                op("dve", lambda e: e.tensor_tensor(out=yt32[:], in0=osb[0:64, :], in1=pbc[0:64, :], op=ALU.mult),
                   [osb_b, pbc_b], [yt32_b])
                op("dve", lambda e: e.tensor_tensor(out=yfT[:], in0=yt32[:], in1=sgT[:, hh, :], op=ALU.mult),
                   [yt32_b, sgT_b], [yfT_b])
                t0 = qb * 512
                sch.dma(ybuf[4 + hh][:, t0:t0 + 512], yfT[:], [yfT_b], [ybuf_b[4 + hh]])

            yf_rot = 1 - yf_rot
            pending = part_b
        if pending is not None:
            pending()
            pending = None

    sch.barrier()
    P2.close()
    if stop_after <= 2:
        sch.finish()
        return nc

    for c in range(4, 8):
        gather(c)
    P3 = Scope()

    def sb3(name, shape, dt=F32):
        return P3.sb(name, shape, dt)

    Wo, Wo_b = sb3("Wo", [128, KC, D_MODEL], BF16)
    onesg, onesg_b = sb3("onesg", [128, KC])
    op("dve", lambda e: e.memset(onesg[:], 1.0), [], [onesg_b])
    gcol, gcol_b = onesg, onesg_b
    load_weight(Wo, Wo_b, w_outp, D_MODEL)
    fg, fg_b = sb3("fg", [128, D_MODEL])
    sch.dma(fg[:], pb3_d[:, :], [], [fg_b])
    bank = [P3.ps("bank3_%d" % i, [128, 512], F32) for i in range(4)]
    yA2 = [sb3("yA%d" % i, [128, KC, 512], BF16) for i in range(2)]
    yB2 = [sb3("yB%d" % i, [128, KC, 512], BF16) for i in range(2)]
    sel, sel_b = sb3("sel", [128, 2])
    sch.dma(sel[:], sel_d[:, :], [], [sel_b])
    WoA, WoA_b = sb3("WoA", [128, KC, D_MODEL], BF16)
    WoB, WoB_b = sb3("WoB", [128, KC, D_MODEL], BF16)
    for c in range(KC):
        op("act", lambda e: e.activation(out=WoA[:, c, :], in_=Wo[:, c, :], func=AF.Copy, scale=sel[:, 0:1]), [Wo_b, sel_b], [WoA_b])
        op("act", lambda e: e.activation(out=WoB[:, c, :], in_=Wo[:, c, :], func=AF.Copy, scale=sel[:, 1:2]), [Wo_b, sel_b], [WoB_b])
    xr2 = [sb3("xr%d" % i, [128, D_MODEL]) for i in range(2)]
    z2 = [sb3("z%d" % i, [128, D_MODEL]) for i in range(2)]
    junk, junk_b = sb3("junk3", [128, D_MODEL], BF16)
    ss3, ss3_b = sb3("ss3", [128, 4])

    def load_y(blk):
        yA, yA_b = yA2[blk % 2]
        yB, yB_b = yB2[blk % 2]
        for c in range(KC):
            sch.dma(yA[:, c, :], yall[c][:, blk * 512:(blk + 1) * 512], [yall_b[c]], [yA_b])
            sch.dma(yB[:, c, :], yall[c][:, SH + blk * 512:SH + (blk + 1) * 512], [yall_b[c]], [yB_b])

    NT3 = SH // 128
    sch.dma(xr2[0][0][:], xres[0:128, :], [], [xr2[0][1]])
    load_y(0)
    for blk in range(SH // 512):
        yA, yA_b = yA2[blk % 2]
        yB, yB_b = yB2[blk % 2]
        if blk + 1 < SH // 512:
            load_y(blk + 1)
        for ti in range(4):
            i = blk * 4 + ti
            xr, xr_b = xr2[i % 2]
            z, z_b = z2[i % 2]
            if i + 1 < NT3:
                sch.dma(xr2[(i + 1) % 2][0][:], xres[(i + 1) * 128:(i + 2) * 128, :], [], [xr2[(i + 1) % 2][1]])
            for nn in range(2):
                po, po_b = bank[(i % 2) * 2 + nn]
                for c in range(KC):
                    op("pe", lambda e: e.matmul(po[:], yA[:, c, ti * 128:(ti + 1) * 128], WoA[:, c, nn * 512:(nn + 1) * 512],
                                                start=(c == 0), stop=False), [yA_b, WoA_b], [po_b])
                for c in range(KC):
                    op("pe", lambda e: e.matmul(po[:], yB[:, c, ti * 128:(ti + 1) * 128], WoB[:, c, nn * 512:(nn + 1) * 512],
                                                start=False, stop=(c == KC - 1)), [yB_b, WoB_b], [po_b])
                op("dve", lambda e: e.tensor_tensor(out=z[:, nn * 512:(nn + 1) * 512], in0=po[:], in1=xr[:, nn * 512:(nn + 1) * 512],
                                                    op=ALU.add), [po_b, xr_b], [z_b])
            op("act", lambda e: e.activation(out=junk[:], in_=z[:], func=AF.Square, accum_out=ss3[:, 0:1]), [z_b], [junk_b, ss3_b])
            op("act", lambda e: e.activation(out=ss3[:, 1:2], in_=ss3[:, 0:1], func=AF.Ln, scale=1.0 / D_MODEL, bias=NORM_EPS),
               [ss3_b], [ss3_b])
            op("act", lambda e: e.activation(out=ss3[:, 2:3], in_=ss3[:, 1:2], func=AF.Exp, scale=-0.5), [ss3_b], [ss3_b])
            op("dve", lambda e: e.scalar_tensor_tensor(out=z[:], in0=z[:], scalar=ss3[:, 2:3], in1=fg[:], op0=ALU.mult,
                                                       op1=ALU.mult), [z_b, ss3_b, fg_b], [z_b])
            sch.dma(out[i * 128:(i + 1) * 128, :], z[:], [z_b], [out_b])
    sch.finish()
    return nc


def make_core_inputs(inputs, b, g, S):
    f = lambda a: np.ascontiguousarray(np.asarray(a, dtype=np.float32))
    w_in = np.asarray(inputs["w_in"])[0]
    SH = S // 2
    hs = slice(g * 256, (g + 1) * 256)

    def cols(base):
        return np.arange(base + g * 256, base + (g + 1) * 256)

    rw_cols = np.concatenate([cols(0), cols(512), cols(1024), np.arange(1536, 1664)])
    fx0 = 1664
    fox_cols = np.concatenate([cols(fx0), cols(fx0 + 512), cols(fx0 + 1024), np.arange(fx0 + 1536 + 4 * g, fx0 + 1536 + 4 * g + 4)])
    g0 = 1664 + 1544
    mu = np.asarray(inputs["rw_mu"])[0][rw_cols]
    rep = lambda v: np.broadcast_to(np.asarray(v, np.float32).reshape(1, -1), (128, np.asarray(v).size))
    pb1 = np.concatenate([rep(mu), rep(np.asarray(inputs["rw_w0"])[0][hs]), rep(np.asarray(inputs["rw_a0"])[0][hs]),
                          rep(np.asarray(inputs["rw_k_k"])[0][hs]), rep(np.asarray(inputs["rw_k_a"])[0][hs]),
                          rep(np.asarray(inputs["rw_r_k"])[0][4 * g:4 * g + 4].reshape(-1)),
                          rep(np.asarray(inputs["rw_gn_g"])[0][hs]), rep(np.asarray(inputs["rw_gn_b"])[0][hs])], axis=1)
    pb2 = np.concatenate([rep(np.asarray(inputs["fox_q_g"])[0]), rep(np.asarray(inputs["fox_k_g"])[0]),
                          rep(np.asarray(inputs["fox_b_f"])[0][4 * g:4 * g + 4])], axis=1)
    w_out = np.asarray(inputs["w_out"])[0]
    rows = []
    for c in range(8):
        for gg in range(2):
            base = (4 * gg + c) * 64 if c < 4 else 512 + (4 * gg + c - 4) * 64
            rows.append(np.arange(base, base + 64))
    w_outp = w_out[np.concatenate(rows)]
    xb = np.asarray(inputs["x"])[b]
    return {
        "x": f(xb),
        "xres": f(xb[g * SH:(g + 1) * SH]),
        "w_rw": f(w_in[:, rw_cols]),
        "w_fox": f(w_in[:, fox_cols]),
        "w_grw": f(w_in[:, cols(g0)]),
        "w_gfx": f(w_in[:, cols(g0 + 512)]),
        "w_outp": f(w_outp),
        "g_col": f(np.asarray(inputs["norm_g"])[0].reshape(KC, 128).T),
        "pb1": f(pb1),
        "pb2": f(pb2),
        "pb3": f(rep(np.asarray(inputs["final_g"]))),
        "w_up": f(np.asarray(inputs["rw_w_up"])[0][:, hs]),
        "a_up": f(np.asarray(inputs["rw_a_up"])[0][:, hs]),
        "sel": f(np.tile(np.array([[1.0 - g, float(g)]], np.float32), (128, 1))),
    }


def kernel(**inputs):
    x = np.asarray(inputs["x"])
    B, S, D = x.shape
    nc = build_nc(S)
    in_maps = [make_core_inputs(inputs, c // 2, c % 2, S) for c in range(2 * B)]
    res = run_bass_kernel_spmd(nc, in_maps, core_ids=list(range(2 * B)))
    out = np.empty((B, S, D), np.float32)
    SH = S // 2
    for c in range(2 * B):
        out[c // 2, (c % 2) * SH:(c % 2 + 1) * SH] = res.results[c]["out"]
    return out
```
